# Optimizing a Trainium2 kernel written in Bass

```python
import math
import jax, jax.numpy as jnp
from jax import lax
import numpy as np

D_MODEL = 1024
BATCH = 8
SEQ = 2048
DEPTH = 1
DEC_BATCH = 128
DEC_SEQ = 1
PAST_LEN = 16384
PAGE_SIZE = 128

S5_WIDTH = D_MODEL // 2
S5_GROUP = 16
S5_GROUPS = S5_WIDTH // S5_GROUP
S5_STATE = 64
DT_MIN = 1e-3
DT_MAX = 1e-1
RWKV_HEAD = 64
RWKV_WIDTH = D_MODEL // 2
RWKV_HEADS = RWKV_WIDTH // RWKV_HEAD
W_LORA = 64
A_LORA = 64
G_LORA = 128
N_SHIFT = 3 * RWKV_WIDTH + W_LORA + A_LORA + G_LORA
RWKV_SPLITS = (RWKV_WIDTH, 2 * RWKV_WIDTH, 3 * RWKV_WIDTH, 3 * RWKV_WIDTH + W_LORA, 3 * RWKV_WIDTH + W_LORA + A_LORA)
N_IN = S5_WIDTH + N_SHIFT + 2 * D_MODEL
D_FF = 128 * ((8 * D_MODEL // 3 + 127) // 128)
N_MOD = 9
NORM_EPS = 1e-6
GN_EPS = 64e-5

kernel_name = 'hybrid_s5_rwkv7_macaron_adaln_step'


def _rmsnorm(x, g):
    xf = x.astype(jnp.float32)
    xf = xf * lax.rsqrt(jnp.mean(xf * xf, axis=-1, keepdims=True) + NORM_EPS)
    return (xf * g).astype(x.dtype)


def _modulate(n, shift, scale):
    return n * (1 + scale) + shift


def _swiglu(x, w1, w3, w2):
    return (jax.nn.silu(x @ w1) * (x @ w3)) @ w2


def _s5_branch(u, h_re0, h_im0, p):
    f32 = jnp.float32
    bn, length, _ = u.shape
    uf = u.astype(f32)
    ug = uf.reshape(bn, length, S5_GROUPS, S5_GROUP)
    dt = jnp.exp(p['s5_log_dt'].astype(f32))[:, None]
    lam_re = p['s5_lam_re'].astype(f32)
    lam_im = p['s5_lam_im'].astype(f32)
    mag = jnp.exp(lam_re * dt)
    ang = lam_im * dt
    lb_re = mag * jnp.cos(ang)
    lb_im = mag * jnp.sin(ang)
    den = lam_re * lam_re + lam_im * lam_im
    num_re = lb_re - 1
    k_re = (num_re * lam_re + lb_im * lam_im) / den
    k_im = (lb_im * lam_re - num_re * lam_im) / den
    b_re = p['s5_b_re'].astype(f32)
    b_im = p['s5_b_im'].astype(f32)
    bb_re = k_re[..., None] * b_re - k_im[..., None] * b_im
    bb_im = k_re[..., None] * b_im + k_im[..., None] * b_re
    bu_re = jnp.einsum('blgc,gnc->blgn', ug, bb_re)
    bu_im = jnp.einsum('blgc,gnc->blgn', ug, bb_im)
    h_re0 = h_re0.astype(f32)
    h_im0 = h_im0.astype(f32)
    bu_re = bu_re.at[:, 0].add(lb_re * h_re0 - lb_im * h_im0)
    bu_im = bu_im.at[:, 0].add(lb_re * h_im0 + lb_im * h_re0)
    a_re = jnp.broadcast_to(lb_re, bu_re.shape)
    a_im = jnp.broadcast_to(lb_im, bu_im.shape)

    def combine(e1, e2):
        a1r, a1i, b1r, b1i = e1
        a2r, a2i, b2r, b2i = e2
        return (a1r * a2r - a1i * a2i,
                a1r * a2i + a1i * a2r,
                a2r * b1r - a2i * b1i + b2r,
                a2r * b1i + a2i * b1r + b2i)

    _, _, xs_re, xs_im = lax.associative_scan(combine, (a_re, a_im, bu_re, bu_im), axis=1)
    y = (jnp.einsum('blgn,gcn->blgc', xs_re, p['s5_c_re'].astype(f32))
         - jnp.einsum('blgn,gcn->blgc', xs_im, p['s5_c_im'].astype(f32)))
    y = y.reshape(bn, length, S5_WIDTH) + p['s5_d'] * uf
    z = jax.nn.gelu(y)
    out = (z @ p['s5_glu_v']) * jax.nn.sigmoid(z @ p['s5_glu_g'])
    return out.astype(u.dtype), xs_re[:, -1], xs_im[:, -1]


def _wkv7_step(s, inp):
    r, dec, k, v, a, b = inp
    sa = jnp.einsum('bhij,bhj->bhi', s, a)
    s = s * dec[:, :, None, :] + sa[..., None] * b[:, :, None, :] + v[..., None] * k[:, :, None, :]
    return s, jnp.einsum('bhij,bhj->bhi', s, r)


def _rwkv_branch(mixed, wkv0, p):
    f32 = jnp.float32
    bn, length, _ = mixed.shape
    mf = mixed.astype(f32)
    r, k, v, wd, ad, gd = jnp.split(mf, RWKV_SPLITS, axis=-1)
    w = -jax.nn.softplus(-(p['rwkv_w0'] + jnp.tanh(wd) @ p['rwkv_w2'])) - 0.5
    decay = jnp.exp(-jnp.exp(w))
    a = jax.nn.sigmoid(p['rwkv_a0'] + ad @ p['rwkv_a2'])
    g = jax.nn.sigmoid(gd) @ p['rwkv_g2']
    hs = (bn, length, RWKV_HEADS, RWKV_HEAD)
    kk = (k * p['rwkv_k_k']).reshape(hs)
    kk = kk / jnp.maximum(jnp.sqrt(jnp.sum(kk * kk, axis=-1, keepdims=True)), 1e-12)
    k = k * (1 + (a - 1) * p['rwkv_k_a'])
    r_h = r.reshape(hs)
    k_h = k.reshape(hs)
    v_h = v.reshape(hs)
    d_h = decay.reshape(hs)
    a_h = a.reshape(hs)
    xs = tuple(jnp.moveaxis(t, 1, 0) for t in (r_h, d_h, k_h, v_h, -kk, kk * a_h))
    s_fin, ys = lax.scan(_wkv7_step, wkv0.astype(f32), xs)
    y = jnp.moveaxis(ys, 0, 1)
    mean = jnp.mean(y, axis=-1, keepdims=True)
    var = jnp.mean(jnp.square(y - mean), axis=-1, keepdims=True)
    yn = ((y - mean) * lax.rsqrt(var + GN_EPS)).reshape(bn, length, RWKV_WIDTH)
    yn = yn * p['rwkv_ln_w'] + p['rwkv_ln_b']
    bonus = (jnp.sum(r_h * k_h * p['rwkv_r_k'], axis=-1, keepdims=True) * v_h).reshape(bn, length, RWKV_WIDTH)
    o = (yn + bonus) * g
    return (o @ p['rwkv_proj']).astype(mixed.dtype), s_fin


def _layer(x, c, s5_re0, s5_im0, wkv0, shift0, p):
    mod = (jax.nn.silu(c) @ p['w_ada'] + p['b_ada']).reshape(c.shape[0], N_MOD, 1, D_MODEL)
    sh1, sc1, gt1, sh2, sc2, gt2, sh3, sc3, gt3 = [mod[:, i] for i in range(N_MOD)]
    h = x
    n = _modulate(_rmsnorm(h, p['g_ffn1']), sh1, sc1)
    h = h + 0.5 * gt1 * _swiglu(n, p['ffn1_w1'], p['ffn1_w3'], p['ffn1_w2'])
    u = _modulate(_rmsnorm(h, p['g_mix']), sh2, sc2)
    proj = u @ p['w_in']
    s5_in = proj[..., :S5_WIDTH]
    cur = proj[..., S5_WIDTH:S5_WIDTH + N_SHIFT]
    gates = proj[..., S5_WIDTH + N_SHIFT:]
    prev = jnp.concatenate([shift0[:, None, :].astype(cur.dtype), cur[:, :-1]], axis=1)
    mixed = cur + p['mu_shift'] * (prev - cur)
    y_s5, s5_re1, s5_im1 = _s5_branch(s5_in, s5_re0, s5_im0, p)
    y_rwkv, wkv1 = _rwkv_branch(mixed, wkv0, p)
    m = jax.nn.sigmoid(gates[..., :D_MODEL]) * y_s5 + jax.nn.sigmoid(gates[..., D_MODEL:]) * y_rwkv
    h = h + gt2 * (m @ p['w_out'])
    n = _modulate(_rmsnorm(h, p['g_ffn2']), sh3, sc3)
    h = h + 0.5 * gt3 * _swiglu(n, p['ffn2_w1'], p['ffn2_w3'], p['ffn2_w2'])
    y = _rmsnorm(h, p['g_final']).astype(x.dtype)
    sd = s5_re0.dtype
    return (y, s5_re1.astype(sd), s5_im1.astype(sd), wkv1.astype(wkv0.dtype),
            cur[:, -1].astype(shift0.dtype))


def setup_inputs(seed: int = 0) -> dict:
    key = jax.random.key(seed)
    ks = iter(jax.random.split(key, 64))
    f32 = jnp.float32

    def nrm(shape, s):
        return jax.random.normal(next(ks), shape, f32) * s

    def uni(shape, lo, hi):
        return jax.random.uniform(next(ks), shape, f32, lo, hi)

    d = D_MODEL
    g, n = S5_GROUPS, S5_STATE
    return {
        'x_prompt': nrm((BATCH, SEQ, d), 1.0),
        'x_sample': nrm((DEC_BATCH, DEC_SEQ, d), 1.0),
        'c_prompt': nrm((BATCH, d), 1.0),
        'c_sample': nrm((DEC_BATCH, d), 1.0),
        'state_s5_re': nrm((DEC_BATCH, g, n), 0.3),
        'state_s5_im': nrm((DEC_BATCH, g, n), 0.3),
        'state_wkv': nrm((DEC_BATCH, RWKV_HEADS, RWKV_HEAD, RWKV_HEAD), 0.3),
        'state_shift': nrm((DEC_BATCH, N_SHIFT), 1.0),
        'w_ada': nrm((d, N_MOD * d), 0.3 * d ** -0.5),
        'b_ada': nrm((N_MOD * d,), 0.01),
        'g_ffn1': 1.0 + nrm((d,), 0.02),
        'g_mix': 1.0 + nrm((d,), 0.02),
        'g_ffn2': 1.0 + nrm((d,), 0.02),
        'g_final': 1.0 + nrm((d,), 0.02),
        'ffn1_w1': nrm((d, D_FF), d ** -0.5),
        'ffn1_w3': nrm((d, D_FF), d ** -0.5),
        'ffn1_w2': nrm((D_FF, d), D_FF ** -0.5),
        'ffn2_w1': nrm((d, D_FF), d ** -0.5),
        'ffn2_w3': nrm((d, D_FF), d ** -0.5),
        'ffn2_w2': nrm((D_FF, d), D_FF ** -0.5),
        'w_in': nrm((d, N_IN), d ** -0.5),
        'mu_shift': uni((N_SHIFT,), 0.0, 1.0),
        's5_lam_re': -0.5 + nrm((g, n), 0.005),
        's5_lam_im': jnp.pi * jnp.arange(n, dtype=f32)[None, :] + nrm((g, n), 0.01),
        's5_log_dt': uni((g,), math.log(DT_MIN), math.log(DT_MAX)),
        's5_b_re': nrm((g, n, S5_GROUP), (2 * S5_GROUP) ** -0.5),
        's5_b_im': nrm((g, n, S5_GROUP), (2 * S5_GROUP) ** -0.5),
        's5_c_re': nrm((g, S5_GROUP, n), n ** -0.5),
        's5_c_im': nrm((g, S5_GROUP, n), n ** -0.5),
        's5_d': nrm((S5_WIDTH,), 1.0),
        's5_glu_v': nrm((S5_WIDTH, d), S5_WIDTH ** -0.5),
        's5_glu_g': nrm((S5_WIDTH, d), S5_WIDTH ** -0.5),
        'rwkv_w0': uni((RWKV_WIDTH,), -6.0, 1.0),
        'rwkv_w2': nrm((W_LORA, RWKV_WIDTH), 0.1 * W_LORA ** -0.5),
        'rwkv_a0': nrm((RWKV_WIDTH,), 0.1),
        'rwkv_a2': nrm((A_LORA, RWKV_WIDTH), 0.1 * A_LORA ** -0.5),
        'rwkv_g2': nrm((G_LORA, RWKV_WIDTH), G_LORA ** -0.5),
        'rwkv_k_k': 0.85 + nrm((RWKV_WIDTH,), 0.02),
        'rwkv_k_a': 1.0 + nrm((RWKV_WIDTH,), 0.02),
        'rwkv_r_k': nrm((RWKV_HEADS, RWKV_HEAD), 0.1),
        'rwkv_ln_w': 1.0 + nrm((RWKV_WIDTH,), 0.02),
        'rwkv_ln_b': nrm((RWKV_WIDTH,), 0.01),
        'rwkv_proj': nrm((RWKV_WIDTH, d), RWKV_WIDTH ** -0.5),
        'w_out': nrm((d, d), d ** -0.5),
    }


def reference(x_prompt, x_sample, c_prompt, c_sample, state_s5_re, state_s5_im, state_wkv, state_shift,
              w_ada, b_ada, g_ffn1, g_mix, g_ffn2, g_final,
              ffn1_w1, ffn1_w3, ffn1_w2, ffn2_w1, ffn2_w3, ffn2_w2,
              w_in, mu_shift,
              s5_lam_re, s5_lam_im, s5_log_dt, s5_b_re, s5_b_im, s5_c_re, s5_c_im, s5_d, s5_glu_v, s5_glu_g,
              rwkv_w0, rwkv_w2, rwkv_a0, rwkv_a2, rwkv_g2, rwkv_k_k, rwkv_k_a, rwkv_r_k,
              rwkv_ln_w, rwkv_ln_b, rwkv_proj, w_out):
    p = dict(w_ada=w_ada, b_ada=b_ada, g_ffn1=g_ffn1, g_mix=g_mix, g_ffn2=g_ffn2, g_final=g_final,
             ffn1_w1=ffn1_w1, ffn1_w3=ffn1_w3, ffn1_w2=ffn1_w2,
             ffn2_w1=ffn2_w1, ffn2_w3=ffn2_w3, ffn2_w2=ffn2_w2,
             w_in=w_in, mu_shift=mu_shift,
             s5_lam_re=s5_lam_re, s5_lam_im=s5_lam_im, s5_log_dt=s5_log_dt,
             s5_b_re=s5_b_re, s5_b_im=s5_b_im, s5_c_re=s5_c_re, s5_c_im=s5_c_im,
             s5_d=s5_d, s5_glu_v=s5_glu_v, s5_glu_g=s5_glu_g,
             rwkv_w0=rwkv_w0, rwkv_w2=rwkv_w2, rwkv_a0=rwkv_a0, rwkv_a2=rwkv_a2, rwkv_g2=rwkv_g2,
             rwkv_k_k=rwkv_k_k, rwkv_k_a=rwkv_k_a, rwkv_r_k=rwkv_r_k,
             rwkv_ln_w=rwkv_ln_w, rwkv_ln_b=rwkv_ln_b, rwkv_proj=rwkv_proj, w_out=w_out)
    bp = x_prompt.shape[0]
    z_s5 = jnp.zeros((bp, S5_GROUPS, S5_STATE), state_s5_re.dtype)
    z_wkv = jnp.zeros((bp, RWKV_HEADS, RWKV_HEAD, RWKV_HEAD), state_wkv.dtype)
    z_shift = jnp.zeros((bp, N_SHIFT), state_shift.dtype)
    y_prompt, s5_re_p, s5_im_p, wkv_p, shift_p = _layer(x_prompt, c_prompt, z_s5, z_s5, z_wkv, z_shift, p)
    y_sample, s5_re_s, s5_im_s, wkv_s, shift_s = _layer(x_sample, c_sample, state_s5_re, state_s5_im,
                                                        state_wkv, state_shift, p)
    return (y_prompt, y_sample, s5_re_p, s5_im_p, wkv_p, shift_p, s5_re_s, s5_im_s, wkv_s, shift_s)
```

```python
from contextlib import ExitStack
import numpy as np
import concourse.bass as bass
import concourse.mybir as mybir
from concourse.bass_utils import run_bass_kernel_spmd

F32 = mybir.dt.float32
BF16 = mybir.dt.bfloat16
AF = mybir.ActivationFunctionType
ALU = mybir.AluOpType
AX = mybir.AxisListType

COMPUTE = ("pe", "act", "dve", "pool")
N_DMA_SEMS = 24
import os as _os
SAME_ENGINE_FIFO = _os.environ.get('SEF', '0') == '1'
D = 1024
DFF = 2816
NJ = 22
NIN = 4352
NSH = 1792
EPS = 1e-6
GN_EPS = 64e-5


class Prog:
    def __init__(self, nc, stack):
        self.nc = nc
        self.ins = []
        self.sems = {e: stack.enter_context(nc.semaphore("s_" + e)) for e in COMPUTE}
        self.dsems = [stack.enter_context(nc.semaphore("d%d" % i)) for i in range(N_DMA_SEMS)]
        self.dcount = [0] * N_DMA_SEMS
        self.seq = {e: 0 for e in COMPUTE}
        self.last_w = {}
        self.readers = {}
        self.engs = list(COMPUTE) + ["sp"]
        self.waited = {e: {} for e in self.engs}
        self.ndma = 0
        self.barrier = False
        self.n_instr = {e: 0 for e in self.engs}

    def op(self, eng, fn, reads=(), writes=()):
        self.ins.append(dict(eng=eng, fn=fn, reads=tuple(reads), writes=tuple(writes), dma=False))

    def dma(self, eng, out, in_, reads=(), writes=(), **kw):
        def fn(e, out=out, in_=in_, kw=kw):
            return e.dma_start(out=out, in_=in_, **kw)
        self.ins.append(dict(eng=eng, fn=fn, reads=tuple(reads), writes=tuple(writes), dma=True))

    def mark(self):
        return len(self.ins)

    def take(self, mark):
        out = self.ins[mark:]
        del self.ins[mark:]
        return out

    def put_interleaved(self, A, B):
        ia = ib = 0
        while ia < len(A) or ib < len(B):
            if ib >= len(B) or (ia < len(A) and ia * len(B) <= ib * len(A)):
                self.ins.append(A[ia]); ia += 1
            else:
                self.ins.append(B[ib]); ib += 1

    def flush(self, final=False):
        nc = self.nc
        sems, dsems, dcount, seq = self.sems, self.dsems, self.dcount, self.seq
        last_w, readers, waited = self.last_w, self.readers, self.waited
        streams = {e: [] for e in self.engs}
        bar = None
        if self.barrier:
            bar = [("c", e, seq[e]) for e in COMPUTE if seq[e] > 0] + \
                  [("d", si, 16 * dcount[si]) for si in range(N_DMA_SEMS) if dcount[si] > 0]
            last_w.clear()
            readers.clear()
        first = {e: True for e in self.engs}
        for I in self.ins:
            E = I["eng"]
            deps = []
            if bar is not None and first[E]:
                deps.extend(bar)
            first[E] = False
            for r in I["reads"]:
                if r in last_w:
                    deps.append(last_w[r])
            for w in I["writes"]:
                if w in last_w:
                    deps.append(last_w[w])
                deps.extend(readers.get(w, ()))
            if I["dma"]:
                si = self.ndma % N_DMA_SEMS
                self.ndma += 1
                if dcount[si] > 0:
                    deps.append(("d", si, 16 * dcount[si]))
                dcount[si] += 1
                ev = ("d", si, 16 * dcount[si])
            else:
                seq[E] += 1
                ev = ("c", E, seq[E])
            need = {}
            for d in deps:
                if d[0] == "c" and d[1] == "pe" and E == "pe" and not I["dma"]:
                    continue
                if SAME_ENGINE_FIFO and d[0] == "c" and d[1] == E and not I["dma"]:
                    continue
                key = (d[0], d[1])
                if d[2] > need.get(key, 0):
                    need[key] = d[2]
            waits = []
            for key, v in need.items():
                if waited[E].get(key, 0) >= v:
                    continue
                waited[E][key] = v
                waits.append((key, v))
            streams[E].append((waits, I, ev))
            for r in I["reads"]:
                readers.setdefault(r, []).append(ev)
            for w in I["writes"]:
                last_w[w] = ev
                readers[w] = []
        for e in self.engs:
            self.n_instr[e] += len(streams[e])
        self.ins = []
        self.barrier = True

        def run_stream(E):
            def body(eng):
                for waits, I, ev in streams[E]:
                    for key, v in waits:
                        s = sems[key[1]] if key[0] == "c" else dsems[key[1]]
                        eng.wait_ge(s, v)
                    ins = I["fn"](eng)
                    if ev[0] == "c":
                        ins.then_inc(sems[E], 1)
                    else:
                        ins.then_inc(dsems[ev[1]], 16)
                if final and E == "sp":
                    for si in range(N_DMA_SEMS):
                        if dcount[si] > 0:
                            eng.wait_ge(dsems[si], 16 * dcount[si])
            return body

        with nc.Block() as block:
            block.sync(run_stream("sp"))
            block.tensor(run_stream("pe"))
            block.scalar(run_stream("act"))
            block.vector(run_stream("dve"))
            block.gpsimd(run_stream("pool"))


W_SPECS = [
    ("w_ada", [D, 9 * D]), ("ffn1_w1", [D, DFF]), ("ffn1_w3", [D, DFF]), ("ffn1_w2", [DFF, D]),
    ("ffn2_w1", [D, DFF]), ("ffn2_w3", [D, DFF]), ("ffn2_w2", [DFF, D]), ("w_in", [D, NIN]),
    ("s5_glu_v", [512, D]), ("s5_glu_g", [512, D]), ("rwkv_proj", [512, D]), ("w_out", [D, D]),
    ("rwkv_w2", [64, 512]), ("rwkv_a2", [64, 512]), ("rwkv_g2", [128, 512]),
    ("g_ffn1", [8, 128]), ("g_mix", [8, 128]), ("g_ffn2", [8, 128]), ("g_final", [8, 128]),
    ("b_ada", [72, 128]), ("s5_d", [4, 128]),
    ("mu_shift", [1, NSH]), ("rwkv_w0", [1, 512]), ("rwkv_a0", [1, 512]), ("rwkv_k_k", [1, 512]),
    ("rwkv_k_a", [1, 512]), ("rwkv_ln_w", [1, 512]), ("rwkv_ln_b", [1, 512]), ("rwkv_r_k", [1, 512]),
    ("s5_lam_re", [32, 64]), ("s5_lam_im", [32, 64]), ("s5_log_dt", [32, 1]),
    ("s5_b_re", [32, 64, 16]), ("s5_b_im", [32, 64, 16]), ("s5_c_re", [32, 16, 64]), ("s5_c_im", [32, 16, 64]),
]
IN_SPECS = [("xP", [2048, D]), ("xS", [16, D]), ("cA", [17, D]), ("s5re0", [16, 2048]), ("s5im0", [16, 2048]),
            ("wkv0", [128, 4096]), ("shift0", [16, NSH])]
OUT_SPECS = [("yP", [2048, D]), ("yS", [16, D]), ("s5reP", [16, 128]), ("s5imP", [16, 128]),
             ("wkvP", [512, 64]), ("shiftP", [1, NSH]), ("s5reS", [16, 2048]), ("s5imS", [16, 2048]),
             ("wkvS", [128, 4096]), ("shiftS", [16, NSH])]


def build_nc(debug=None, upto="all"):
    debug = debug or {}
    nc = bass.Bass("TRN2", target_bir_lowering=False)
    dr = {}
    for n, s in IN_SPECS + W_SPECS:
        dr[n] = nc.dram_tensor(n, s, F32, kind="ExternalInput").ap()
    for n, s in OUT_SPECS:
        dr[n] = nc.dram_tensor(n, s, F32, kind="ExternalOutput").ap()
    for n, s in debug.items():
        dr[n] = nc.dram_tensor(n, s, F32, kind="ExternalOutput").ap()
    scr1 = nc.dram_tensor("scr1", [16, 3072], F32).ap()
    scr2 = nc.dram_tensor("scr2", [128, 64], F32).ap()
    uT_scr = nc.dram_tensor("uT_scr", [128, 8 * 1040], BF16).ap().rearrange("p (k t) -> p k t", k=8)
    wcur_scr = nc.dram_tensor("wcur_scr", [128, 8 * NSH], BF16).ap()
    bc_scr = nc.dram_tensor("bc_scr", [128, NSH + 7 * 512], F32).ap()
    cache = {"c_Mw": nc.dram_tensor("c_Mw", [128, 32 * 128], BF16).ap(), "c_Zw": nc.dram_tensor("c_Zw", [128, 2 * 16 * 128], BF16).ap(),
             "c_YwZ": nc.dram_tensor("c_YwZ", [128, 2 * 32 * 128], BF16).ap(), "c_prm": nc.dram_tensor("c_prm", [128, 12 * 16], F32).ap(),
             "c_LP": nc.dram_tensor("c_LP", [128, 10 * 3 * 16], F32).ap(), "c_LI": nc.dram_tensor("c_LI", [128, 2 * 16], F32).ap()}

    st = ExitStack()
    with st:
        uniq = [0]

        def mk_sb(stack):
            def sb(name, shape, dt=F32):
                uniq[0] += 1
                return stack.enter_context(nc.sbuf_tensor("%s_%d" % (name, uniq[0]), shape, dt))
            return sb
        sb = mk_sb(st)
        P = Prog(nc, st)
        ps = [st.enter_context(nc.psum_tensor("ps%d" % i, [128, 512], F32)) for i in range(8)]
        PSK = ["ps%d" % i for i in range(8)]
        psb16 = [p_[:, :].bitcast(BF16) for p_ in ps]

        def dbg_dump(name, ap, key):
            if name in debug:
                P.dma("pool", dr[name], ap, reads=[key])

        onesF = sb("onesF", [128, 128])
        onesB = sb("onesB", [128, 128], BF16)
        identF = sb("identF", [128, 128])
        identB = sb("identB", [128, 128], BF16)
        epsc = sb("epsc", [128, 2])
        P.op("pool", lambda e: e.memset(onesF[:], 1.0), writes=["onesF"])
        P.op("pool", lambda e: e.memset(onesB[:], 1.0), writes=["onesB"])
        P.op("pool", lambda e: e.memset(epsc[:, 0:1], EPS), writes=["epsc"])
        P.op("pool", lambda e: e.memset(epsc[:, 1:2], GN_EPS), writes=["epsc"])
        mhalf = sb("mhalf", [128, 8])
        P.op("pool", lambda e: e.memset(mhalf[:], -0.5), writes=["mhalf"])
        P.op("pool", lambda e: e.affine_select(identF[:], onesF[:], [[-1, 128]], ALU.is_equal, 0.0, base=0, channel_multiplier=1),
             reads=["onesF"], writes=["identF"])
        P.op("pool", lambda e: e.affine_select(identB[:], onesF[:], [[-1, 128]], ALU.is_equal, 0.0, base=0, channel_multiplier=1),
             reads=["onesF"], writes=["identB"])

        NC = 1040
        CP = sb("CP", [128, 108])
        modT = sb("modT", [128, 72, 17])
        GSC = sb("GSC", [128, 3, 8, 17])
        GATE = sb("GATE", [128, 3, 8, 17])
        hT = sb("hT", [128, 8, NC])
        class _NT:
            pass
        NT = _NT()

        def alloc_norm(sbx_):
            NT.sq1 = [sbx_("sq1_%d" % i, [128, 512], BF16) for i in range(2)]
            NT.rstd = sbx_("rstd", [128, 512])
            NT.tmpn = sbx_("tmpn", [128, 512])
        lastrow = sb("lastrow", [1, NSH])
        Hst = sb("Hst", [128, 4, 64])
        Hb = sb("Hb", [128, 4, 64], BF16)
        s5car = sb("s5car", [128, 2, 16])
        P.op("pool", lambda e: e.memset(Hst[:], 0.0), writes=["Hst"])
        P.op("pool", lambda e: e.memset(Hb[:], 0.0), writes=["Hb"])
        P.op("pool", lambda e: e.memset(s5car[:], 0.0), writes=["s5car"])

        def ranges_of(sbk):
            return [(0, 512), (512, 512)] if sbk == 0 else [(0, 512), (512, 512), (1024, 16)]

        def is_samp(c0):
            return c0 == 1024

        def phase0(sb0):
            stg = sb0("stg", [108, 128])
            r0 = 0
            for n, k in (("g_ffn1", 8), ("g_mix", 8), ("g_ffn2", 8), ("g_final", 8), ("b_ada", 72), ("s5_d", 4)):
                P.dma("sp", stg[r0:r0 + k, :], dr[n], writes=["stg"])
                r0 += k
            P.op("pe", lambda e: e.transpose(ps[0][:, 0:108], stg[:, :], identF[0:108, 0:108]), reads=["stg", "identF"], writes=["ps0"])
            P.op("dve", lambda e: e.tensor_copy(CP[:], ps[0][:, 0:108]), reads=["ps0"], writes=["CP"])
            cin = sb0("cin", [17, D])
            csl = sb0("csl", [17, D])
            scT = sb0("scT", [128, 8, 17], BF16)
            P.dma("sp", cin[:], dr["cA"], writes=["cin"])
            P.op("act", lambda e: e.activation(csl[:], cin[:], AF.Silu), reads=["cin"], writes=["csl"])
            for kt in range(8):
                P.op("pe", lambda e, kt=kt: e.transpose(ps[1][:, kt * 17:(kt + 1) * 17], csl[:, kt * 128:(kt + 1) * 128], identF[0:17, 0:17]),
                     reads=["csl", "identF"], writes=["ps1"])
            P.op("dve", lambda e: e.tensor_copy(scT[:].rearrange("p a b -> p (a b)"), ps[1][:, 0:136]), reads=["ps1"], writes=["scT"])
            wada = [sb0("wada%d" % i, [128, 8, 512], BF16) for i in range(2)]
            for ch in range(18):
                wb = wada[ch % 2]
                wk = "wada%d" % (ch % 2)
                P.dma("pool", wb[:], dr["w_ada"][:, ch * 512:(ch + 1) * 512].rearrange("(kt p) n -> p kt n", p=128), writes=[wk])
                pb = ps[2 + (ch % 2)]
                pk = PSK[2 + (ch % 2)]
                for jj in range(4):
                    for kt in range(8):
                        P.op("pe", lambda e, wb=wb, pb=pb, jj=jj, kt=kt: e.matmul(pb[:, jj * 17:(jj + 1) * 17], wb[:, kt, jj * 128:(jj + 1) * 128],
                                                                                  scT[:, kt, :], start=(kt == 0), stop=(kt == 7)),
                             reads=[wk, "scT"], writes=[pk])
                for jj in range(4):
                    j = ch * 4 + jj
                    P.op("act", lambda e, pb=pb, jj=jj, j=j: e.activation(modT[:, j, :], pb[:, jj * 17:(jj + 1) * 17], AF.Identity, bias=CP[:, 32 + j:33 + j], scale=1.0),
                         reads=[pk, "CP"], writes=["modT"])
            for L in range(3):
                for dt in range(8):
                    P.op("dve", lambda e, L=L, dt=dt: e.tensor_scalar(GSC[:, L, dt, :], modT[:, (3 * L + 1) * 8 + dt, :], 1.0, CP[:, L * 8 + dt:L * 8 + dt + 1], ALU.add, ALU.mult),
                         reads=["modT", "CP"], writes=["GSC"])
                gsc = 1.0 if L == 1 else 0.5
                P.op("dve", lambda e, L=L, gsc=gsc: e.tensor_scalar(GATE[:, L, :, :], modT[:, (3 * L + 2) * 8:(3 * L + 3) * 8, :], gsc, None, ALU.mult),
                     reads=["modT"], writes=["GATE"])
            dbg_dump("d_modT", modT[:].rearrange("p a b -> p (a b)"), "modT")

        def sumsq_rstd(c0, n):
            for dt in range(8):
                sq = NT.sq1[dt % 2]
                sk = "sq1_%d" % (dt % 2)
                P.op("act", lambda e, sq=sq, dt=dt: e.activation(sq[:, 0:n], hT[:, dt, c0:c0 + n], AF.Square), reads=["hT"], writes=[sk])
                P.op("pe", lambda e, sq=sq, dt=dt: e.matmul(ps[7][:, 0:n], onesB[:, :], sq[:, 0:n], start=(dt == 0), stop=(dt == 7)),
                     reads=["onesB", sk], writes=["ps7"])
            P.op("act", lambda e: e.activation(NT.rstd[:, 0:n], ps[7][:, 0:n], AF.Sqrt, bias=epsc[:, 0:1], scale=1.0 / D), reads=["ps7", "epsc"], writes=["rstd"])
            P.op("dve", lambda e: e.reciprocal(NT.rstd[:, 0:n], NT.rstd[:, 0:n]), reads=["rstd"], writes=["rstd"])

        def normmod(sbk, L, out_t, out_key0, rngs=None, obase=0, tmp2=None, tmp2_key="tmpn2", krange=False):
            temps = [(NT.tmpn, "tmpn"), ((tmp2, tmp2_key) if tmp2 is not None else (NT.tmpn, "tmpn"))]
            for (c0, n) in (rngs if rngs is not None else ranges_of(sbk)):
                o0 = c0 - obase
                out_key = (out_key0, c0) if krange else out_key0
                sumsq_rstd(c0, n)
                for dt in range(8):
                    tt_, tk_ = temps[dt % 2]
                    if not is_samp(c0):
                        P.op("dve", lambda e, dt=dt, c0=c0, n=n, tt_=tt_: e.scalar_tensor_tensor(tt_[:, 0:n], hT[:, dt, c0:c0 + n], GSC[:, L, dt, 0:1], NT.rstd[:, 0:n], ALU.mult, ALU.mult),
                             reads=["hT", "GSC", "rstd"], writes=[tk_])
                        P.op("act", lambda e, dt=dt, c0=c0, n=n, o0=o0, tt_=tt_: e.activation(out_t[:, dt, o0:o0 + n], tt_[:, 0:n], AF.Identity, bias=modT[:, (3 * L) * 8 + dt, 0:1], scale=1.0),
                             reads=[tk_, "modT"], writes=[out_key])
                    else:
                        P.op("dve", lambda e, dt=dt, c0=c0, n=n: e.tensor_tensor(NT.tmpn[:, 0:n], hT[:, dt, c0:c0 + n], NT.rstd[:, 0:n], ALU.mult), reads=["hT", "rstd"], writes=["tmpn"])
                        P.op("dve", lambda e, dt=dt, n=n: e.tensor_tensor(NT.tmpn[:, 0:n], NT.tmpn[:, 0:n], GSC[:, L, dt, 1:17], ALU.mult), reads=["tmpn", "GSC"], writes=["tmpn"])
                        P.op("dve", lambda e, dt=dt, c0=c0, n=n, o0=o0: e.tensor_tensor(out_t[:, dt, o0:o0 + n], NT.tmpn[:, 0:n], modT[:, (3 * L) * 8 + dt, 1:17], ALU.add),
                             reads=["tmpn", "modT"], writes=[out_key])

        def resid_update(L, dt, c0, n, pb, pk):
            if not is_samp(c0):
                P.op("dve", lambda e: e.scalar_tensor_tensor(hT[:, dt, c0:c0 + n], pb[:, 0:n], GATE[:, L, dt, 0:1], hT[:, dt, c0:c0 + n], ALU.mult, ALU.add),
                     reads=[pk, "GATE", "hT"], writes=["hT"])
            else:
                P.op("dve", lambda e: e.tensor_tensor(NT.tmpn[:, 0:n], pb[:, 0:n], GATE[:, L, dt, 1:17], ALU.mult), reads=[pk, "GATE"], writes=["tmpn"])
                P.op("dve", lambda e: e.tensor_tensor(hT[:, dt, c0:c0 + n], hT[:, dt, c0:c0 + n], NT.tmpn[:, 0:n], ALU.add), reads=["tmpn", "hT"], writes=["hT"])

        def load_x(sbk, xin):
            for ti in range(8 + (1 if sbk == 1 else 0)):
                xb_ = xin[ti % 2]
                xk = "xin%d" % (ti % 2)
                if ti < 8:
                    rows = 128
                    P.dma("sp", xb_[:, :], dr["xP"][sbk * 1024 + ti * 128: sbk * 1024 + (ti + 1) * 128, :], writes=[xk])
                else:
                    rows = 16
                    P.dma("sp", xb_[0:16, :], dr["xS"], writes=[xk])
                for half in range(2):
                    pb = ps[half]
                    for q in range(4):
                        dt = half * 4 + q
                        P.op("pe", lambda e, pb=pb, q=q, dt=dt, xb_=xb_, rows=rows: e.transpose(pb[:, q * 128:q * 128 + rows], xb_[0:rows, dt * 128:(dt + 1) * 128], identF[0:rows, 0:rows]),
                             reads=[xk, "identF"], writes=[PSK[half]])
                    src = pb[:, :].rearrange("p (q t) -> p q t", q=4)[:, :, 0:rows]
                    dst = hT[:, half * 4:(half + 1) * 4, ti * 128:ti * 128 + rows]
                    if half == 0:
                        P.op("dve", lambda e, dst=dst, src=src: e.tensor_copy(dst, src), reads=[PSK[half]], writes=["hT"])
                    else:
                        P.op("act", lambda e, dst=dst, src=src: e.activation(dst, src, AF.Copy), reads=[PSK[half]], writes=["hT"])

        def ffn_bufs(sbx):
            return dict(nT=sbx("nT", [128, 8, NC], BF16), hid=sbx("hid", [128, NJ, NC], BF16),
                        wA=[sbx("wA%d" % i, [128, 2, 8, 256], BF16) for i in range(2)],
                        wB=[sbx("wB%d" % i, [128, NJ, 128], BF16) for i in range(2)],
                        silu=[sbx("silu%d" % i, [128, 512]) for i in range(2)], tmpn2=sbx("tmpn2", [128, 512]))

        def ffn(sbk, L, w1n, w3n, w2n, sbx, bufs=None):
            B_ = bufs if bufs is not None else ffn_bufs(sbx)
            nT, hid, wA, wB, silu_t, tmpn2 = B_["nT"], B_["hid"], B_["wA"], B_["wB"], B_["silu"], B_["tmpn2"]
            normmod(sbk, L, nT, "nT", tmp2=tmpn2, krange=True)
            rngs = ranges_of(sbk)
            cnt = [0]
            for ch in range(11):
                wb = wA[ch % 2]
                wk = "wA%d" % (ch % 2)
                P.dma("pool", wb[:, 0, :, :], dr[w1n][:, ch * 256:(ch + 1) * 256].rearrange("(kt p) n -> p kt n", p=128), writes=[wk])
                P.dma("pool", wb[:, 1, :, :], dr[w3n][:, ch * 256:(ch + 1) * 256].rearrange("(kt p) n -> p kt n", p=128), writes=[wk])
                for jj in range(2):
                    j = ch * 2 + jj
                    for (c0, n) in rngs:
                        k = cnt[0] % 2
                        cnt[0] += 1
                        p1, p3 = ps[2 * k], ps[2 * k + 1]
                        for kt in range(8):
                            P.op("pe", lambda e, p1=p1, wb=wb, jj=jj, kt=kt, c0=c0, n=n: e.matmul(p1[:, 0:n], wb[:, 0, kt, jj * 128:(jj + 1) * 128], nT[:, kt, c0:c0 + n], start=(kt == 0), stop=(kt == 7)),
                                 reads=[wk, ("nT", c0)], writes=[PSK[2 * k]])
                        for kt in range(8):
                            P.op("pe", lambda e, p3=p3, wb=wb, jj=jj, kt=kt, c0=c0, n=n: e.matmul(p3[:, 0:n], wb[:, 1, kt, jj * 128:(jj + 1) * 128], nT[:, kt, c0:c0 + n], start=(kt == 0), stop=(kt == 7)),
                                 reads=[wk, ("nT", c0)], writes=[PSK[2 * k + 1]])
                        sl = silu_t[k]
                        slk = "silu%d" % k
                        P.op("act", lambda e, sl=sl, p1=p1, n=n: e.activation(sl[:, 0:n], p1[:, 0:n], AF.Silu), reads=[PSK[2 * k]], writes=[slk])
                        P.op("dve", lambda e, sl=sl, p3=p3, j=j, c0=c0, n=n: e.tensor_tensor(hid[:, j, c0:c0 + n], p3[:, 0:n], sl[:, 0:n], ALU.mult),
                             reads=[PSK[2 * k + 1], slk], writes=[("hid", j, c0)])
            for dt in range(8):
                wb = wB[dt % 2]
                wk = "wB%d" % (dt % 2)
                P.dma("pool", wb[:, :, :], dr[w2n][:, dt * 128:(dt + 1) * 128].rearrange("(j p) n -> p j n", p=128), writes=[wk])
                for (c0, n) in rngs:
                    k = 4 + (cnt[0] % 2)
                    cnt[0] += 1
                    pb = ps[k]
                    for j in range(NJ):
                        P.op("pe", lambda e, pb=pb, wb=wb, j=j, c0=c0, n=n: e.matmul(pb[:, 0:n], wb[:, j, :], hid[:, j, c0:c0 + n], start=(j == 0), stop=(j == NJ - 1)),
                             reads=[wk, ("hid", j, c0)], writes=[PSK[k]])
                    resid_update(L, dt, c0, n, pb, PSK[k])

        def final_out(sbk, yout):
            for (c0, n) in ranges_of(sbk):
                sumsq_rstd(c0, n)
                for dt in range(8):
                    P.op("dve", lambda e, dt=dt, c0=c0, n=n: e.scalar_tensor_tensor(hT[:, dt, c0:c0 + n], hT[:, dt, c0:c0 + n], CP[:, 24 + dt:25 + dt], NT.rstd[:, 0:n], ALU.mult, ALU.mult),
                         reads=["hT", "CP", "rstd"], writes=["hT"])
            for ti in range(8 + (1 if sbk == 1 else 0)):
                rows = 128 if ti < 8 else 16
                yo = yout[ti % 2]
                yk = "yout%d" % (ti % 2)
                for half in range(2):
                    pb = ps[half]
                    for q in range(4):
                        dt = half * 4 + q
                        P.op("pe", lambda e, pb=pb, q=q, dt=dt, ti=ti, rows=rows: e.transpose(pb[0:rows, q * 128:(q + 1) * 128], hT[:, dt, ti * 128:ti * 128 + rows], identF[:, :]),
                             reads=["hT", "identF"], writes=[PSK[half]])
                    if half == 0:
                        P.op("dve", lambda e, pb=pb, yo=yo, rows=rows: e.tensor_copy(yo[0:rows, 0:512], pb[0:rows, :]), reads=[PSK[half]], writes=[yk])
                    else:
                        P.op("act", lambda e, pb=pb, yo=yo, rows=rows: e.activation(yo[0:rows, 512:1024], pb[0:rows, :], AF.Copy), reads=[PSK[half]], writes=[yk])
                if ti < 8:
                    P.dma("sp", dr["yP"][sbk * 1024 + ti * 128: sbk * 1024 + (ti + 1) * 128, :], yo[:, :], reads=[yk])
                else:
                    P.dma("sp", dr["yS"], yo[0:16, :], reads=[yk])

        TWO_PI = 6.283185307179586
        import os
        S5STOP = int(os.environ.get('S5STOP', '99'))

        def s5_phase(sbk, sbx, zT):
            rngs = ranges_of(sbk)
            nT = sbx("uT", [128, 8, NC], BF16)
            normmod(sbk, 1, nT, "uT")
            if sbk == 1:
                dbg_dump("d_uT", nT[:].rearrange("p a b -> p (a b)"), "uT")
            s5in = sbx("s5in", [128, 4, NC], BF16)
            wch = sbx("wch", [128, 8, 512], BF16)
            P.dma("pool", wch[:], dr["w_in"][:, 0:512].rearrange("(kt p) n -> p kt n", p=128), writes=["wch"])
            cnt = 0
            for ct in range(4):
                for (c0, n) in rngs:
                    pb = ps[cnt % 2]
                    pk = PSK[cnt % 2]
                    cnt += 1
                    for kt in range(8):
                        P.op("pe", lambda e, pb=pb, ct=ct, kt=kt, c0=c0, n=n: e.matmul(pb[:, 0:n], wch[:, kt, ct * 128:(ct + 1) * 128], nT[:, kt, c0:c0 + n], start=(kt == 0), stop=(kt == 7)),
                             reads=["wch", "uT"], writes=[pk])
                    P.op("act", lambda e, pb=pb, ct=ct, c0=c0, n=n: e.activation(s5in[:, ct, c0:c0 + n], pb[:, 0:n], AF.Copy), reads=[pk], writes=["s5in"])
            if S5STOP <= 1:
                return
            lam = sbx("lam", [128, 3, 16])
            prm = sbx("prm", [128, 12, 16])
            P.dma("sp", lam[:, 0, :], dr["s5_lam_re"].rearrange("(g2 gp) n -> (gp n) g2", gp=2), writes=["lam"], allow_slow_non_contiguous=True)
            P.dma("sp", lam[:, 1, :], dr["s5_lam_im"].rearrange("(g2 gp) n -> (gp n) g2", gp=2), writes=["lam"], allow_slow_non_contiguous=True)
            ldv = dr["s5_log_dt"].rearrange("(g2 gp) o -> gp (g2 o)", gp=2)
            P.dma("sp", lam[0:64, 2, :], ldv[0:1, :].to_broadcast([64, 16]), writes=["lam"], allow_slow_non_contiguous=True)
            P.dma("sp", lam[64:128, 2, :], ldv[1:2, :].to_broadcast([64, 16]), writes=["lam"], allow_slow_non_contiguous=True)
            pr = lambda i: prm[:, i, :]
            def dv(fn, reads=("prm", "lam")):
                P.op("dve", fn, reads=list(reads), writes=["prm"])
            def ac(fn):
                P.op("act", fn, reads=["prm", "lam"], writes=["prm"])
            ac(lambda e: e.activation(pr(0), lam[:, 2, :], AF.Exp))
            dv(lambda e: e.tensor_tensor(pr(1), lam[:, 0, :], pr(0), ALU.mult))
            dv(lambda e: e.tensor_tensor(pr(2), lam[:, 1, :], pr(0), ALU.mult))
            ac(lambda e: e.activation(pr(3), pr(1), AF.Exp))
            def sin_of(dst, shift):
                dv(lambda e: e.tensor_scalar(pr(4), pr(2), shift, 1.0 / TWO_PI, ALU.add, ALU.mult))
                dv(lambda e: e.tensor_scalar(pr(4), pr(4), 12582912.0, 12582912.0, ALU.add, ALU.subtract))
                dv(lambda e: e.scalar_tensor_tensor(pr(4), pr(4), -TWO_PI, pr(2), ALU.mult, ALU.add))
                dv(lambda e: e.tensor_scalar(pr(4), pr(4), shift, 3.1415925, ALU.add, ALU.min))
                dv(lambda e: e.tensor_scalar_max(pr(4), pr(4), -3.1415925))
                ac(lambda e: e.activation(pr(dst), pr(4), AF.Sin))
            sin_of(5, 0.0)
            sin_of(6, 1.5707963267948966)
            if S5STOP <= 2:
                return
            LP = sbx("LP", [128, 10, 3, 16])
            P.op("dve", lambda e: e.tensor_tensor(LP[:, 0, 0, :], pr(3), pr(6), ALU.mult), reads=["prm"], writes=["LP"])
            P.op("dve", lambda e: e.tensor_tensor(LP[:, 0, 1, :], pr(3), pr(5), ALU.mult), reads=["prm"], writes=["LP"])
            for k in range(9):
                P.op("dve", lambda e, k=k: e.tensor_tensor(pr(4), LP[:, k, 0, :], LP[:, k, 0, :], ALU.mult), reads=["LP"], writes=["prm"])
                P.op("dve", lambda e, k=k: e.tensor_tensor(pr(11), LP[:, k, 1, :], LP[:, k, 1, :], ALU.mult), reads=["LP"], writes=["prm"])
                P.op("dve", lambda e, k=k: e.tensor_tensor(LP[:, k + 1, 0, :], pr(4), pr(11), ALU.subtract), reads=["prm"], writes=["LP"])
                P.op("dve", lambda e, k=k: e.scalar_tensor_tensor(LP[:, k + 1, 1, :], LP[:, k, 0, :], 2.0, LP[:, k, 1, :], ALU.mult, ALU.mult), reads=["LP"], writes=["LP"])
            P.op("dve", lambda e: e.tensor_scalar(LP[:, :, 2, :], LP[:, :, 1, :], -1.0, None, ALU.mult), reads=["LP"], writes=["LP"])
            dv(lambda e: e.tensor_tensor(pr(7), lam[:, 0, :], lam[:, 0, :], ALU.mult))
            dv(lambda e: e.tensor_tensor(pr(4), lam[:, 1, :], lam[:, 1, :], ALU.mult))
            dv(lambda e: e.tensor_tensor(pr(7), pr(7), pr(4), ALU.add))
            dv(lambda e: e.reciprocal(pr(7), pr(7)))
            P.op("dve", lambda e: e.tensor_scalar(pr(8), LP[:, 0, 0, :], -1.0, None, ALU.add), reads=["LP"], writes=["prm"])
            dv(lambda e: e.tensor_tensor(pr(9), pr(8), lam[:, 0, :], ALU.mult))
            P.op("dve", lambda e: e.tensor_tensor(pr(4), LP[:, 0, 1, :], lam[:, 1, :], ALU.mult), reads=["LP", "lam"], writes=["prm"])
            dv(lambda e: e.tensor_tensor(pr(9), pr(9), pr(4), ALU.add))
            dv(lambda e: e.tensor_tensor(pr(9), pr(9), pr(7), ALU.mult))
            P.op("dve", lambda e: e.tensor_tensor(pr(10), LP[:, 0, 1, :], lam[:, 0, :], ALU.mult), reads=["LP", "lam"], writes=["prm"])
            dv(lambda e: e.tensor_tensor(pr(4), pr(8), lam[:, 1, :], ALU.mult))
            dv(lambda e: e.tensor_tensor(pr(10), pr(10), pr(4), ALU.subtract))
            dv(lambda e: e.tensor_tensor(pr(10), pr(10), pr(7), ALU.mult))
            if S5STOP <= 3:
                return
            maskcol = sbx("maskcol", [128, 8])
            P.op("pool", lambda e: e.affine_select(maskcol[:], onesF[:, 0:8], [[-16, 8]], ALU.is_ge, 0.0, base=0, channel_multiplier=1), reads=["onesF"], writes=["maskcol"])
            P.op("pool", lambda e: e.affine_select(maskcol[:], maskcol[:], [[16, 8]], ALU.is_ge, 0.0, base=15, channel_multiplier=-1), reads=["maskcol"], writes=["maskcol"])
            bst = sbx("bst", [64, 2, 512])
            P.dma("sp", bst[:, 0, :].rearrange("n (g c) -> n g c", g=32), dr["s5_b_re"].rearrange("g n c -> n g c"), writes=["bst"])
            P.dma("sp", bst[:, 1, :].rearrange("n (g c) -> n g c", g=32), dr["s5_b_im"].rearrange("g n c -> n g c"), writes=["bst"])
            Wb = sbx("Wb", [128, 2, 32, 64], BF16)
            bT = sbx("bT", [128, 64])
            for ri in range(2):
                for ct in range(4):
                    P.op("pe", lambda e, ri=ri, ct=ct: e.transpose(ps[2][:, 0:64], bst[:, ri, ct * 128:(ct + 1) * 128], identF[0:64, 0:64]), reads=["bst", "identF"], writes=["ps2"])
                    P.op("act", lambda e: e.activation(bT[:], ps[2][:, 0:64], AF.Copy), reads=["ps2"], writes=["bT"])
                    for g8 in range(8):
                        g = ct * 8 + g8
                        P.op("dve", lambda e, ri=ri, g=g, g8=g8: e.tensor_scalar(Wb[:, ri, g, :], bT[:], maskcol[:, g8:g8 + 1], None, ALU.mult), reads=["bT", "maskcol"], writes=["Wb"])
            if S5STOP <= 4:
                return
            cst = sbx("cst", [128, 4, 128])
            Cw = sbx("Cw", [128, 2, 32, 128], BF16)
            cTt = sbx("cTt", [128, 128])
            P.op("pool", lambda e: e.memset(Cw[:], 0.0), writes=["Cw"])
            for ri in range(2):
                src = dr["s5_c_re" if ri == 0 else "s5_c_im"].rearrange("(ct g8) c n -> (g8 c) ct n", g8=8)
                P.dma("sp", cst[:, :, 0:64], src, writes=["cst"])
                P.dma("sp", cst[:, :, 64:128], src, writes=["cst"])
                for ct in range(4):
                    P.op("pe", lambda e, ct=ct: e.transpose(ps[3][:, 0:128], cst[:, ct, :], identF[:, :]), reads=["cst", "identF"], writes=["ps3"])
                    P.op("act", lambda e: e.activation(cTt[:], ps[3][:, 0:128], AF.Copy), reads=["ps3"], writes=["cTt"])
                    for g8 in range(8):
                        g = ct * 8 + g8
                        gp, g2 = g % 2, g // 2
                        sgn = 1.0 if ri == 0 else -1.0
                        P.op("dve", lambda e, ri=ri, gp=gp, g2=g2, g8=g8, sgn=sgn: e.tensor_scalar(Cw[gp * 64:(gp + 1) * 64, ri, 2 * g2 + gp, g8 * 16:(g8 + 1) * 16],
                                                                                                   cTt[gp * 64:(gp + 1) * 64, g8 * 16:(g8 + 1) * 16], sgn, None, ALU.mult),
                             reads=["cTt"], writes=["Cw"])
            if S5STOP <= 5:
                return
            TB = 128
            X = [sbx("X%d" % i, [128, 2, 16, TB]) for i in range(2)]
            Xb = sbx("Xb", [128, 2, 16, TB], BF16)
            tmpk = sbx("tmpk", [128, TB])
            yv = sbx("yv", [128, TB])
            gw = sbx("gw", [128, TB])
            gs = sbx("gs", [128, TB])

            def bproj(c0, n, dst):
                for g2 in range(16):
                    for gp in range(2):
                        g = 2 * g2 + gp
                        ct = g // 8
                        for ri in range(2):
                            P.op("pe", lambda e, gp=gp, g=g, ct=ct, ri=ri: e.matmul(ps[ri][gp * 64:(gp + 1) * 64, 0:n], Wb[:, ri, g, :], s5in[:, ct, c0:c0 + n], start=True, stop=True),
                                 reads=["Wb", "s5in"], writes=[PSK[ri]])
                    P.op("dve", lambda e, g2=g2: e.tensor_scalar(tmpk[:, 0:n], ps[1][:, 0:n], prm[:, 10, g2:g2 + 1], None, ALU.mult), reads=["ps1", "prm"], writes=["tmpk"])
                    P.op("dve", lambda e, g2=g2: e.scalar_tensor_tensor(dst[:, 0, g2, 0:n], ps[0][:, 0:n], prm[:, 9, g2:g2 + 1], tmpk[:, 0:n], ALU.mult, ALU.subtract),
                         reads=["ps0", "prm", "tmpk"], writes=["XA"])
                    P.op("dve", lambda e, g2=g2: e.tensor_scalar(tmpk[:, 0:n], ps[0][:, 0:n], prm[:, 10, g2:g2 + 1], None, ALU.mult), reads=["ps0", "prm"], writes=["tmpk"])
                    P.op("dve", lambda e, g2=g2: e.scalar_tensor_tensor(dst[:, 1, g2, 0:n], ps[1][:, 0:n], prm[:, 9, g2:g2 + 1], tmpk[:, 0:n], ALU.mult, ALU.add),
                         reads=["ps1", "prm", "tmpk"], writes=["XA"])

            def cproj(c0, n):
                for ct in range(4):
                    pb = ps[2 + (ct % 2)]
                    pk = PSK[2 + (ct % 2)]
                    first = True
                    for g8 in range(8):
                        g = ct * 8 + g8
                        gp, g2 = g % 2, g // 2
                        for ri in range(2):
                            last = (g8 == 7 and ri == 1)
                            P.op("pe", lambda e, pb=pb, gp=gp, g2=g2, ri=ri, first=first, last=last: e.matmul(pb[:, 0:n], Cw[:, ri, 2 * g2 + gp, :], Xb[:, ri, g2, 0:n], start=first, stop=last),
                                 reads=["Cw", "Xb"], writes=[pk])
                            first = False
                    P.op("dve", lambda e, pb=pb, ct=ct: e.scalar_tensor_tensor(yv[:, 0:n], s5in[:, ct, c0:c0 + n], CP[:, 104 + ct:105 + ct], pb[:, 0:n], ALU.mult, ALU.add),
                         reads=["s5in", "CP", pk], writes=["yv"])
                    P.op("act", lambda e: e.activation(gw[:, 0:n], yv[:, 0:n], AF.Square), reads=["yv"], writes=["gw"])
                    P.op("dve", lambda e: e.tensor_scalar(gw[:, 0:n], gw[:, 0:n], 0.044715, 1.0, ALU.mult, ALU.add), reads=["gw"], writes=["gw"])
                    P.op("dve", lambda e: e.tensor_tensor(gw[:, 0:n], gw[:, 0:n], yv[:, 0:n], ALU.mult), reads=["gw", "yv"], writes=["gw"])
                    P.op("act", lambda e: e.activation(gs[:, 0:n], gw[:, 0:n], AF.Sigmoid, scale=1.5957691216057308), reads=["gw"], writes=["gs"])
                    P.op("dve", lambda e, ct=ct: e.tensor_tensor(zT[:, ct, c0:c0 + n], yv[:, 0:n], gs[:, 0:n], ALU.mult), reads=["yv", "gs"], writes=["zT"])

            if S5STOP <= 6:
                return
            for blk in range(1024 // TB):
                c0 = blk * TB
                A, B_ = X[0], X[1]
                bproj(c0, TB, A)
                if S5STOP <= 7:
                    return
                P.op("dve", lambda e: e.tensor_tensor(prm[:, 4, :], LP[:, 0, 0, :], s5car[:, 0, :], ALU.mult), reads=["LP", "s5car"], writes=["prm"])
                P.op("dve", lambda e: e.tensor_tensor(prm[:, 11, :], LP[:, 0, 1, :], s5car[:, 1, :], ALU.mult), reads=["LP", "s5car"], writes=["prm"])
                P.op("dve", lambda e: e.tensor_tensor(prm[:, 4, :], prm[:, 4, :], prm[:, 11, :], ALU.subtract), reads=["prm"], writes=["prm"])
                P.op("dve", lambda e, A=A: e.tensor_tensor(A[:, 0, :, 0], A[:, 0, :, 0], prm[:, 4, :], ALU.add), reads=["XA", "prm"], writes=["XA"])
                P.op("dve", lambda e: e.tensor_tensor(prm[:, 4, :], LP[:, 0, 0, :], s5car[:, 1, :], ALU.mult), reads=["LP", "s5car"], writes=["prm"])
                P.op("dve", lambda e: e.tensor_tensor(prm[:, 11, :], LP[:, 0, 1, :], s5car[:, 0, :], ALU.mult), reads=["LP", "s5car"], writes=["prm"])
                P.op("dve", lambda e: e.tensor_tensor(prm[:, 4, :], prm[:, 4, :], prm[:, 11, :], ALU.add), reads=["prm"], writes=["prm"])
                P.op("dve", lambda e, A=A: e.tensor_tensor(A[:, 1, :, 0], A[:, 1, :, 0], prm[:, 4, :], ALU.add), reads=["XA", "prm"], writes=["XA"])
                cur_, nxt_ = A, B_
                ck, nk = "XA", "XB"
                k = 0
                while (1 << k) < TB:
                    s = 1 << k
                    P.op("act", lambda e, cur_=cur_, nxt_=nxt_, s=s: e.activation(nxt_[:, :, :, 0:s], cur_[:, :, :, 0:s], AF.Copy), reads=[ck], writes=[nk])
                    for g2 in range(16):
                        lr = LP[:, k, 0, g2:g2 + 1]
                        li = LP[:, k, 1, g2:g2 + 1]
                        nli = LP[:, k, 2, g2:g2 + 1]
                        P.op("dve", lambda e, cur_=cur_, nxt_=nxt_, s=s, g2=g2, lr=lr: e.scalar_tensor_tensor(nxt_[:, 0, g2, s:TB], cur_[:, 0, g2, 0:TB - s], lr, cur_[:, 0, g2, s:TB], ALU.mult, ALU.add),
                             reads=[ck, "LP"], writes=[nk])
                        P.op("dve", lambda e, cur_=cur_, nxt_=nxt_, s=s, g2=g2, nli=nli: e.scalar_tensor_tensor(nxt_[:, 0, g2, s:TB], cur_[:, 1, g2, 0:TB - s], nli, nxt_[:, 0, g2, s:TB], ALU.mult, ALU.add),
                             reads=[ck, nk, "LP"], writes=[nk])
                        P.op("dve", lambda e, cur_=cur_, nxt_=nxt_, s=s, g2=g2, lr=lr: e.scalar_tensor_tensor(nxt_[:, 1, g2, s:TB], cur_[:, 1, g2, 0:TB - s], lr, cur_[:, 1, g2, s:TB], ALU.mult, ALU.add),
                             reads=[ck, "LP"], writes=[nk])
                        P.op("dve", lambda e, cur_=cur_, nxt_=nxt_, s=s, g2=g2, li=li: e.scalar_tensor_tensor(nxt_[:, 1, g2, s:TB], cur_[:, 0, g2, 0:TB - s], li, nxt_[:, 1, g2, s:TB], ALU.mult, ALU.add),
                             reads=[ck, nk, "LP"], writes=[nk])
                    cur_, nxt_ = nxt_, cur_
                    ck, nk = nk, ck
                    k += 1
                if S5STOP <= 8:
                    return
                P.op("dve", lambda e, cur_=cur_: e.tensor_copy(s5car[:, :, :], cur_[:, :, :, TB - 1]), reads=[ck], writes=["s5car"])
                P.op("act", lambda e, cur_=cur_: e.activation(Xb[:], cur_[:], AF.Copy), reads=[ck], writes=["Xb"])
                if S5STOP <= 9:
                    return
                cproj(c0, TB)
            if S5STOP <= 10:
                return
            if sbk == 1:
                for ri, nm in ((0, "s5reP"), (1, "s5imP")):
                    P.op("pe", lambda e, ri=ri: e.transpose(ps[4][0:16, 0:128], s5car[:, ri, :], identF[:, :]), reads=["s5car", "identF"], writes=["ps4"])
                    P.op("act", lambda e: e.activation(tmpk[0:16, 0:128], ps[4][0:16, 0:128], AF.Copy), reads=["ps4"], writes=["tmpk"])
                    P.dma("sp", dr[nm], tmpk[0:16, 0:128], reads=["tmpk"])
                if S5STOP <= 11:
                    return
                h0in = sbx("h0in", [16, 2048])
                h0 = sbx("h0", [128, 2, 16, 16])
                for ri in range(2):
                    P.dma("sp", h0in[:, :], dr["s5re0" if ri == 0 else "s5im0"], writes=["h0in"])
                    for g2 in range(16):
                        P.op("pe", lambda e, ri=ri, g2=g2: e.transpose(ps[5][:, g2 * 16:(g2 + 1) * 16], h0in[:, g2 * 128:(g2 + 1) * 128], identF[0:16, 0:16]), reads=["h0in", "identF"], writes=["ps5"])
                    P.op("act", lambda e, ri=ri: e.activation(h0[:, ri, :, :].rearrange("p a b -> p (a b)"), ps[5][:, 0:256], AF.Copy), reads=["ps5"], writes=["h0"])
                A = X[0]
                bproj(1024, 16, A)
                x1 = sbx("x1", [128, 2, 16, 16])
                t16 = sbx("t16", [128, 16, 16])
                lrb = LP[:, 0, 0, :].unsqueeze(2).to_broadcast([128, 16, 16])
                lib = LP[:, 0, 1, :].unsqueeze(2).to_broadcast([128, 16, 16])
                P.op("dve", lambda e: e.tensor_tensor(x1[:, 0, :, :], h0[:, 0, :, :], lrb, ALU.mult), reads=["h0", "LP"], writes=["x1"])
                P.op("dve", lambda e: e.tensor_tensor(t16[:], h0[:, 1, :, :], lib, ALU.mult), reads=["h0", "LP"], writes=["t16"])
                P.op("dve", lambda e: e.tensor_tensor(x1[:, 0, :, :], x1[:, 0, :, :], t16[:], ALU.subtract), reads=["x1", "t16"], writes=["x1"])
                P.op("dve", lambda e: e.tensor_tensor(x1[:, 0, :, :], x1[:, 0, :, :], A[:, 0, :, 0:16], ALU.add), reads=["x1", "XA"], writes=["x1"])
                P.op("dve", lambda e: e.tensor_tensor(x1[:, 1, :, :], h0[:, 1, :, :], lrb, ALU.mult), reads=["h0", "LP"], writes=["x1"])
                P.op("dve", lambda e: e.tensor_tensor(t16[:], h0[:, 0, :, :], lib, ALU.mult), reads=["h0", "LP"], writes=["t16"])
                P.op("dve", lambda e: e.tensor_tensor(x1[:, 1, :, :], x1[:, 1, :, :], t16[:], ALU.add), reads=["x1", "t16"], writes=["x1"])
                P.op("dve", lambda e: e.tensor_tensor(x1[:, 1, :, :], x1[:, 1, :, :], A[:, 1, :, 0:16], ALU.add), reads=["x1", "XA"], writes=["x1"])
                P.op("act", lambda e: e.activation(Xb[:, :, :, 0:16], x1[:], AF.Copy), reads=["x1"], writes=["Xb"])
                cproj(1024, 16)
                for ri, nm in ((0, "s5reS"), (1, "s5imS")):
                    for half in range(2):
                        for q in range(8):
                            g2 = half * 8 + q
                            P.op("pe", lambda e, ri=ri, g2=g2, q=q, half=half: e.transpose(ps[6 + half][0:16, q * 128:(q + 1) * 128] if q < 4 else ps[6 + half][0:16, q * 128 - 512:(q + 1) * 128 - 512],
                                                                                          x1[:, ri, g2, :], identF[:, :]), reads=["x1", "identF"], writes=[PSK[6 + half]])
                            if q == 3 or q == 7:
                                lo = half * 1024 + (0 if q == 3 else 512)
                                P.op("act", lambda e, ri=ri, half=half, lo=lo: e.activation(h0in[:, lo:lo + 512], ps[6 + half][0:16, :], AF.Copy), reads=[PSK[6 + half]], writes=["h0in"])
                    P.dma("sp", dr[nm], h0in[:, :], reads=["h0in"])

        def s5_setup(sbs):
            lam = sbs("lam", [128, 3, 16])
            prm = sbs("prm", [128, 12, 16])
            LP = sbs("LP", [128, 10, 3, 16])
            LI = sbs("LI", [128, 2, 16])
            Mw = sbs("Mw", [128, 32, 128], BF16)
            Zw = sbs("Zw", [128, 2, 16, 128], BF16)
            YwZ = sbs("YwZ", [128, 2, 32, 128], BF16)
            pr = lambda i: prm[:, i, :]

            def dv(fn, reads=("prm", "lam")):
                P.op("dve", fn, reads=list(reads), writes=["prm"])

            def ac(fn):
                P.op("act", fn, reads=["prm", "lam"], writes=["prm"])
            P.dma("sp", lam[:, 0, :], dr["s5_lam_re"].rearrange("(g2 gp) n -> (gp n) g2", gp=2), writes=["lam"], allow_slow_non_contiguous=True)
            P.dma("sp", lam[:, 1, :], dr["s5_lam_im"].rearrange("(g2 gp) n -> (gp n) g2", gp=2), writes=["lam"], allow_slow_non_contiguous=True)
            ldv = dr["s5_log_dt"].rearrange("(g2 gp) o -> gp (g2 o)", gp=2)
            P.dma("sp", lam[0:64, 2, :], ldv[0:1, :].to_broadcast([64, 16]), writes=["lam"], allow_slow_non_contiguous=True)
            P.dma("sp", lam[64:128, 2, :], ldv[1:2, :].to_broadcast([64, 16]), writes=["lam"], allow_slow_non_contiguous=True)
            ac(lambda e: e.activation(pr(0), lam[:, 2, :], AF.Exp))
            dv(lambda e: e.tensor_tensor(pr(1), lam[:, 0, :], pr(0), ALU.mult))
            dv(lambda e: e.tensor_tensor(pr(2), lam[:, 1, :], pr(0), ALU.mult))
            ac(lambda e: e.activation(pr(3), pr(1), AF.Exp))

            def sin_of(dst, shift):
                dv(lambda e: e.tensor_scalar(pr(4), pr(2), shift, 1.0 / TWO_PI, ALU.add, ALU.mult))
                dv(lambda e: e.tensor_scalar(pr(4), pr(4), 12582912.0, 12582912.0, ALU.add, ALU.subtract))
                dv(lambda e: e.scalar_tensor_tensor(pr(4), pr(4), -TWO_PI, pr(2), ALU.mult, ALU.add))
                dv(lambda e: e.tensor_scalar(pr(4), pr(4), shift, 3.1415925, ALU.add, ALU.min))
                dv(lambda e: e.tensor_scalar_max(pr(4), pr(4), -3.1415925))
                ac(lambda e: e.activation(pr(dst), pr(4), AF.Sin))
            sin_of(5, 0.0)
            sin_of(6, 1.5707963267948966)
            P.op("dve", lambda e: e.tensor_tensor(LP[:, 0, 0, :], pr(3), pr(6), ALU.mult), reads=["prm"], writes=["LP"])
            P.op("dve", lambda e: e.tensor_tensor(LP[:, 0, 1, :], pr(3), pr(5), ALU.mult), reads=["prm"], writes=["LP"])
            for k in range(9):
                P.op("dve", lambda e, k=k: e.tensor_tensor(pr(4), LP[:, k, 0, :], LP[:, k, 0, :], ALU.mult), reads=["LP"], writes=["prm"])
                P.op("dve", lambda e, k=k: e.tensor_tensor(pr(11), LP[:, k, 1, :], LP[:, k, 1, :], ALU.mult), reads=["LP"], writes=["prm"])
                P.op("dve", lambda e, k=k: e.tensor_tensor(LP[:, k + 1, 0, :], pr(4), pr(11), ALU.subtract), reads=["prm"], writes=["LP"])
                P.op("dve", lambda e, k=k: e.scalar_tensor_tensor(LP[:, k + 1, 1, :], LP[:, k, 0, :], 2.0, LP[:, k, 1, :], ALU.mult, ALU.mult), reads=["LP"], writes=["LP"])
            P.op("dve", lambda e: e.tensor_scalar(LP[:, :, 2, :], LP[:, :, 1, :], -1.0, None, ALU.mult), reads=["LP"], writes=["LP"])
            dv(lambda e: e.tensor_tensor(pr(7), lam[:, 0, :], lam[:, 0, :], ALU.mult))
            dv(lambda e: e.tensor_tensor(pr(4), lam[:, 1, :], lam[:, 1, :], ALU.mult))
            dv(lambda e: e.tensor_tensor(pr(7), pr(7), pr(4), ALU.add))
            dv(lambda e: e.reciprocal(pr(7), pr(7)))
            P.op("dve", lambda e: e.tensor_scalar(pr(8), LP[:, 0, 0, :], -1.0, None, ALU.add), reads=["LP"], writes=["prm"])
            dv(lambda e: e.tensor_tensor(pr(9), pr(8), lam[:, 0, :], ALU.mult))
            P.op("dve", lambda e: e.tensor_tensor(pr(4), LP[:, 0, 1, :], lam[:, 1, :], ALU.mult), reads=["LP", "lam"], writes=["prm"])
            dv(lambda e: e.tensor_tensor(pr(9), pr(9), pr(4), ALU.add))
            dv(lambda e: e.tensor_tensor(pr(9), pr(9), pr(7), ALU.mult))
            P.op("dve", lambda e: e.tensor_tensor(pr(10), LP[:, 0, 1, :], lam[:, 0, :], ALU.mult), reads=["LP", "lam"], writes=["prm"])
            dv(lambda e: e.tensor_tensor(pr(4), pr(8), lam[:, 1, :], ALU.mult))
            dv(lambda e: e.tensor_tensor(pr(10), pr(10), pr(4), ALU.subtract))
            dv(lambda e: e.tensor_tensor(pr(10), pr(10), pr(7), ALU.mult))
            P.op("dve", lambda e: e.tensor_tensor(pr(4), pr(3), pr(3), ALU.mult), reads=["prm"], writes=["prm"])
            P.op("dve", lambda e: e.reciprocal(pr(4), pr(4)), reads=["prm"], writes=["prm"])
            P.op("dve", lambda e: e.tensor_tensor(LI[:, 0, :], LP[:, 0, 0, :], pr(4), ALU.mult), reads=["LP", "prm"], writes=["LI"])
            P.op("dve", lambda e: e.tensor_tensor(LI[:, 1, :], LP[:, 0, 2, :], pr(4), ALU.mult), reads=["LP", "prm"], writes=["LI"])
            PW = sbs("PW", [128, 2, 16, 8])
            PN = sbs("PN", [128, 2, 16, 8])
            tq = sbs("tq", [128, 2, 16])
            for (T_, b0r, b0i, key) in ((PW, LP[:, 0, 0, :], LP[:, 0, 1, :], "PW"), (PN, LI[:, 0, :], LI[:, 1, :], "PN")):
                P.op("dve", lambda e, T_=T_, b0r=b0r: e.tensor_copy(T_[:, 0, :, 0], b0r), reads=["LP", "LI"], writes=[key])
                P.op("dve", lambda e, T_=T_, b0i=b0i: e.tensor_copy(T_[:, 1, :, 0], b0i), reads=["LP", "LI"], writes=[key])
                for j in range(7):
                    P.op("dve", lambda e, T_=T_, j=j, b0r=b0r: e.tensor_tensor(tq[:, 0, :], T_[:, 0, :, j], b0r, ALU.mult), reads=[key, "LP", "LI"], writes=["tq"])
                    P.op("dve", lambda e, T_=T_, j=j, b0i=b0i: e.tensor_tensor(tq[:, 1, :], T_[:, 1, :, j], b0i, ALU.mult), reads=[key, "LP", "LI"], writes=["tq"])
                    P.op("dve", lambda e, T_=T_, j=j: e.tensor_tensor(T_[:, 0, :, j + 1], tq[:, 0, :], tq[:, 1, :], ALU.subtract), reads=["tq"], writes=[key])
                    P.op("dve", lambda e, T_=T_, j=j, b0i=b0i: e.tensor_tensor(tq[:, 0, :], T_[:, 0, :, j], b0i, ALU.mult), reads=[key, "LP", "LI"], writes=["tq"])
                    P.op("dve", lambda e, T_=T_, j=j, b0r=b0r: e.tensor_tensor(tq[:, 1, :], T_[:, 1, :, j], b0r, ALU.mult), reads=[key, "LP", "LI"], writes=["tq"])
                    P.op("dve", lambda e, T_=T_, j=j: e.tensor_tensor(T_[:, 1, :, j + 1], tq[:, 0, :], tq[:, 1, :], ALU.add), reads=["tq"], writes=[key])
            bA = sbs("bA", [128, 2, 16, 16])
            CA = sbs("CA", [128, 2, 16, 16])
            for ri, nm in ((0, "s5_b_re"), (1, "s5_b_im")):
                v = dr[nm].rearrange("(g2 gp) n c -> gp n g2 c", gp=2)
                for gp in range(2):
                    P.dma("sp", bA[gp * 64:(gp + 1) * 64, ri, :, :], v[gp], writes=["bA"])
            cst = sbs("cst", [128, 4, 128])
            for ri in range(2):
                src = dr["s5_c_re" if ri == 0 else "s5_c_im"].rearrange("(ct g8) c n -> (g8 c) ct n", g8=8)
                P.dma("sp", cst[:, :, 0:64], src, writes=["cst"])
                P.dma("sp", cst[:, :, 64:128], src, writes=["cst"])
                for ct in range(4):
                    P.op("pe", lambda e, ct=ct: e.transpose(ps[7][:, 0:128], cst[:, ct, :], identF[:, :]), reads=["cst", "identF"], writes=["ps7"])
                    for gp in range(2):
                        srcv = ps[7][gp * 64:(gp + 1) * 64, 0:128].rearrange("p (j q c) -> p j q c", j=4, q=2)[:, :, gp, :]
                        P.op("dve", lambda e, ri=ri, ct=ct, gp=gp, srcv=srcv: e.tensor_copy(CA[gp * 64:(gp + 1) * 64, ri, ct * 4:(ct + 1) * 4, :], srcv), reads=["ps7"], writes=["CA"])
            BB = sbs("BB", [128, 2, 16, 16])
            t16a = sbs("t16a", [128, 16, 16])
            kb = lambda i: prm[:, i, :].unsqueeze(2).to_broadcast([128, 16, 16])
            P.op("dve", lambda e: e.tensor_tensor(BB[:, 0, :, :], bA[:, 0, :, :], kb(9), ALU.mult), reads=["bA", "prm"], writes=["BB"])
            P.op("dve", lambda e: e.tensor_tensor(t16a[:], bA[:, 1, :, :], kb(10), ALU.mult), reads=["bA", "prm"], writes=["t16a"])
            P.op("dve", lambda e: e.tensor_tensor(BB[:, 0, :, :], BB[:, 0, :, :], t16a[:], ALU.subtract), reads=["BB", "t16a"], writes=["BB"])
            P.op("dve", lambda e: e.tensor_tensor(BB[:, 1, :, :], bA[:, 1, :, :], kb(9), ALU.mult), reads=["bA", "prm"], writes=["BB"])
            P.op("dve", lambda e: e.tensor_tensor(t16a[:], bA[:, 0, :, :], kb(10), ALU.mult), reads=["bA", "prm"], writes=["t16a"])
            P.op("dve", lambda e: e.tensor_tensor(BB[:, 1, :, :], BB[:, 1, :, :], t16a[:], ALU.add), reads=["BB", "t16a"], writes=["BB"])
            TA = sbs("TA", [128, 16, 8, 16])
            TBb = sbs("TBb", [128, 16, 8, 16])
            T1 = sbs("T1", [128, 16, 8, 16])
            T2 = sbs("T2", [128, 16, 8, 16])
            Wall = sbs("Wall", [128, 2, 16, 128], BF16)
            Pz = sbs("Pz", [128, 2, 32, 128], BF16)
            SH = [128, 16, 8, 16]

            def outer_cplx(X, Pw, xk, pk_):
                xr = X[:, 0, :, :].unsqueeze(2).to_broadcast(SH)
                xi = X[:, 1, :, :].unsqueeze(2).to_broadcast(SH)
                wr = Pw[:, 0, :, :].unsqueeze(3).to_broadcast(SH)
                wi = Pw[:, 1, :, :].unsqueeze(3).to_broadcast(SH)
                P.op("dve", lambda e: e.tensor_tensor(TA[:], xr, wr, ALU.mult), reads=[xk, pk_], writes=["TA"])
                P.op("dve", lambda e: e.tensor_tensor(T1[:], xi, wi, ALU.mult), reads=[xk, pk_], writes=["T1"])
                P.op("dve", lambda e: e.tensor_tensor(TA[:], TA[:], T1[:], ALU.subtract), reads=["TA", "T1"], writes=["TA"])
                P.op("dve", lambda e: e.tensor_tensor(TBb[:], xr, wi, ALU.mult), reads=[xk, pk_], writes=["TBb"])
                P.op("dve", lambda e: e.tensor_tensor(T2[:], xi, wr, ALU.mult), reads=[xk, pk_], writes=["T2"])
                P.op("dve", lambda e: e.tensor_tensor(TBb[:], TBb[:], T2[:], ALU.add), reads=["TBb", "T2"], writes=["TBb"])

            f3 = lambda t: t[:].rearrange("p g a b -> p g (a b)")
            outer_cplx(CA, PW, "CA", "PW")
            P.op("act", lambda e: e.activation(Wall[:, 0, :, :], f3(TA), AF.Copy), reads=["TA"], writes=["Wall"])
            P.op("act", lambda e: e.activation(Wall[:, 1, :, :], f3(TBb), AF.Copy, scale=-1.0), reads=["TBb"], writes=["Wall"])
            P.op("pool", lambda e: e.memset(YwZ[:], 0.0), writes=["YwZ"])
            P.op("pool", lambda e: e.memset(Pz[:], 0.0), writes=["Pz"])
            for gp in range(2):
                H_ = slice(gp * 64, (gp + 1) * 64)
                for ri in range(2):
                    dst = YwZ[H_, ri, :, :].rearrange("p (g2 q) m -> p g2 q m", q=2)[:, :, gp, :]
                    P.op("dve", lambda e, dst=dst, ri=ri, H_=H_: e.tensor_copy(dst, Wall[H_, ri, :, :]), reads=["Wall"], writes=["YwZ"])
            outer_cplx(BB, PN, "BB", "PN")
            for gp in range(2):
                H_ = slice(gp * 64, (gp + 1) * 64)
                for ri, T_, tk in ((0, TA, "TA"), (1, TBb, "TBb")):
                    dst = Pz[H_, ri, :, :].rearrange("p (g2 q) m -> p g2 q m", q=2)[:, :, gp, :]
                    P.op("act", lambda e, dst=dst, T_=T_, H_=H_: e.activation(dst, T_[H_].rearrange("p g a b -> p g (a b)"), AF.Copy), reads=[tk], writes=["Pz"])
            maskM = sbs("maskM", [128, 128])
            P.op("pool", lambda e: e.affine_select(maskM[:], onesF[:], [[16, 8], [0, 16]], ALU.is_ge, 0.0, base=15, channel_multiplier=-1), reads=["onesF"], writes=["maskM"])
            for g4 in range(8):
                pb = ps[4 + g4 % 2]
                pk = PSK[4 + g4 % 2]
                for q in range(4):
                    g = g4 * 4 + q
                    g2 = g // 2
                    P.op("pe", lambda e, pb=pb, q=q, g=g, g2=g2: e.matmul(pb[:, q * 128:(q + 1) * 128], Pz[:, 0, g, :], Wall[:, 0, g2, :], start=True, stop=False), reads=["Pz", "Wall"], writes=[pk])
                    P.op("pe", lambda e, pb=pb, q=q, g=g, g2=g2: e.matmul(pb[:, q * 128:(q + 1) * 128], Pz[:, 1, g, :], Wall[:, 1, g2, :], start=False, stop=True), reads=["Pz", "Wall"], writes=[pk])
                P.op("dve", lambda e, pb=pb, g4=g4: e.tensor_tensor(Mw[:, g4 * 4:(g4 + 1) * 4, :], pb[:, :].rearrange("p (q m) -> p q m", q=4),
                                                                   maskM[:].unsqueeze(1).to_broadcast([128, 4, 128]), ALU.mult), reads=[pk, "maskM"], writes=["Mw"])
            l8r = LP[:, 3, 0, :].unsqueeze(2).to_broadcast([128, 16, 128])
            l8i = LP[:, 3, 1, :].unsqueeze(2).to_broadcast([128, 16, 128])
            P.op("dve", lambda e: e.tensor_tensor(f3(T1), f3(TA), l8r, ALU.mult), reads=["TA", "LP"], writes=["T1"])
            P.op("dve", lambda e: e.tensor_tensor(f3(T2), f3(TBb), l8i, ALU.mult), reads=["TBb", "LP"], writes=["T2"])
            P.op("dve", lambda e: e.tensor_tensor(f3(T1), f3(T1), f3(T2), ALU.subtract), reads=["T1", "T2"], writes=["T1"])
            P.op("dve", lambda e: e.tensor_tensor(f3(T2), f3(TA), l8i, ALU.mult), reads=["TA", "LP"], writes=["T2"])
            P.op("dve", lambda e: e.tensor_tensor(f3(TA), f3(TBb), l8r, ALU.mult), reads=["TBb", "LP"], writes=["TA"])
            P.op("dve", lambda e: e.tensor_tensor(f3(T2), f3(T2), f3(TA), ALU.add), reads=["T2", "TA"], writes=["T2"])
            for ri, T_, tk in ((0, T1, "T1"), (1, T2, "T2")):
                for g24 in range(4):
                    pb = ps[6 + (g24 % 2)]
                    pk = PSK[6 + (g24 % 2)]
                    for q in range(4):
                        g2 = g24 * 4 + q
                        P.op("pe", lambda e, pb=pb, q=q, g2=g2, T_=T_: e.transpose(pb[:, q * 128:(q + 1) * 128], T_[:, g2, :, :].rearrange("p a b -> p (a b)"), identF[:, :]), reads=[tk, "identF"], writes=[pk])
                    P.op("act", lambda e, pb=pb, ri=ri, g24=g24: e.activation(Zw[:, ri, g24 * 4:(g24 + 1) * 4, :].rearrange("p a b -> p (a b)"), pb[:, :], AF.Copy), reads=[pk], writes=["Zw"])
            for nm_, t_, k_ in (("c_Mw", Mw, "Mw"), ("c_Zw", Zw, "Zw"), ("c_YwZ", YwZ, "YwZ")):
                P.dma("sp", cache[nm_], t_[:].rearrange("p a b -> p (a b)") if len(t_.shape) == 3 else t_[:].rearrange("p a b c -> p (a b c)"), reads=[k_], writes=[nm_])
            P.dma("sp", cache["c_prm"], prm[:].rearrange("p a b -> p (a b)"), reads=["prm"], writes=["c_prm"])
            P.dma("sp", cache["c_LP"], LP[:].rearrange("p a b c -> p (a b c)"), reads=["LP"], writes=["c_LP"])
            P.dma("sp", cache["c_LI"], LI[:].rearrange("p a b -> p (a b)"), reads=["LI"], writes=["c_LI"])

        def s5_phase2(sbk, zT):
            outer = ExitStack()
            with outer:
                sbo = mk_sb(outer)
                lam = sbo("lam", [128, 3, 16])
                prm = sbo("prm", [128, 12, 16])
                LP = sbo("LP", [128, 10, 3, 16])
                LI = sbo("LI", [128, 2, 16])
                Mw = sbo("Mw", [128, 32, 128], BF16)
                Zw = sbo("Zw", [128, 2, 16, 128], BF16)
                YwZ = sbo("YwZ", [128, 2, 32, 128], BF16)
                pr = lambda i: prm[:, i, :]

                def dv(fn, reads=("prm", "lam")):
                    P.op("dve", fn, reads=list(reads), writes=["prm"])

                def ac(fn):
                    P.op("act", fn, reads=["prm", "lam"], writes=["prm"])
                for nm_, t_, k_ in (("c_Mw", Mw, "Mw"), ("c_Zw", Zw, "Zw"), ("c_YwZ", YwZ, "YwZ")):
                    P.dma("sp", t_[:].rearrange("p a b -> p (a b)") if len(t_.shape) == 3 else t_[:].rearrange("p a b c -> p (a b c)"), cache[nm_], writes=[k_])
                P.dma("sp", prm[:].rearrange("p a b -> p (a b)"), cache["c_prm"], writes=["prm"])
                P.dma("sp", LP[:].rearrange("p a b c -> p (a b c)"), cache["c_LP"], writes=["LP"])
                P.dma("sp", LI[:].rearrange("p a b -> p (a b)"), cache["c_LI"], writes=["LI"])
                with ExitStack() as ssc:
                    sbt = mk_sb(ssc)
                    alloc_norm(sbt)
                    uTs = sbt("uTs", [128, 8, 8, 128], BF16)
                    uS = sbt("uS", [128, 8, 16], BF16)
                    s5in = sbt("s5in", [128, 4, NC], BF16)
                    wch = sbt("wch", [128, 8, 512], BF16)
                    TMU = sbt("TMU", [128, 8192], BF16)
                    TM = TMU[:, 0:4096].rearrange("p (g s c) -> p g s c", g=32, s=8)
                    U = TMU[:, 4096:8192].rearrange("p (g c) -> p g c", g=32)
                    TMY = uTs[:].rearrange("p a b c -> p (a b c)").bitcast(F32).rearrange("p (t c) -> p t c", t=8)
                    XA = sbt("XA", [128, 2, 16, 128])
                    XB = sbt("XB", [128, 2, 16, 128])
                    Hp = wch[:].rearrange("p a b -> p (a b)").rearrange("p (r g c) -> p r g c", r=2, g=16)
                    Ysb = [sbt("Ysb%d" % i, [128, 512]) for i in range(2)]
                    yv = sbt("yv0", [128, 512])
                    gw = sbt("gw", [128, 512])
                    gs = sbt("gs", [128, 512])
                    yvB = [yv, sbt("yv2", [128, 512])]
                    gwB = [gw, sbt("gw2", [128, 512])]
                    gsB = [gs, sbt("gs2", [128, 512])]

                    ust = TMU[:, 0:4096].rearrange("p (k t) -> p k t", k=8)
                    tmp2v = TMU[:, 4096:5120].bitcast(F32)
                    for (c0, n) in ranges_of(sbk):
                        normmod(sbk, 1, ust, "TM", rngs=[(c0, n)], obase=c0, tmp2=tmp2v, tmp2_key="U")
                        P.dma("sp", uT_scr[:, :, c0:c0 + n], ust[:, :, 0:n], reads=["TM"], writes=["uT_scr"])
                        if is_samp(c0):
                            P.op("pool", lambda e: e.tensor_copy(uS[:, :, :], ust[:, :, 0:16]), reads=["TM"], writes=["uTs"])
                        else:
                            dstv = uTs[:, :, :, c0 // 8:(c0 + n) // 8].rearrange("p k s c -> p k c s")
                            srcv = ust[:, :, 0:n].rearrange("p k (c s) -> p k c s", s=8)
                            P.op("dve", lambda e, dstv=dstv, srcv=srcv: e.tensor_copy(dstv[:, 0:4], srcv[:, 0:4]), reads=["TM"], writes=["uTs"])
                            P.op("act", lambda e, dstv=dstv, srcv=srcv: e.activation(dstv[:, 4:8], srcv[:, 4:8], AF.Copy), reads=["TM"], writes=["uTs"])
                    P.dma("pool", wch[:], dr["w_in"][:, 0:512].rearrange("(kt p) n -> p kt n", p=128), writes=["wch"])
                    rngs = ranges_of(sbk)
                    for sg in range(8):
                        pb = ps[2 + (sg % 2)]
                        pk = PSK[2 + (sg % 2)]
                        for kt in range(8):
                            P.op("pe", lambda e, pb=pb, sg=sg, kt=kt: e.matmul(pb[:, :], uTs[:, kt, sg, :], wch[:, kt, :], start=(kt == 0), stop=(kt == 7)), reads=["uTs", "wch"], writes=[pk])
                        if sg % 2 == 0:
                            P.op("act", lambda e, pb=pb, sg=sg: e.activation(TM[:, :, sg, :], pb[:, :].rearrange("p (g c) -> p g c", g=32), AF.Copy), reads=[pk], writes=["TM"])
                        else:
                            P.op("dve", lambda e, pb=pb, sg=sg: e.tensor_copy(TM[:, :, sg, :], pb[:, :].rearrange("p (g c) -> p g c", g=32)), reads=[pk], writes=["TM"])
                    for g4 in range(8):
                        pbk = 4 + (g4 % 2)
                        for q in range(4):
                            g = g4 * 4 + q
                            P.op("pe", lambda e, pbk=pbk, q=q, g=g: e.transpose(psb16[pbk][:, q * 128:(q + 1) * 128], TM[:, g, :, :].rearrange("p a b -> p (a b)"), identB[:, :]), reads=["TM", "identB"], writes=[PSK[pbk]])
                        if g4 % 2 == 0:
                            P.op("act", lambda e, pbk=pbk, g4=g4: e.activation(U[:, g4 * 4:(g4 + 1) * 4, :].rearrange("p a b -> p (a b)"), psb16[pbk][:, 0:512], AF.Copy), reads=[PSK[pbk]], writes=["U"])
                        else:
                            P.op("dve", lambda e, pbk=pbk, g4=g4: e.tensor_copy(U[:, g4 * 4:(g4 + 1) * 4, :].rearrange("p a b -> p (a b)"), psb16[pbk][:, 0:512]), reads=[PSK[pbk]], writes=["U"])
                    for ri in range(2):
                        for g24 in range(4):
                            pb = ps[6 + (g24 % 2)]
                            pk = PSK[6 + (g24 % 2)]
                            for q in range(4):
                                g2 = g24 * 4 + q
                                for gp in range(2):
                                    g = 2 * g2 + gp
                                    P.op("pe", lambda e, pb=pb, q=q, g2=g2, gp=gp, g=g, ri=ri: e.matmul(pb[gp * 64:(gp + 1) * 64, q * 128:(q + 1) * 128], Zw[:, ri, g2, gp * 64:(gp + 1) * 64], U[:, g, :], start=True, stop=True),
                                         reads=["Zw", "U"], writes=[pk])
                            P.op("dve", lambda e, pb=pb, ri=ri, g24=g24: e.tensor_copy(XA[:, ri, g24 * 4:(g24 + 1) * 4, :].rearrange("p a b -> p (a b)"), pb[:, :]), reads=[pk], writes=["XA"])
                    cnt = 0
                    for ct in range(4):
                        for sq in range(2):
                            pb = ps[cnt % 2]
                            pk = PSK[cnt % 2]
                            cnt += 1
                            for q in range(4):
                                sg = sq * 4 + q
                                for kt in range(8):
                                    P.op("pe", lambda e, pb=pb, ct=ct, kt=kt, q=q, sg=sg: e.matmul(pb[:, q * 128:(q + 1) * 128], wch[:, kt, ct * 128:(ct + 1) * 128], uTs[:, kt, sg, :], start=(kt == 0), stop=(kt == 7)),
                                         reads=["wch", "uTs"], writes=[pk])
                            dstv = s5in[:, ct, 0:1024].rearrange("p (c s) -> p s c", s=8)[:, sq * 4:(sq + 1) * 4, :]
                            P.op("act", lambda e, pb=pb, dstv=dstv: e.activation(dstv, pb[:, :].rearrange("p (s c) -> p s c", s=4), AF.Copy), reads=[pk], writes=["s5in"])
                        if sbk == 1:
                            pb = ps[cnt % 2]
                            pk = PSK[cnt % 2]
                            cnt += 1
                            for kt in range(8):
                                P.op("pe", lambda e, pb=pb, ct=ct, kt=kt: e.matmul(pb[:, 0:16], wch[:, kt, ct * 128:(ct + 1) * 128], uS[:, kt, :], start=(kt == 0), stop=(kt == 7)), reads=["wch", "uTs"], writes=[pk])
                            P.op("act", lambda e, pb=pb, ct=ct: e.activation(s5in[:, ct, 1024:1040], pb[:, 0:16], AF.Copy), reads=[pk], writes=["s5in"])
                    L0 = 3
                    P.op("dve", lambda e: e.tensor_tensor(prm[:, 4, :], LP[:, L0, 0, :], s5car[:, 0, :], ALU.mult), reads=["LP", "s5car"], writes=["prm"])
                    P.op("dve", lambda e: e.tensor_tensor(prm[:, 11, :], LP[:, L0, 1, :], s5car[:, 1, :], ALU.mult), reads=["LP", "s5car"], writes=["prm"])
                    P.op("dve", lambda e: e.tensor_tensor(prm[:, 4, :], prm[:, 4, :], prm[:, 11, :], ALU.subtract), reads=["prm"], writes=["prm"])
                    P.op("dve", lambda e: e.tensor_tensor(XA[:, 0, :, 0], XA[:, 0, :, 0], prm[:, 4, :], ALU.add), reads=["XA", "prm"], writes=["XA"])
                    P.op("dve", lambda e: e.tensor_tensor(prm[:, 4, :], LP[:, L0, 0, :], s5car[:, 1, :], ALU.mult), reads=["LP", "s5car"], writes=["prm"])
                    P.op("dve", lambda e: e.tensor_tensor(prm[:, 11, :], LP[:, L0, 1, :], s5car[:, 0, :], ALU.mult), reads=["LP", "s5car"], writes=["prm"])
                    P.op("dve", lambda e: e.tensor_tensor(prm[:, 4, :], prm[:, 4, :], prm[:, 11, :], ALU.add), reads=["prm"], writes=["prm"])
                    P.op("dve", lambda e: e.tensor_tensor(XA[:, 1, :, 0], XA[:, 1, :, 0], prm[:, 4, :], ALU.add), reads=["XA", "prm"], writes=["XA"])
                    cur_, nxt_ = XA, XB
                    ck_, nk = "XA", "XB"
                    TB = 128
                    k = 0
                    P.op("pool", lambda e: e.engine_nop() if False else e.memset(gs[:, 0:1], 0.0), reads=["XA"], writes=[("XA", g2) for g2 in range(16)] + ["gs"])
                    while (1 << k) < TB:
                        s_ = 1 << k
                        P.op("act", lambda e, cur_=cur_, nxt_=nxt_, s_=s_: e.activation(nxt_[:, :, :, 0:s_], cur_[:, :, :, 0:s_], AF.Copy),
                             reads=[(ck_, g2) for g2 in range(16)], writes=[(nk, g2) for g2 in range(16)])
                        for opi in range(3):
                            for g2 in range(16):
                                lr = LP[:, L0 + k, 0, g2:g2 + 1]
                                li = LP[:, L0 + k, 1, g2:g2 + 1]
                                nli = LP[:, L0 + k, 2, g2:g2 + 1]
                                if opi == 0:
                                    P.op("dve", lambda e, cur_=cur_, nxt_=nxt_, s_=s_, g2=g2, lr=lr: e.scalar_tensor_tensor(nxt_[:, :, g2, s_:TB], cur_[:, :, g2, 0:TB - s_], lr, cur_[:, :, g2, s_:TB], ALU.mult, ALU.add),
                                         reads=[(ck_, g2), "LP"], writes=[(nk, g2, 0), (nk, g2, 1)])
                                elif opi == 1:
                                    P.op("dve", lambda e, cur_=cur_, nxt_=nxt_, s_=s_, g2=g2, nli=nli: e.scalar_tensor_tensor(nxt_[:, 0, g2, s_:TB], cur_[:, 1, g2, 0:TB - s_], nli, nxt_[:, 0, g2, s_:TB], ALU.mult, ALU.add), reads=[(ck_, g2), (nk, g2, 0), "LP"], writes=[(nk, g2, 0)])
                                else:
                                    P.op("dve", lambda e, cur_=cur_, nxt_=nxt_, s_=s_, g2=g2, li=li: e.scalar_tensor_tensor(nxt_[:, 1, g2, s_:TB], cur_[:, 0, g2, 0:TB - s_], li, nxt_[:, 1, g2, s_:TB], ALU.mult, ALU.add), reads=[(ck_, g2), (nk, g2, 1), "LP"], writes=[(nk, g2, 1)])
                        P.op("dve", lambda e: e.memset(gs[:, 0:1], 0.0), reads=[(nk, g2, r) for g2 in range(16) for r in range(2)] + [(ck_, g2) for g2 in range(16)],
                             writes=[(nk, g2) for g2 in range(16)] + [(ck_, g2, r) for g2 in range(16) for r in range(2)] + ["gs"])
                        cur_, nxt_ = nxt_, cur_
                        ck_, nk = nk, ck_
                        k += 1
                    P.op("dve", lambda e: e.memset(gs[:, 0:1], 0.0), reads=[(ck_, g2) for g2 in range(16)], writes=[ck_, "gs"])
                    P.op("act", lambda e: e.activation(Hp[:, :, :, 0], s5car[:, :, :], AF.Copy), reads=["s5car", "wch", "s5in", "TM"], writes=["wch"])
                    P.op("act", lambda e, cur_=cur_: e.activation(Hp[:, :, :, 1:128], cur_[:, :, :, 0:127], AF.Copy), reads=[ck_, "s5in", "TM"], writes=["wch"])
                    P.op("dve", lambda e, cur_=cur_: e.tensor_copy(s5car[:, :, :], cur_[:, :, :, 127]), reads=[ck_, "wch"], writes=["s5car"])
                    for g4 in range(8):
                        pb = ps[g4 % 2]
                        pk = PSK[g4 % 2]
                        for q in range(4):
                            g = g4 * 4 + q
                            g2 = g // 2
                            O = pb[:, q * 128:(q + 1) * 128]
                            P.op("pe", lambda e, O=O, g=g: e.matmul(O, Mw[:, g, :], U[:, g, :], start=True, stop=False), reads=["Mw", "U"], writes=[pk])
                            P.op("pe", lambda e, O=O, g=g, g2=g2: e.matmul(O, YwZ[:, 0, g, :], Hp[:, 0, g2, :], start=False, stop=False), reads=["YwZ", "wch"], writes=[pk])
                            P.op("pe", lambda e, O=O, g=g, g2=g2: e.matmul(O, YwZ[:, 1, g, :], Hp[:, 1, g2, :], start=False, stop=True), reads=["YwZ", "wch"], writes=[pk])
                        ysb = Ysb[g4 % 2]
                        yk = "Ysb%d" % (g4 % 2)
                        P.op("act", lambda e, pb=pb, ysb=ysb: e.activation(ysb[:, :], pb[:, :], AF.Copy), reads=[pk], writes=[yk])
                        pb2 = ps[2 + (g4 % 2)]
                        pk2 = PSK[2 + (g4 % 2)]
                        for q in range(4):
                            P.op("pe", lambda e, pb2=pb2, q=q, ysb=ysb: e.transpose(pb2[:, q * 128:(q + 1) * 128], ysb[:, q * 128:(q + 1) * 128], identF[:, :]), reads=[yk, "identF"], writes=[pk2])
                        dst = TMY[:, :, g4 * 64:(g4 + 1) * 64].rearrange("p t (q c) -> p q t c", q=4)
                        src = pb2[:, :].rearrange("p (q t c) -> p q t c", q=4, t=8)
                        P.op("dve", lambda e, dst=dst, src=src: e.tensor_copy(dst, src), reads=[pk2], writes=["uTs"])
                    for ct in range(4):
                        for tq_ in range(2):
                            pb = ps[4 + (tq_ % 2)]
                            pk = PSK[4 + (tq_ % 2)]
                            for q in range(4):
                                t_ = tq_ * 4 + q
                                P.op("pe", lambda e, pb=pb, q=q, t_=t_, ct=ct: e.transpose(pb[:, q * 128:(q + 1) * 128], TMY[:, t_, ct * 128:(ct + 1) * 128], identF[:, :]), reads=["uTs", "identF"], writes=[pk])
                            sv = s5in[:, ct, 0:1024].rearrange("p (c s) -> p s c", s=8)[:, tq_ * 4:(tq_ + 1) * 4, :]
                            zv = zT[:, ct, 0:1024].rearrange("p (c s) -> p s c", s=8)[:, tq_ * 4:(tq_ + 1) * 4, :]
                            v3 = lambda t: t[:, :].rearrange("p (s c) -> p s c", s=4)
                            yv_, gw_, gs_ = yvB[tq_], gwB[tq_], gsB[tq_]
                            yk_, wk_, sk_ = "yv%d" % tq_, "gw%d" % tq_, "gs%d" % tq_
                            P.op("dve", lambda e, pb=pb, ct=ct, sv=sv, yv_=yv_: e.scalar_tensor_tensor(v3(yv_), sv, CP[:, 104 + ct:105 + ct], v3(pb), ALU.mult, ALU.add), reads=["s5in", "CP", pk], writes=[yk_])
                            P.op("act", lambda e, yv_=yv_, gw_=gw_: e.activation(gw_[:, :], yv_[:, :], AF.Square), reads=[yk_], writes=[wk_])
                            P.op("dve", lambda e, gw_=gw_: e.tensor_scalar(gw_[:, :], gw_[:, :], 0.044715, 1.0, ALU.mult, ALU.add), reads=[wk_], writes=[wk_])
                            P.op("pool", lambda e, gw_=gw_, yv_=yv_: e.tensor_tensor(gw_[:, :], gw_[:, :], yv_[:, :], ALU.mult), reads=[wk_, yk_], writes=[wk_])
                            P.op("act", lambda e, gw_=gw_, gs_=gs_: e.activation(gs_[:, :], gw_[:, :], AF.Sigmoid, scale=1.5957691216057308), reads=[wk_], writes=[sk_])
                            P.op("dve", lambda e, zv=zv, yv_=yv_, gs_=gs_: e.tensor_tensor(zv, v3(yv_), v3(gs_), ALU.mult), reads=[yk_, sk_], writes=["zT"])
                    if sbk == 1:
                        for ri, nm in ((0, "s5reP"), (1, "s5imP")):
                            P.op("pe", lambda e, ri=ri: e.transpose(ps[4][0:16, 0:128], s5car[:, ri, :], identF[:, :]), reads=["s5car", "identF"], writes=["ps4"])
                            P.op("act", lambda e: e.activation(yv[0:16, 0:128], ps[4][0:16, 0:128], AF.Copy), reads=["ps4"], writes=["yv0"])
                            P.dma("sp", dr[nm], yv[0:16, 0:128], reads=["yv0"])
                        TMf = TMU[:, :].bitcast(F32)
                        TMs = TMU[0:16, 0:4096].rearrange("p (g s c) -> p g s c", g=32, s=8)
                        Us = sbt("Us", [128, 32, 16], BF16)
                        h0 = sbt("h0", [128, 2, 16, 16])
                        x1 = sbt("x1", [128, 2, 16, 16])
                        t16 = sbt("t16", [128, 16, 16])
                        Hps = sbt("Hps", [128, 2, 16, 16], BF16)
                        P.op("pool", lambda e: e.memset(TMU[0:16, 0:4096], 0.0), reads=["TM", "U"], writes=["TM"])
                        for ct in range(4):
                            P.op("pe", lambda e, ct=ct: e.transpose(psb16[0][0:16, ct * 128:(ct + 1) * 128], s5in[:, ct, 1024:1040], identB[:, :]), reads=["s5in", "identB"], writes=["ps0"])
                        P.op("act", lambda e: e.activation(TMs[:, :, 7, :], psb16[0][0:16, 0:512].rearrange("p (g c) -> p g c", g=32), AF.Copy), reads=["ps0"], writes=["TM"])
                        for g4 in range(8):
                            for q in range(4):
                                g = g4 * 4 + q
                                P.op("pe", lambda e, g=g: e.transpose(psb16[1][:, g * 16:(g + 1) * 16], TMs[:, g, :, :].rearrange("p a b -> p (a b)"), identB[0:16, 0:16]), reads=["TM", "identB"], writes=["ps1"])
                        P.op("act", lambda e: e.activation(Us[:].rearrange("p a b -> p (a b)"), psb16[1][:, 0:512], AF.Copy), reads=["ps1"], writes=["Us"])
                        for ri in range(2):
                            for g2 in range(16):
                                for gp in range(2):
                                    g = 2 * g2 + gp
                                    c_ = (ri * 16 + g2) * 16
                                    P.op("pe", lambda e, ri=ri, g2=g2, gp=gp, g=g, c_=c_: e.matmul(ps[2][gp * 64:(gp + 1) * 64, c_:c_ + 16], Zw[:, ri, g2, gp * 64:(gp + 1) * 64], Us[:, g, :], start=True, stop=True),
                                         reads=["Zw", "Us"], writes=["ps2"])
                        Zs = ps[2][:, :].rearrange("p (r g b) -> p r g b", r=2, g=16)
                        h0in = TMf[0:16, 2048:4096]
                        for ri in range(2):
                            P.dma("sp", h0in, dr["s5re0" if ri == 0 else "s5im0"], reads=["U", "TM", "Us"], writes=["U"])
                            for g2 in range(16):
                                P.op("pe", lambda e, ri=ri, g2=g2: e.transpose(ps[5][:, g2 * 16:(g2 + 1) * 16], h0in[:, g2 * 128:(g2 + 1) * 128], identF[0:16, 0:16]), reads=["U", "identF"], writes=["ps5"])
                            P.op("act", lambda e, ri=ri: e.activation(h0[:, ri, :, :].rearrange("p a b -> p (a b)"), ps[5][:, 0:256], AF.Copy), reads=["ps5"], writes=["h0"])
                        lrb = LP[:, 0, 0, :].unsqueeze(2).to_broadcast([128, 16, 16])
                        lib = LP[:, 0, 1, :].unsqueeze(2).to_broadcast([128, 16, 16])
                        P.op("dve", lambda e: e.tensor_tensor(x1[:, 0, :, :], h0[:, 0, :, :], lrb, ALU.mult), reads=["h0", "LP"], writes=["x1"])
                        P.op("dve", lambda e: e.tensor_tensor(t16[:], h0[:, 1, :, :], lib, ALU.mult), reads=["h0", "LP"], writes=["t16"])
                        P.op("dve", lambda e: e.tensor_tensor(x1[:, 0, :, :], x1[:, 0, :, :], t16[:], ALU.subtract), reads=["x1", "t16"], writes=["x1"])
                        P.op("dve", lambda e: e.tensor_tensor(x1[:, 0, :, :], x1[:, 0, :, :], Zs[:, 0, :, :], ALU.add), reads=["x1", "ps2"], writes=["x1"])
                        P.op("dve", lambda e: e.tensor_tensor(x1[:, 1, :, :], h0[:, 1, :, :], lrb, ALU.mult), reads=["h0", "LP"], writes=["x1"])
                        P.op("dve", lambda e: e.tensor_tensor(t16[:], h0[:, 0, :, :], lib, ALU.mult), reads=["h0", "LP"], writes=["t16"])
                        P.op("dve", lambda e: e.tensor_tensor(x1[:, 1, :, :], x1[:, 1, :, :], t16[:], ALU.add), reads=["x1", "t16"], writes=["x1"])
                        P.op("dve", lambda e: e.tensor_tensor(x1[:, 1, :, :], x1[:, 1, :, :], Zs[:, 1, :, :], ALU.add), reads=["x1", "ps2"], writes=["x1"])
                        for ri, nm in ((0, "s5reS"), (1, "s5imS")):
                            for half in range(2):
                                for q in range(8):
                                    g2 = half * 8 + q
                                    P.op("pe", lambda e, ri=ri, g2=g2, q=q, half=half: e.transpose(ps[6 + half][0:16, (q % 4) * 128:(q % 4 + 1) * 128], x1[:, ri, g2, :], identF[:, :]), reads=["x1", "identF"], writes=[PSK[6 + half]])
                                    if q == 3 or q == 7:
                                        lo = half * 1024 + (0 if q == 3 else 512)
                                        P.op("act", lambda e, half=half, lo=lo: e.activation(h0in[:, lo:lo + 512], ps[6 + half][0:16, :], AF.Copy), reads=[PSK[6 + half]], writes=["U"])
                            P.dma("sp", dr[nm], h0in, reads=["U"])
                        ir = LI[:, 0, :].unsqueeze(2).to_broadcast([128, 16, 16])
                        ii = LI[:, 1, :].unsqueeze(2).to_broadcast([128, 16, 16])
                        hx = h0
                        P.op("dve", lambda e: e.tensor_tensor(hx[:, 0, :, :], x1[:, 0, :, :], ir, ALU.mult), reads=["x1", "LI"], writes=["h0"])
                        P.op("dve", lambda e: e.tensor_tensor(t16[:], x1[:, 1, :, :], ii, ALU.mult), reads=["x1", "LI"], writes=["t16"])
                        P.op("dve", lambda e: e.tensor_tensor(hx[:, 0, :, :], hx[:, 0, :, :], t16[:], ALU.subtract), reads=["h0", "t16"], writes=["h0"])
                        P.op("dve", lambda e: e.tensor_tensor(hx[:, 1, :, :], x1[:, 1, :, :], ir, ALU.mult), reads=["x1", "LI"], writes=["h0"])
                        P.op("dve", lambda e: e.tensor_tensor(t16[:], x1[:, 0, :, :], ii, ALU.mult), reads=["x1", "LI"], writes=["t16"])
                        P.op("dve", lambda e: e.tensor_tensor(hx[:, 1, :, :], hx[:, 1, :, :], t16[:], ALU.add), reads=["h0", "t16"], writes=["h0"])
                        P.op("act", lambda e: e.activation(Hps[:], hx[:], AF.Copy), reads=["h0"], writes=["Hps"])
                        for g in range(32):
                            g2 = g // 2
                            P.op("pe", lambda e, g=g, g2=g2: e.matmul(ps[3][0:16, g * 16:(g + 1) * 16], Hps[:, 0, g2, :], YwZ[:, 0, g, 0:16], start=True, stop=False), reads=["Hps", "YwZ"], writes=["ps3"])
                            P.op("pe", lambda e, g=g, g2=g2: e.matmul(ps[3][0:16, g * 16:(g + 1) * 16], Hps[:, 1, g2, :], YwZ[:, 1, g, 0:16], start=False, stop=True), reads=["Hps", "YwZ"], writes=["ps3"])
                        P.op("act", lambda e: e.activation(yv[0:16, :], ps[3][0:16, :], AF.Copy), reads=["ps3"], writes=["yv0"])
                        for ct in range(4):
                            P.op("pe", lambda e, ct=ct: e.transpose(ps[4][:, ct * 16:(ct + 1) * 16], yv[0:16, ct * 128:(ct + 1) * 128], identF[0:16, 0:16]), reads=["yv0", "identF"], writes=["ps4"])
                        ysv = sbt("ysv", [128, 4, 16])
                        gws = sbt("gws", [128, 4, 16])
                        sv = s5in[:, :, 1024:1040]
                        for ct in range(4):
                            P.op("dve", lambda e, ct=ct: e.scalar_tensor_tensor(ysv[:, ct, :], s5in[:, ct, 1024:1040], CP[:, 104 + ct:105 + ct], ps[4][:, ct * 16:(ct + 1) * 16], ALU.mult, ALU.add), reads=["s5in", "CP", "ps4"], writes=["ysv"])
                        P.op("act", lambda e: e.activation(gws[:], ysv[:], AF.Square), reads=["ysv"], writes=["gws"])
                        P.op("dve", lambda e: e.tensor_scalar(gws[:], gws[:], 0.044715, 1.0, ALU.mult, ALU.add), reads=["gws"], writes=["gws"])
                        P.op("dve", lambda e: e.tensor_tensor(gws[:], gws[:], ysv[:], ALU.mult), reads=["gws", "ysv"], writes=["gws"])
                        P.op("act", lambda e: e.activation(gws[:], gws[:], AF.Sigmoid, scale=1.5957691216057308), reads=["gws"], writes=["gws"])
                        P.op("dve", lambda e: e.tensor_tensor(zT[:, :, 1024:1040], ysv[:], gws[:], ALU.mult), reads=["ysv", "gws"], writes=["zT"])
                    if sbk == 1:
                        dbg_dump("d_zT", zT[:].rearrange("p a b -> p (a b)"), "zT")
                    P.flush()

        C0 = -0.6065306597126334
        RWSTOP = int(os.environ.get('RWSTOP', '99'))

        class _Stop(Exception):
            pass

        def ck(l):
            if RWSTOP <= l:
                raise _Stop()

        def rwkv_phase(sbk, sbx, oT):
            nT2 = [sbx("uTt%d" % i, [128, 8, 128], BF16) for i in range(2)]
            tix = [0]
            wcur = sbx("wcur", [128, 8, NSH], BF16)
            wcur_f = wcur[:].rearrange("p a b -> p (a b)")
            if sbk == 0:
                for c4 in range(4):
                    lo = 512 + c4 * 512
                    w = 512 if c4 < 3 else 256
                    P.dma("pool", wcur[:, :, c4 * 512:c4 * 512 + w], dr["w_in"][:, lo:lo + w].rearrange("(kt p) n -> p kt n", p=128), writes=["wcur"])
                P.dma("sp", wcur_scr, wcur_f, reads=["wcur"], writes=["wcur_scr"])
            else:
                P.dma("sp", wcur_f[:, 0:4 * NSH], wcur_scr[:, 0:4 * NSH], writes=["wcur"])
                P.dma("sp", wcur_f[:, 4 * NSH:8 * NSH], wcur_scr[:, 4 * NSH:8 * NSH], writes=["wcur"])
            rowp = [sbx("rowp%d" % i, [1, 512]) for i in range(2)]
            bc = sbx("bc", [128, NSH + 7 * 512])
            pieces = [("mu_shift", q * 512, 512 if q < 3 else 256, q * 512) for q in range(4)]
            for i, nm in enumerate(("rwkv_w0", "rwkv_a0", "rwkv_k_k", "rwkv_k_a", "rwkv_ln_w", "rwkv_ln_b", "rwkv_r_k")):
                pieces.append((nm, 0, 512, NSH + i * 512))
            if sbk == 1:
                P.dma("sp", bc[:, :], bc_scr, writes=["bc"])
                pieces = []
            for pc, (nm, so, w, lo) in enumerate(pieces):
                rp = rowp[pc % 2]
                rk = "rowp%d" % (pc % 2)
                pb = ps[pc % 2]
                P.dma("sp", rp[0:1, 0:w], dr[nm][:, so:so + w], writes=[rk])
                P.op("pe", lambda e, pb=pb, rp=rp, w=w: e.matmul(pb[:, 0:w], onesF[0:1, :], rp[0:1, 0:w], start=True, stop=True), reads=["onesF", rk], writes=[PSK[pc % 2]])
                P.op("act", lambda e, pb=pb, lo=lo, w=w: e.activation(bc[:, lo:lo + w], pb[:, 0:w], AF.Copy), reads=[PSK[pc % 2]], writes=["bc"])
            if sbk == 0:
                P.dma("sp", bc_scr, bc[:, :], reads=["bc"], writes=["bc_scr"])
            mu_bc = bc[:, 0:NSH]
            def bcs(i):
                return bc[:, NSH + i * 512: NSH + (i + 1) * 512]
            w0_bc, a0_bc, kk_bc, ka_bc, lnw_bc, lnb_bc, rk_bc = [bcs(i) for i in range(7)]
            LW = sbx("LW", [128, 512], BF16)
            G2 = sbx("G2w", [128, 512], BF16)
            P.dma("pool", LW[0:64, :], dr["rwkv_w2"], writes=["LW"])
            P.dma("pool", LW[64:128, :], dr["rwkv_a2"], writes=["LW"])
            P.dma("pool", G2[:, :], dr["rwkv_g2"], writes=["G2w"])
            mSU = sbx("mSU", [128, 128])
            mIU = sbx("mIU", [128, 128])
            mSL = sbx("mSL", [128, 128])
            ShM = sbx("ShM", [128, 128])
            CM = sbx("CM", [128, 512])
            P.op("pool", lambda e: e.affine_select(mSU[:], onesF[:], [[1, 128]], ALU.is_gt, 0.0, base=0, channel_multiplier=-1), reads=["onesF"], writes=["mSU"])
            P.op("pool", lambda e: e.affine_select(mIU[:], onesF[:], [[1, 128]], ALU.is_ge, 0.0, base=0, channel_multiplier=-1), reads=["onesF"], writes=["mIU"])
            P.op("pool", lambda e: e.affine_select(mSL[:], onesF[:], [[-1, 128]], ALU.is_gt, 0.0, base=0, channel_multiplier=1), reads=["onesF"], writes=["mSL"])
            P.op("pool", lambda e: e.affine_select(ShM[:], onesF[:], [[1, 128]], ALU.is_equal, 0.0, base=-1, channel_multiplier=-1), reads=["onesF"], writes=["ShM"])
            P.op("pool", lambda e: e.tensor_tensor(ShM[:], ShM[:], identF[:], ALU.subtract), reads=["ShM", "identF"], writes=["ShM"])
            for q in range(4):
                src = mSU if q % 2 == 0 else mIU
                P.op("pool", lambda e, q=q, src=src: e.tensor_copy(CM[:, q * 128:(q + 1) * 128], src[:]), reads=["mSU", "mIU"], writes=["CM"])
            ck(1)
            cur = sbx("cur", [128, NSH])
            mixed = sbx("mixed", [128, NSH])
            Lb = sbx("Lb", [128, 256], BF16)
            LT = sbx("LT", [128, 2, 128], BF16)
            F = {n: sbx("f_" + n, [128, 512]) for n in ("logw", "a", "g", "kk", "k2", "kka", "t1", "t2", "y")}
            st8 = sbx("st8", [128, 6, 8])
            TB16 = {n: sbx("b_" + n, [128, 512], BF16) for n in ("At", "Bt", "Kt", "Rt", "Bg", "Kg", "Vb", "ob")}
            XT = {n: sbx("xt_" + n, [128, 4, 128], BF16) for n in ("At", "Bt", "Kt", "Rt")}
            gT = sbx("gT", [128, 4])
            NHG = 8
            AMall = sbx("AMall", [128, NHG, 512], BF16)
            AM = [AMall[:, i, :] for i in range(NHG)]
            ptmp = AMall[:, 0:2, :].rearrange("p a b -> p (a b)").bitcast(F32)
            Fg = [F["g"], sbx("f_g2", [128, 512])]
            bon = [sbx("bon%d" % i, [128, 512]) for i in range(2)]
            Ak = [[sbx("Ak%d_%d" % (i, j), [128, 384], BF16) for j in range(2)] for i in range(NHG)]
            W1 = [sbx("W1_%d" % i, [128, 64], BF16) for i in range(NHG)]
            AU = [sbx("AU%d" % i, [128, 128], BF16) for i in range(8)]
            RhT = sbx("RhT", [128, 8, 128], BF16)
            GTp = sbx("GTp", [128, 8, 64], BF16)
            P.op("pool", lambda e: e.memset(RhT[:], 0.0), writes=["RhT"])
            P.op("pool", lambda e: e.memset(GTp[:], 0.0), writes=["GTp"])
            r_ = mixed[:, 0:512]
            k_ = mixed[:, 512:1024]
            v_ = mixed[:, 1024:1536]

            def prelim(rows, tcols, first_tile, samp, par):
                R = slice(0, rows)
                gk = "f_g%d" % par
                nT = nT2[tix[0] % 2]
                nTk = "uTt%d" % (tix[0] % 2)
                tix[0] += 1
                P.dma("sp", nT[:, :, 0:rows], uT_scr[:, :, tcols], writes=[nTk])
                for pc in range(4):
                    lo = pc * 512
                    w = 512 if pc < 3 else 256
                    pb = ps[pc]
                    for kt in range(8):
                        P.op("pe", lambda e, pb=pb, kt=kt, lo=lo, w=w, nT=nT: e.matmul(pb[R, 0:w], nT[:, kt, 0:rows], wcur[:, kt, lo:lo + w], start=(kt == 0), stop=(kt == 7)),
                             reads=[nTk, "wcur"], writes=[PSK[pc]])
                    if pc % 2 == 0:
                        P.op("act", lambda e, pb=pb, lo=lo, w=w: e.activation(cur[R, lo:lo + w], pb[R, 0:w], AF.Copy), reads=[PSK[pc]], writes=["cur"])
                    else:
                        P.op("dve", lambda e, pb=pb, lo=lo, w=w: e.tensor_copy(cur[R, lo:lo + w], pb[R, 0:w]), reads=[PSK[pc]], writes=["cur"])
                ck(2)
                if not samp:
                    for pc in range(4):
                        lo = pc * 512
                        w = 512 if pc < 3 else 256
                        bsh = 4 + (pc % 3)
                        pb = ps[bsh]
                        P.op("pe", lambda e, pb=pb, lo=lo, w=w: e.matmul(pb[R, 0:w], ShM[R, 0:rows], cur[R, lo:lo + w], start=True, stop=first_tile), reads=["ShM", "cur"], writes=[PSK[bsh]])
                        if not first_tile:
                            P.op("pe", lambda e, pb=pb, lo=lo, w=w: e.matmul(pb[R, 0:w], identF[0:1, 0:rows], lastrow[0:1, lo:lo + w], start=False, stop=True), reads=["identF", "lastrow"], writes=[PSK[bsh]])
                        P.op("dve", lambda e, pb=pb, lo=lo, w=w: e.tensor_tensor(mixed[R, lo:lo + w], pb[R, 0:w], mu_bc[R, lo:lo + w], ALU.mult), reads=[PSK[bsh], "bc"], writes=["mixed"])
                        P.op("dve", lambda e, lo=lo, w=w: e.tensor_tensor(mixed[R, lo:lo + w], mixed[R, lo:lo + w], cur[R, lo:lo + w], ALU.add), reads=["mixed", "cur"], writes=["mixed"])
                    P.dma("sp", lastrow[0:1, :], cur[127:128, :], reads=["cur"], writes=["lastrow"])
                else:
                    P.dma("sp", mixed[R, :], dr["shift0"], writes=["mixed"])
                    P.op("dve", lambda e: e.tensor_tensor(mixed[R, :], mixed[R, :], cur[R, :], ALU.subtract), reads=["mixed", "cur"], writes=["mixed"])
                    P.op("dve", lambda e: e.tensor_tensor(mixed[R, :], mixed[R, :], mu_bc[R, :], ALU.mult), reads=["mixed", "bc"], writes=["mixed"])
                    P.op("dve", lambda e: e.tensor_tensor(mixed[R, :], mixed[R, :], cur[R, :], ALU.add), reads=["mixed", "cur"], writes=["mixed"])
                ck(3)
                m_l = P.mark()
                P.op("act", lambda e: e.activation(Lb[R, 0:64], mixed[R, 1536:1600], AF.Tanh), reads=["mixed"], writes=["Lb"])
                P.op("act", lambda e: e.activation(Lb[R, 64:128], mixed[R, 1600:1664], AF.Copy), reads=["mixed"], writes=["Lb"])
                P.op("act", lambda e: e.activation(Lb[R, 128:256], mixed[R, 1664:1792], AF.Sigmoid), reads=["mixed"], writes=["Lb"])
                for q in range(2):
                    P.op("pe", lambda e, q=q: e.transpose(psb16[0][:, q * 128:q * 128 + rows], Lb[R, q * 128:(q + 1) * 128], identB[0:rows, 0:rows]), reads=["Lb", "identB"], writes=["ps0"])
                P.op("dve", lambda e: e.tensor_copy(LT[:, :, 0:rows], psb16[0][:, 0:256].rearrange("p (q t) -> p q t", q=2)[:, :, 0:rows]), reads=["ps0"], writes=["LT"])
                P.op("pe", lambda e: e.matmul(ps[1][R, :], LT[0:64, 0, 0:rows], LW[0:64, :], start=True, stop=True), reads=["LT", "LW"], writes=["ps1"])
                P.op("pe", lambda e: e.matmul(ps[2][R, :], LT[64:128, 0, 0:rows], LW[64:128, :], start=True, stop=True), reads=["LT", "LW"], writes=["ps2"])
                P.op("pe", lambda e: e.matmul(ps[3][R, :], LT[:, 1, 0:rows], G2[:, :], start=True, stop=True), reads=["LT", "G2w"], writes=["ps3"])
                P.op("dve", lambda e: e.tensor_tensor(F["t1"][R, :], ps[1][R, :], w0_bc[R, :], ALU.add), reads=["ps1", "bc"], writes=["f_t1"])
                P.op("act", lambda e: e.activation(F["t1"][R, :], F["t1"][R, :], AF.Sigmoid), reads=["f_t1"], writes=["f_t1"])
                P.op("dve", lambda e: e.tensor_scalar(F["logw"][R, :], F["t1"][R, :], C0, None, ALU.mult), reads=["f_t1"], writes=["f_logw"])
                P.op("dve", lambda e: e.tensor_tensor(F["t2"][R, :], ps[2][R, :], a0_bc[R, :], ALU.add), reads=["ps2", "bc"], writes=["f_t2"])
                P.op("act", lambda e: e.activation(F["a"][R, :], F["t2"][R, :], AF.Sigmoid), reads=["f_t2"], writes=["f_a"])
                P.op("act", lambda e: e.activation(Fg[par][R, :], ps[3][R, :], AF.Copy), reads=["ps3"], writes=[gk])
                l_lora = P.take(m_l)
                P.op("dve", lambda e: e.tensor_tensor(F["kk"][R, :], k_[R, :], kk_bc[R, :], ALU.mult), reads=["mixed", "bc"], writes=["f_kk"])
                P.op("dve", lambda e: e.tensor_tensor(F["kka"][R, :], F["kk"][R, :], F["kk"][R, :], ALU.mult), reads=["f_kk"], writes=["f_kka"])
                P.op("dve", lambda e: e.tensor_reduce(st8[R, 0, :], F["kka"][R, :].rearrange("p (h j) -> p h j", h=8), AX.X, ALU.add), reads=["f_kka"], writes=["st8a"])
                l_kk = P.take(m_l)
                P.put_interleaved(l_lora, l_kk)
                P.op("act", lambda e: e.activation(st8[R, 0, :], st8[R, 0, :], AF.Sqrt), reads=["st8a"], writes=["st8a"])
                P.op("dve", lambda e: e.tensor_scalar_max(st8[R, 0, :], st8[R, 0, :], 1e-12), reads=["st8a"], writes=["st8a"])
                P.op("dve", lambda e: e.reciprocal(st8[R, 0, :], st8[R, 0, :]), reads=["st8a"], writes=["st8a"])
                P.op("dve", lambda e: e.tensor_tensor(F["kk"][R, :].rearrange("p (h j) -> p h j", h=8), F["kk"][R, :].rearrange("p (h j) -> p h j", h=8),
                                                      st8[R, 0, :].unsqueeze(2).to_broadcast([rows, 8, 64]), ALU.mult), reads=["f_kk", "st8a"], writes=["f_kk"])
                P.op("dve", lambda e: e.scalar_tensor_tensor(F["t2"][R, :], F["a"][R, :], -1.0, ka_bc[R, :], ALU.add, ALU.mult), reads=["f_a", "bc"], writes=["f_t2"])
                P.op("dve", lambda e: e.scalar_tensor_tensor(F["k2"][R, :], F["t2"][R, :], 1.0, k_[R, :], ALU.add, ALU.mult), reads=["f_t2", "mixed"], writes=["f_k2"])
                P.op("pool", lambda e: e.tensor_tensor(F["kka"][R, :], F["kk"][R, :], F["a"][R, :], ALU.mult), reads=["f_kk", "f_a"], writes=["f_kka"])

            def bonus(rows, par):
                R = slice(0, rows)
                v3p = lambda ap: ap.rearrange("p (h j) -> p h j", h=8)
                bk = "bon%d" % par
                P.op("dve", lambda e: e.tensor_tensor(F["t1"][R, :], r_[R, :], F["k2"][R, :], ALU.mult), reads=["mixed", "f_k2"], writes=["f_t1"])
                P.op("dve", lambda e: e.tensor_tensor(F["t1"][R, :], F["t1"][R, :], rk_bc[R, :], ALU.mult), reads=["f_t1", "bc"], writes=["f_t1"])
                P.op("dve", lambda e: e.tensor_reduce(st8[R, 3, :], v3p(F["t1"][R, :]), AX.X, ALU.add), reads=["f_t1"], writes=["st8a"])
                P.op("dve", lambda e: e.tensor_tensor(v3p(bon[par][R, :]), v3p(v_[R, :]), st8[R, 3, :].unsqueeze(2).to_broadcast([rows, 8, 64]), ALU.mult), reads=["mixed", "st8a"], writes=[bk])

            def post(rows, tcols, par):
                R = slice(0, rows)
                v3 = lambda ap: ap.rearrange("p (h j) -> p h j", h=8)
                bc8 = lambda i: st8[R, i, :].unsqueeze(2).to_broadcast([rows, 8, 64])
                y = F["y"]
                P.op("dve", lambda e: e.tensor_reduce(st8[R, 1, :], v3(y[R, :]), AX.X, ALU.add), reads=["f_y"], writes=["st8"])
                P.op("dve", lambda e: e.tensor_scalar(st8[R, 1, :], st8[R, 1, :], 1.0 / 64, None, ALU.mult), reads=["st8"], writes=["st8"])
                P.op("dve", lambda e: e.tensor_tensor(v3(y[R, :]), v3(y[R, :]), bc8(1), ALU.subtract), reads=["f_y", "st8"], writes=["f_y"])
                P.op("dve", lambda e: e.tensor_tensor(ptmp[R, :], y[R, :], y[R, :], ALU.mult), reads=["f_y", "AM0", "AM1"], writes=["AM0", "AM1"])
                P.op("dve", lambda e: e.tensor_reduce(st8[R, 2, :], v3(ptmp[R, :]), AX.X, ALU.add), reads=["AM0", "AM1"], writes=["st8"])
                P.op("act", lambda e: e.activation(st8[R, 2, :], st8[R, 2, :], AF.Sqrt, bias=epsc[R, 1:2], scale=1.0 / 64), reads=["st8", "epsc"], writes=["st8"])
                P.op("dve", lambda e: e.reciprocal(st8[R, 2, :], st8[R, 2, :]), reads=["st8"], writes=["st8"])
                P.op("dve", lambda e: e.tensor_tensor(v3(y[R, :]), v3(y[R, :]), bc8(2), ALU.mult), reads=["f_y", "st8"], writes=["f_y"])
                P.op("dve", lambda e: e.tensor_tensor(y[R, :], y[R, :], lnw_bc[R, :], ALU.mult), reads=["f_y", "bc"], writes=["f_y"])
                P.op("dve", lambda e: e.tensor_tensor(y[R, :], y[R, :], lnb_bc[R, :], ALU.add), reads=["f_y", "bc"], writes=["f_y"])
                P.op("dve", lambda e: e.tensor_tensor(y[R, :], y[R, :], bon[par][R, :], ALU.add), reads=["f_y", "bon%d" % par], writes=["f_y"])
                P.op("dve", lambda e: e.tensor_tensor(TB16["ob"][R, :], y[R, :], Fg[par][R, :], ALU.mult), reads=["f_y", "f_g%d" % par], writes=["b_ob"])
                for ct in range(4):
                    P.op("pe", lambda e, ct=ct: e.transpose(psb16[7][:, ct * 128:ct * 128 + rows], TB16["ob"][R, ct * 128:(ct + 1) * 128], identB[0:rows, 0:rows]), reads=["b_ob", "identB"], writes=["ps7"])
                P.op("act", lambda e: e.activation(oT[:, :, tcols], psb16[7][:, 0:512].rearrange("p (c t) -> p c t", c=4)[:, :, 0:rows], AF.Copy), reads=["ps7"], writes=["oT"])

            def chunk_tile(ti):
                rows = 128
                bonus(128, ti % 2)
                ck(5)
                P.op("pe", lambda e: e.matmul(ps[4][:, :], mIU[:, :], F["logw"][:, :], start=True, stop=True), reads=["mIU", "f_logw"], writes=["ps4"])
                P.op("pe", lambda e: e.matmul(ps[5][:, :], mSL[:, :], F["logw"][:, :], start=True, stop=True), reads=["mSL", "f_logw"], writes=["ps5"])
                for ct in range(4):
                    P.op("pe", lambda e, ct=ct: e.matmul(ps[6][:, ct:ct + 1], F["logw"][:, ct * 128:(ct + 1) * 128], onesF[:, 0:1], start=True, stop=True), reads=["f_logw", "onesF"], writes=["ps6"])
                P.op("act", lambda e: e.activation(gT[:, :], ps[6][:, 0:4], AF.Exp), reads=["ps6"], writes=["gT"])
                ex = F["y"]
                P.op("act", lambda e: e.activation(ex[:, :], ps[4][:, :], AF.Exp), reads=["ps4"], writes=["f_y"])
                P.op("dve", lambda e: e.tensor_tensor(TB16["Rt"][:, :], r_, ex[:, :], ALU.mult), reads=["mixed", "f_y"], writes=["b_Rt"])
                P.op("act", lambda e: e.activation(F["t2"][:, :], ps[4][:, :], AF.Exp, scale=-1.0), reads=["ps4"], writes=["f_t2"])
                P.op("dve", lambda e: e.tensor_tensor(TB16["Bt"][:, :], F["kka"][:, :], F["t2"][:, :], ALU.mult), reads=["f_kka", "f_t2"], writes=["b_Bt"])
                P.op("pool", lambda e: e.tensor_tensor(TB16["Kt"][:, :], F["k2"][:, :], F["t2"][:, :], ALU.mult), reads=["f_k2", "f_t2"], writes=["b_Kt"])
                P.op("dve", lambda e: e.tensor_tensor(F["t1"][:, :], ps[4][:, :], F["logw"][:, :], ALU.subtract), reads=["ps4", "f_logw"], writes=["f_t1"])
                P.op("act", lambda e: e.activation(F["t1"][:, :], F["t1"][:, :], AF.Exp), reads=["f_t1"], writes=["f_t1"])
                P.op("dve", lambda e: e.scalar_tensor_tensor(TB16["At"][:, :], F["kk"][:, :], -1.0, F["t1"][:, :], ALU.mult, ALU.mult), reads=["f_kk", "f_t1"], writes=["b_At"])
                P.op("act", lambda e: e.activation(F["t2"][:, :], ps[5][:, :], AF.Exp), reads=["ps5"], writes=["f_t2"])
                P.op("dve", lambda e: e.tensor_tensor(TB16["Bg"][:, :], F["kka"][:, :], F["t2"][:, :], ALU.mult), reads=["f_kka", "f_t2"], writes=["b_Bg"])
                P.op("pool", lambda e: e.tensor_tensor(TB16["Kg"][:, :], F["k2"][:, :], F["t2"][:, :], ALU.mult), reads=["f_k2", "f_t2"], writes=["b_Kg"])
                P.op("act", lambda e: e.activation(TB16["Vb"][:, :], v_, AF.Copy), reads=["mixed"], writes=["b_Vb"])
                ck(6)
                for qi, n in enumerate(("At", "Bt", "Kt", "Rt")):
                    pbk = qi % 4
                    for ct in range(4):
                        P.op("pe", lambda e, n=n, ct=ct, pbk=pbk: e.transpose(psb16[pbk][:, ct * 128:(ct + 1) * 128], TB16[n][:, ct * 128:(ct + 1) * 128], identB[:, :]), reads=["b_" + n, "identB"], writes=[PSK[pbk]])
                    if qi % 2 == 0:
                        P.op("act", lambda e, n=n, pbk=pbk: e.activation(XT[n][:].rearrange("p c t -> p (c t)"), psb16[pbk][:, 0:512], AF.Copy), reads=[PSK[pbk]], writes=["xt_" + n])
                    else:
                        P.op("dve", lambda e, n=n, pbk=pbk: e.tensor_copy(XT[n][:].rearrange("p c t -> p (c t)"), psb16[pbk][:, 0:512]), reads=[PSK[pbk]], writes=["xt_" + n])
                ck(7)
                for hg in range(8 // NHG):
                    heads = [hg * NHG + i for i in range(NHG)]
                    def hs(h):
                        return h // 2, 64 * (h % 2), h - hg * NHG
                    bank = lambda i, par: i
                    for h in heads:
                        ct, pb0, i = hs(h)
                        S = slice(pb0, pb0 + 64)
                        b = bank(i, 0)
                        P.op("pe", lambda e, ct=ct, S=S, b=b: e.matmul(ps[b][:, 0:128], XT["Bt"][S, ct, :], XT["At"][S, ct, :], start=True, stop=True), reads=["xt_Bt", "xt_At"], writes=[PSK[b]])
                        P.op("pe", lambda e, ct=ct, S=S, b=b: e.matmul(ps[b][:, 128:256], XT["Bt"][S, ct, :], XT["Rt"][S, ct, :], start=True, stop=True), reads=["xt_Bt", "xt_Rt"], writes=[PSK[b]])
                        P.op("pe", lambda e, ct=ct, S=S, b=b: e.matmul(ps[b][:, 256:384], XT["Kt"][S, ct, :], XT["At"][S, ct, :], start=True, stop=True), reads=["xt_Kt", "xt_At"], writes=[PSK[b]])
                        P.op("pe", lambda e, ct=ct, S=S, b=b: e.matmul(ps[b][:, 384:512], XT["Kt"][S, ct, :], XT["Rt"][S, ct, :], start=True, stop=True), reads=["xt_Kt", "xt_Rt"], writes=[PSK[b]])
                        P.op("dve", lambda e, i=i, b=b: e.tensor_tensor(AM[i][:, :], ps[b][:, :], CM[:, :], ALU.mult), reads=[PSK[b], "CM"], writes=["AM%d" % i])
                    ck(8)
                    for h in heads:
                        ct, pb0, i = hs(h)
                        S = slice(pb0, pb0 + 64)
                        b = bank(i, 1)
                        P.op("pe", lambda e, ct=ct, S=S, b=b: e.matmul(ps[b][:, 0:128], XT["At"][S, ct, :], XT["Bt"][S, ct, :], start=True, stop=True), reads=["xt_At", "xt_Bt"], writes=[PSK[b]])
                        P.op("dve", lambda e, i=i, b=b: e.tensor_tensor(Ak[i][0][:, 0:128], ps[b][:, 0:128], mSL[:, :], ALU.mult), reads=[PSK[b], "mSL"], writes=["Ak%d_0" % i])
                        P.op("act", lambda e, i=i: e.activation(Ak[i][0][:, 128:256], AM[i][:, 0:128], AF.Copy), reads=["AM%d" % i], writes=["Ak%d_0" % i])
                        P.op("dve", lambda e, i=i: e.tensor_tensor(Ak[i][0][:, 256:384], AM[i][:, 0:128], identB[:, :], ALU.add), reads=["AM%d" % i, "identB"], writes=["Ak%d_0" % i])
                    ck(9)
                    for lv in range(1, 8):
                        src, dst = (lv - 1) % 2, lv % 2
                        for h in heads:
                            ct, pb0, i = hs(h)
                            b = bank(i, 0)
                            rk = ["Ak%d_%d" % (i, src)]
                            if lv <= 6:
                                P.op("pe", lambda e, i=i, b=b, src=src: e.matmul(ps[b][:, 0:128], Ak[i][src][:, 128:256], Ak[i][src][:, 0:128], start=True, stop=True), reads=rk, writes=[PSK[b]])
                            if lv < 6:
                                P.op("pe", lambda e, i=i, b=b, src=src: e.matmul(ps[b][:, 128:256], Ak[i][src][:, 0:128], Ak[i][src][:, 128:256], start=True, stop=True), reads=rk, writes=[PSK[b]])
                            if lv >= 2:
                                P.op("pe", lambda e, i=i, b=b, src=src: e.matmul(ps[b][:, 256:384], Ak[i][src][:, 0:128], Ak[i][src][:, 256:384], start=True, stop=False), reads=rk, writes=[PSK[b]])
                                P.op("pe", lambda e, i=i, b=b, src=src: e.matmul(ps[b][:, 256:384], identB[:, :], Ak[i][src][:, 256:384], start=False, stop=True), reads=rk + ["identB"], writes=[PSK[b]])
                            else:
                                P.op("pe", lambda e, i=i, b=b, src=src: e.matmul(ps[b][:, 256:384], identB[:, :], Ak[i][src][:, 256:384], start=True, stop=True), reads=rk + ["identB"], writes=[PSK[b]])
                        for h in heads:
                            ct, pb0, i = hs(h)
                            b = bank(i, 0)
                            c_lo = 0 if lv <= 6 else 256
                            if i % 2 == 0:
                                P.op("act", lambda e, i=i, b=b, dst=dst, c_lo=c_lo: e.activation(Ak[i][dst][:, c_lo:384], ps[b][:, c_lo:384], AF.Copy), reads=[PSK[b]], writes=["Ak%d_%d" % (i, dst)])
                            else:
                                P.op("dve", lambda e, i=i, b=b, dst=dst, c_lo=c_lo: e.tensor_copy(Ak[i][dst][:, c_lo:384], ps[b][:, c_lo:384]), reads=[PSK[b]], writes=["Ak%d_%d" % (i, dst)])
                    pfin = 7 % 2
                    ck(10)
                    for h in heads:
                        ct, pb0, i = hs(h)
                        b = bank(i, 0)
                        P.op("pe", lambda e, i=i, b=b, h=h: e.matmul(ps[b][:, 0:64], AM[i][:, 256:384], TB16["Vb"][:, h * 64:(h + 1) * 64], start=True, stop=True), reads=["AM%d" % i, "b_Vb"], writes=[PSK[b]])
                        P.op("act", lambda e, i=i, b=b: e.activation(W1[i][:, :], ps[b][:, 0:64], AF.Copy), reads=[PSK[b]], writes=["W1_%d" % i])
                    for h in heads:
                        ct, pb0, i = hs(h)
                        b = bank(i, 1)
                        P.op("pe", lambda e, i=i, b=b, h=h: e.matmul(ps[b][:, 0:64], Ak[i][pfin][:, 256:384], TB16["At"][:, h * 64:(h + 1) * 64], start=True, stop=True), reads=["Ak%d_%d" % (i, pfin), "b_At"], writes=[PSK[b]])
                        P.op("pe", lambda e, i=i, b=b: e.matmul(ps[b][:, 64:128], Ak[i][pfin][:, 256:384], W1[i][:, :], start=True, stop=True), reads=["Ak%d_%d" % (i, pfin), "W1_%d" % i], writes=[PSK[b]])
                        P.op("dve", lambda e, h=h, b=b: e.tensor_copy(AU[h][:, :], ps[b][:, 0:128]), reads=[PSK[b]], writes=["AU%d" % h])
                    ck(11)
                    for h in heads:
                        ct, pb0, i = hs(h)
                        S = slice(pb0, pb0 + 64)
                        b = bank(i, 0)
                        RWVAR = os.environ.get("RWVAR", "abcd")
                        if "a" in RWVAR:
                            P.op("pe", lambda e, i=i, b=b, h=h, S=S: e.matmul(ps[b][S, 0:128], AU[h][:, 0:64], AM[i][:, 128:256], start=True, stop=True), reads=["AU%d" % h, "AM%d" % i], writes=[PSK[b]])
                        if "b" in RWVAR:
                            P.op("pe", lambda e, b=b, h=h, S=S: e.matmul(ps[b][S, 128:192], AU[h][:, 0:64], TB16["Bg"][:, h * 64:(h + 1) * 64], start=True, stop=True), reads=["AU%d" % h, "b_Bg"], writes=[PSK[b]])
                        if "c" in RWVAR:
                            P.op("dve", lambda e, b=b, S=S, ct=ct, h=h: e.tensor_tensor(RhT[S, h, :], ps[b][S, 0:128], XT["Rt"][S, ct, :], ALU.add), reads=[PSK[b], "xt_Rt"], writes=["RhT"])
                        if "d" in RWVAR:
                            P.op("dve", lambda e, b=b, S=S, h=h: e.tensor_copy(GTp[S, h, :], ps[b][S, 128:192]), reads=[PSK[b]], writes=["GTp"])
                    ck(12)
                    for h in heads:
                        ct, pb0, i = hs(h)
                        S = slice(pb0, pb0 + 64)
                        yb = 7
                        P.op("pe", lambda e, i=i, h=h: e.matmul(ps[7][:, h * 64:(h + 1) * 64], AM[i][:, 128:256], AU[h][:, 64:128], start=True, stop=False), reads=["AM%d" % i, "AU%d" % h], writes=["ps7"])
                        P.op("pe", lambda e, i=i, h=h: e.matmul(ps[7][:, h * 64:(h + 1) * 64], AM[i][:, 384:512], TB16["Vb"][:, h * 64:(h + 1) * 64], start=False, stop=False), reads=["AM%d" % i, "b_Vb"], writes=["ps7"])
                        P.op("pe", lambda e, h=h, S=S, ct=ct: e.matmul(ps[7][:, h * 64:(h + 1) * 64], RhT[:, h, :], Hb[:, ct, :], start=False, stop=True), reads=["RhT", "Hb"], writes=["ps7"])
                    GC = slice(hg * NHG * 64, (hg + 1) * NHG * 64)
                    P.op("act", lambda e, GC=GC: e.activation(F["y"][:, GC], ps[7][:, GC], AF.Copy), reads=["ps7"], writes=["f_y"])
                ck(13)
                for h in range(8):
                    ct, pb0 = h // 2, 64 * (h % 2)
                    S = slice(pb0, pb0 + 64)
                    P.op("pe", lambda e, h=h, S=S, ct=ct: e.matmul(ps[6][S, ct * 64:(ct + 1) * 64], GTp[:, h, :], Hb[:, ct, :], start=True, stop=False), reads=["GTp", "Hb"], writes=["ps6"])
                    P.op("pe", lambda e, h=h, S=S, ct=ct: e.matmul(ps[6][S, ct * 64:(ct + 1) * 64], TB16["Bg"][:, h * 64:(h + 1) * 64], AU[h][:, 64:128], start=False, stop=False), reads=["b_Bg", "AU%d" % h], writes=["ps6"])
                    P.op("pe", lambda e, h=h, S=S, ct=ct: e.matmul(ps[6][S, ct * 64:(ct + 1) * 64], TB16["Kg"][:, h * 64:(h + 1) * 64], TB16["Vb"][:, h * 64:(h + 1) * 64], start=False, stop=True), reads=["b_Kg", "b_Vb"], writes=["ps6"])
                for ct in range(4):
                    P.op("dve", lambda e, ct=ct: e.scalar_tensor_tensor(Hst[:, ct, :], Hst[:, ct, :], gT[:, ct:ct + 1], ps[6][:, ct * 64:(ct + 1) * 64], ALU.mult, ALU.add), reads=["Hst", "gT", "ps6"], writes=["Hst"])
                P.op("act", lambda e: e.activation(Hb[:], Hst[:], AF.Copy), reads=["Hst"], writes=["Hb"])

            prelim(128, slice(0, 128), first_tile=(sbk == 0), samp=False, par=0)
            for ti in range(8):
                tcols = slice(ti * 128, (ti + 1) * 128)
                if sbk == 1 and ti == 7:
                    P.dma("sp", dr["shiftP"], cur[127:128, :], reads=["cur"])
                chunk_tile(ti)
                m_p = P.mark()
                post(128, tcols, ti % 2)
                l_post = P.take(m_p)
                if ti < 7:
                    prelim(128, slice((ti + 1) * 128, (ti + 2) * 128), first_tile=False, samp=False, par=(ti + 1) % 2)
                elif sbk == 1:
                    prelim(16, slice(1024, 1040), first_tile=False, samp=True, par=0)
                l_pre = P.take(m_p)
                P.put_interleaved(l_post, l_pre)
            if sbk == 1:
                bonus(16, 0)
            if sbk == 1:
                ck(16)
                for ct in range(4):
                    P.op("pe", lambda e, ct=ct: e.transpose(ps[0][0:64, ct * 128:(ct + 1) * 128], Hst[:, ct, :], identF[:, :]), reads=["Hst", "identF"], writes=["ps0"])
                P.op("act", lambda e: e.activation(F["t1"][0:64, :], ps[0][0:64, :], AF.Copy), reads=["ps0"], writes=["f_t1"])
                P.dma("sp", dr["wkvP"].rearrange("(h i) j -> i h j", h=8), F["t1"][0:64, :].rearrange("i (h j) -> i h j", h=8), reads=["f_t1"])
                ck(17)
                tcols = slice(1024, 1040)
                P.dma("sp", dr["shiftS"], cur[0:16, :], reads=["cur"])
                ck(18)
                R = slice(0, 16)
                wf = wcur[:].rearrange("p a b -> p (a b)").bitcast(F32)
                pack = wf[0:16, 0:3072].rearrange("b (h q j) -> b h q j", h=8, q=6)
                v4 = lambda ap: ap.rearrange("p (h j) -> p h j", h=8)
                P.op("dve", lambda e: e.tensor_scalar(pack[:, :, 0, :], v4(F["kk"][R, :]), -1.0, None, ALU.mult), reads=["f_kk", "cur"], writes=["wcur"])
                P.op("act", lambda e: e.activation(pack[:, :, 1, :], v4(F["logw"][R, :]), AF.Exp), reads=["f_logw"], writes=["wcur"])
                P.op("dve", lambda e: e.tensor_copy(pack[:, :, 2, :], v4(F["kka"][R, :])), reads=["f_kka"], writes=["wcur"])
                P.op("dve", lambda e: e.tensor_copy(pack[:, :, 3, :], v4(F["k2"][R, :])), reads=["f_k2"], writes=["wcur"])
                P.op("dve", lambda e: e.tensor_copy(pack[:, :, 4, :], v4(r_[R, :])), reads=["mixed"], writes=["wcur"])
                P.op("dve", lambda e: e.tensor_copy(pack[:, :, 5, :], v4(v_[R, :])), reads=["mixed"], writes=["wcur"])
                P.dma("sp", scr1, wf[0:16, 0:3072], reads=["wcur"], writes=["scr1"])
                vec = wf[:, 5120:5504].rearrange("p (q j) -> p q j", q=6)
                P.dma("sp", wf[:, 5120:5504], scr1.rearrange("b (h x) -> (b h) x", h=8), reads=["scr1"], writes=["vec"])
                ysm = wf[:, 5504:5568]
                Sc = wf[:, 3072:4096].rearrange("p (i j) -> p i j", i=16)
                Tc = wf[:, 4096:5120].rearrange("p (i j) -> p i j", i=16)
                sa = wf[:, 5568:5584]
                bj = lambda q: vec[:, q, :].unsqueeze(1).to_broadcast([128, 16, 64])
                for ic in range(4):
                    isl = slice(ic * 16, (ic + 1) * 16)
                    P.dma("sp", wf[:, 3072:4096], dr["wkv0"][:, ic * 1024:(ic + 1) * 1024], reads=["scr1"], writes=["Sc"])
                    P.op("dve", lambda e: e.tensor_tensor(Tc, Sc, bj(0), ALU.mult), reads=["Sc", "vec"], writes=["Tc"])
                    P.op("dve", lambda e: e.tensor_reduce(sa, Tc, AX.X, ALU.add), reads=["Tc"], writes=["sa"])
                    P.op("dve", lambda e: e.tensor_tensor(Sc, Sc, bj(1), ALU.mult), reads=["Sc", "vec"], writes=["Sc"])
                    P.op("dve", lambda e: e.tensor_tensor(Tc, sa.unsqueeze(2).to_broadcast([128, 16, 64]), bj(2), ALU.mult), reads=["sa", "vec"], writes=["Tc"])
                    P.op("dve", lambda e: e.tensor_tensor(Sc, Sc, Tc, ALU.add), reads=["Sc", "Tc"], writes=["Sc"])
                    P.op("dve", lambda e, isl=isl: e.tensor_tensor(Tc, vec[:, 5, isl].unsqueeze(2).to_broadcast([128, 16, 64]), bj(3), ALU.mult), reads=["vec"], writes=["Tc"])
                    P.op("dve", lambda e: e.tensor_tensor(Sc, Sc, Tc, ALU.add), reads=["Sc", "Tc"], writes=["Sc"])
                    P.dma("sp", dr["wkvS"][:, ic * 1024:(ic + 1) * 1024], wf[:, 3072:4096], reads=["Sc"])
                    P.op("dve", lambda e: e.tensor_tensor(Tc, Sc, bj(4), ALU.mult), reads=["Sc", "vec"], writes=["Tc"])
                    P.op("dve", lambda e, isl=isl: e.tensor_reduce(ysm[:, isl], Tc, AX.X, ALU.add), reads=["Tc"], writes=["ysm"])
                P.dma("sp", scr2, ysm, reads=["ysm"], writes=["scr2"])
                P.dma("sp", F["y"][0:16, :], scr2.rearrange("(b h) i -> b (h i)", h=8), reads=["scr2"], writes=["f_y"])
                post(16, tcols, 0)

        def merge_phase(sbk, sbx, zT, oT):
            rngs = ranges_of(sbk)
            alloc_norm(sbx)
            nT = sbx("uT", [128, 8, NC], BF16)
            ncol_ = 1024 if sbk == 0 else NC
            P.dma("sp", nT[:, :, 0:ncol_], uT_scr[:, :, 0:ncol_], writes=["uT"])
            gv = sbx("gv", [128, 4, D], BF16)
            gg = sbx("gg", [128, 4, D], BF16)
            pj = sbx("pj", [128, 4, D], BF16)
            wo = sbx("wo", [128, 8, D], BF16)
            def ld_half(t_, nm, key, hh):
                P.dma("pool", t_[:, :, hh * 512:(hh + 1) * 512], dr[nm][:, hh * 512:(hh + 1) * 512].rearrange("(kt p) n -> p kt n", p=128), writes=[(key, hh)])
            ld_half(gv, "s5_glu_v", "gv", 0)
            ld_half(gg, "s5_glu_g", "gg", 0)
            wg = [sbx("wg%d" % i, [128, 8, 512], BF16) for i in range(2)]
            mT = sbx("mT", [128, 8, NC], BF16)
            ta = sbx("ta", [128, 512])
            tb = sbx("tb", [128, 512])
            tcg = sbx("tcg", [128, 512])
            for dg in range(2):
                if dg == 1:
                    ld_half(gv, "s5_glu_v", "gv", 1)
                    ld_half(gg, "s5_glu_g", "gg", 1)
                P.dma("pool", wg[0][:], dr["w_in"][:, 2304 + dg * 512: 2304 + (dg + 1) * 512].rearrange("(kt p) n -> p kt n", p=128), writes=["wg0"])
                ld_half(pj, "rwkv_proj", "pj", dg)
                P.dma("pool", wg[1][:], dr["w_in"][:, 3328 + dg * 512: 3328 + (dg + 1) * 512].rearrange("(kt p) n -> p kt n", p=128), writes=["wg1"])
                if dg == 1:
                    P.dma("pool", wo[:], dr["w_out"].rearrange("(kt p) n -> p kt n", p=128), writes=["wo"])
                for dd in range(4):
                    dt = dg * 4 + dd
                    DS = slice(dt * 128, (dt + 1) * 128)
                    for (c0, n) in rngs:
                        CS = slice(c0, c0 + n)
                        for kt in range(4):
                            P.op("pe", lambda e, kt=kt, DS=DS, CS=CS, n=n: e.matmul(ps[0][:, 0:n], gv[:, kt, DS], zT[:, kt, CS], start=(kt == 0), stop=(kt == 3)), reads=[("gv", dg), "zT"], writes=["ps0"])
                        for kt in range(4):
                            P.op("pe", lambda e, kt=kt, DS=DS, CS=CS, n=n: e.matmul(ps[1][:, 0:n], gg[:, kt, DS], zT[:, kt, CS], start=(kt == 0), stop=(kt == 3)), reads=[("gg", dg), "zT"], writes=["ps1"])
                        for kt in range(8):
                            P.op("pe", lambda e, kt=kt, dd=dd, CS=CS, n=n: e.matmul(ps[2][:, 0:n], wg[0][:, kt, dd * 128:(dd + 1) * 128], nT[:, kt, CS], start=(kt == 0), stop=(kt == 7)), reads=["wg0", "uT"], writes=["ps2"])
                        for kt in range(4):
                            P.op("pe", lambda e, kt=kt, DS=DS, CS=CS, n=n: e.matmul(ps[3][:, 0:n], pj[:, kt, DS], oT[:, kt, CS], start=(kt == 0), stop=(kt == 3)), reads=[("pj", dg), "oT"], writes=["ps3"])
                        for kt in range(8):
                            P.op("pe", lambda e, kt=kt, dd=dd, CS=CS, n=n: e.matmul(ps[4][:, 0:n], wg[1][:, kt, dd * 128:(dd + 1) * 128], nT[:, kt, CS], start=(kt == 0), stop=(kt == 7)), reads=["wg1", "uT"], writes=["ps4"])
                        P.op("act", lambda e, n=n: e.activation(ta[:, 0:n], ps[1][:, 0:n], AF.Sigmoid), reads=["ps1"], writes=["ta"])
                        P.op("dve", lambda e, n=n: e.tensor_tensor(ta[:, 0:n], ps[0][:, 0:n], ta[:, 0:n], ALU.mult), reads=["ps0", "ta"], writes=["ta"])
                        P.op("act", lambda e, n=n: e.activation(tb[:, 0:n], ps[2][:, 0:n], AF.Sigmoid), reads=["ps2"], writes=["tb"])
                        P.op("pool", lambda e, n=n: e.tensor_tensor(ta[:, 0:n], ta[:, 0:n], tb[:, 0:n], ALU.mult), reads=["ta", "tb"], writes=["ta"])
                        P.op("act", lambda e, n=n: e.activation(tcg[:, 0:n], ps[4][:, 0:n], AF.Sigmoid), reads=["ps4"], writes=["tcg"])
                        P.op("dve", lambda e, n=n: e.tensor_tensor(tcg[:, 0:n], ps[3][:, 0:n], tcg[:, 0:n], ALU.mult), reads=["ps3", "tcg"], writes=["tcg"])
                        P.op("dve", lambda e, n=n, dt=dt, CS=CS: e.tensor_tensor(mT[:, dt, CS], ta[:, 0:n], tcg[:, 0:n], ALU.add), reads=["ta", "tcg"], writes=["mT"])
            cnt = 0
            for dt in range(8):
                for (c0, n) in rngs:
                    k = 5 + (cnt % 2)
                    cnt += 1
                    for kt in range(8):
                        P.op("pe", lambda e, k=k, kt=kt, dt=dt, c0=c0, n=n: e.matmul(ps[k][:, 0:n], wo[:, kt, dt * 128:(dt + 1) * 128], mT[:, kt, c0:c0 + n], start=(kt == 0), stop=(kt == 7)), reads=["wo", "mT"], writes=[PSK[k]])
                    resid_update(1, dt, c0, n, ps[k], PSK[k])

        with ExitStack() as sp0:
            sbx = mk_sb(sp0)
            xin = [sbx("xin%d" % i, [128, D]) for i in range(2)]
            load_x(0, xin)
            m0 = P.mark()
            phase0(sbx)
            la = P.take(m0)
            s5_setup(sbx)
            lb = P.take(m0)
            P.put_interleaved(la, lb)
            P.flush()
        def mixer(sbk):
            with ExitStack() as sm:
                sbm = mk_sb(sm)
                zT = sbm("zT", [128, 4, NC], BF16)
                s5_phase2(sbk, zT)
                oT = sbm("oT", [128, 4, NC], BF16)
                with ExitStack() as s3:
                    rwkv_phase(sbk, mk_sb(s3), oT)
                    P.flush()
                with ExitStack() as s4:
                    merge_phase(sbk, mk_sb(s4), zT, oT)
                    P.flush()

        with ExitStack() as s1:
            sbx = mk_sb(s1)
            alloc_norm(sbx)
            ffn(0, 0, "ffn1_w1", "ffn1_w3", "ffn1_w2", sbx)
            P.flush()
        mixer(0)
        with ExitStack() as s15:
            sbx = mk_sb(s15)
            alloc_norm(sbx)
            fb = ffn_bufs(sbx)
            ffn(0, 2, "ffn2_w1", "ffn2_w3", "ffn2_w2", sbx, bufs=fb)
            yout = [sbx("yout%d" % i, [128, D]) for i in range(2)]
            final_out(0, yout)
            xin = [sbx("xin%d" % i, [128, D]) for i in range(2)]
            load_x(1, xin)
            ffn(1, 0, "ffn1_w1", "ffn1_w3", "ffn1_w2", sbx, bufs=fb)
            P.flush()
        mixer(1)
        with ExitStack() as s5:
            sbx = mk_sb(s5)
            alloc_norm(sbx)
            ffn(1, 2, "ffn2_w1", "ffn2_w3", "ffn2_w2", sbx)
            yout = [sbx("yout%d" % i, [128, D]) for i in range(2)]
            final_out(1, yout)
            P.flush(final=True)
        build_nc.n_instr = dict(P.n_instr)
    return nc


def _prep_core_inputs(inputs, b):
    f = lambda a: np.ascontiguousarray(np.asarray(a, dtype=np.float32))
    s = slice(16 * b, 16 * b + 16)
    m = {
        "xP": f(inputs["x_prompt"][b]),
        "xS": f(inputs["x_sample"][s, 0, :]),
        "cA": f(np.concatenate([inputs["c_prompt"][b:b + 1], inputs["c_sample"][s]], axis=0)),
        "s5re0": f(inputs["state_s5_re"][s].reshape(16, 2048)),
        "s5im0": f(inputs["state_s5_im"][s].reshape(16, 2048)),
        "wkv0": f(inputs["state_wkv"][s].reshape(128, 4096)),
        "shift0": f(inputs["state_shift"][s]),
    }
    for n, shp in W_SPECS:
        m[n] = f(inputs[n]).reshape(shp)
    return m


def kernel(**inputs):
    nc = build_nc()
    in_maps = [_prep_core_inputs(inputs, b) for b in range(8)]
    res = run_bass_kernel_spmd(nc, in_maps, core_ids=list(range(8)))
    R = res.results
    cat = lambda k: np.concatenate([np.asarray(r[k]) for r in R], axis=0)
    y_prompt = np.stack([np.asarray(r["yP"]) for r in R], axis=0)
    y_sample = cat("yS").reshape(128, 1, D)
    s5_re_p = np.stack([np.asarray(r["s5reP"]).reshape(32, 64) for r in R], axis=0)
    s5_im_p = np.stack([np.asarray(r["s5imP"]).reshape(32, 64) for r in R], axis=0)
    wkv_p = np.stack([np.asarray(r["wkvP"]).reshape(8, 64, 64) for r in R], axis=0)
    shift_p = cat("shiftP")
    s5_re_s = cat("s5reS").reshape(128, 32, 64)
    s5_im_s = cat("s5imS").reshape(128, 32, 64)
    wkv_s = cat("wkvS").reshape(128, 8, 64, 64)
    shift_s = cat("shiftS")
    return (y_prompt, y_sample, s5_re_p, s5_im_p, wkv_p, shift_p, s5_re_s, s5_im_s, wkv_s, shift_s)
```

```python
from contextlib import ExitStack
import numpy as np
import concourse.bass as bass
import concourse.mybir as mybir
from concourse.bass_utils import run_bass_kernel_spmd

F32 = mybir.dt.float32
BF16 = mybir.dt.bfloat16
AF = mybir.ActivationFunctionType
ALU = mybir.AluOpType
AX = mybir.AxisListType

COMPUTE = ("pe", "act", "dve", "pool")
N_DMA_SEMS = 24
import os as _os
SAME_ENGINE_FIFO = _os.environ.get('SEF', '0') == '1'
D = 1024
DFF = 2816
NJ = 22
NIN = 4352
NSH = 1792
EPS = 1e-6
GN_EPS = 64e-5


class Prog:
    def __init__(self, nc, stack):
        self.nc = nc
        self.ins = []
        self.sems = {e: stack.enter_context(nc.semaphore("s_" + e)) for e in COMPUTE}
        self.dsems = [stack.enter_context(nc.semaphore("d%d" % i)) for i in range(N_DMA_SEMS)]
        self.dcount = [0] * N_DMA_SEMS
        self.seq = {e: 0 for e in COMPUTE}
        self.last_w = {}
        self.readers = {}
        self.engs = list(COMPUTE) + ["sp"]
        self.waited = {e: {} for e in self.engs}
        self.ndma = 0
        self.barrier = False
        self.n_instr = {e: 0 for e in self.engs}

    def op(self, eng, fn, reads=(), writes=()):
        self.ins.append(dict(eng=eng, fn=fn, reads=tuple(reads), writes=tuple(writes), dma=False))

    def dma(self, eng, out, in_, reads=(), writes=(), **kw):
        def fn(e, out=out, in_=in_, kw=kw):
            return e.dma_start(out=out, in_=in_, **kw)
        self.ins.append(dict(eng=eng, fn=fn, reads=tuple(reads), writes=tuple(writes), dma=True))

    def mark(self):
        return len(self.ins)

    def take(self, mark):
        out = self.ins[mark:]
        del self.ins[mark:]
        return out

    def put_interleaved(self, A, B):
        ia = ib = 0
        while ia < len(A) or ib < len(B):
            if ib >= len(B) or (ia < len(A) and ia * len(B) <= ib * len(A)):
                self.ins.append(A[ia]); ia += 1
            else:
                self.ins.append(B[ib]); ib += 1

    def flush(self, final=False):
        nc = self.nc
        sems, dsems, dcount, seq = self.sems, self.dsems, self.dcount, self.seq
        last_w, readers, waited = self.last_w, self.readers, self.waited
        streams = {e: [] for e in self.engs}
        bar = None
        if self.barrier:
            bar = [("c", e, seq[e]) for e in COMPUTE if seq[e] > 0] + \
                  [("d", si, 16 * dcount[si]) for si in range(N_DMA_SEMS) if dcount[si] > 0]
            last_w.clear()
            readers.clear()
        first = {e: True for e in self.engs}
        for I in self.ins:
            E = I["eng"]
            deps = []
            if bar is not None and first[E]:
                deps.extend(bar)
            first[E] = False
            for r in I["reads"]:
                if r in last_w:
                    deps.append(last_w[r])
            for w in I["writes"]:
                if w in last_w:
                    deps.append(last_w[w])
                deps.extend(readers.get(w, ()))
            if I["dma"]:
                si = self.ndma % N_DMA_SEMS
                self.ndma += 1
                if dcount[si] > 0:
                    deps.append(("d", si, 16 * dcount[si]))
                dcount[si] += 1
                ev = ("d", si, 16 * dcount[si])
            else:
                seq[E] += 1
                ev = ("c", E, seq[E])
            need = {}
            for d in deps:
                if d[0] == "c" and d[1] == "pe" and E == "pe" and not I["dma"]:
                    continue
                if SAME_ENGINE_FIFO and d[0] == "c" and d[1] == E and not I["dma"]:
                    continue
                key = (d[0], d[1])
                if d[2] > need.get(key, 0):
                    need[key] = d[2]
            waits = []
            for key, v in need.items():
                if waited[E].get(key, 0) >= v:
                    continue
                waited[E][key] = v
                waits.append((key, v))
            streams[E].append((waits, I, ev))
            for r in I["reads"]:
                readers.setdefault(r, []).append(ev)
            for w in I["writes"]:
                last_w[w] = ev
                readers[w] = []
        for e in self.engs:
            self.n_instr[e] += len(streams[e])
        self.ins = []
        self.barrier = True

        def run_stream(E):
            def body(eng):
                for waits, I, ev in streams[E]:
                    for key, v in waits:
                        s = sems[key[1]] if key[0] == "c" else dsems[key[1]]
                        eng.wait_ge(s, v)
                    ins = I["fn"](eng)
                    if ev[0] == "c":
                        ins.then_inc(sems[E], 1)
                    else:
                        ins.then_inc(dsems[ev[1]], 16)
                if final and E == "sp":
                    for si in range(N_DMA_SEMS):
                        if dcount[si] > 0:
                            eng.wait_ge(dsems[si], 16 * dcount[si])
            return body

        with nc.Block() as block:
            block.sync(run_stream("sp"))
            block.tensor(run_stream("pe"))
            block.scalar(run_stream("act"))
            block.vector(run_stream("dve"))
            block.gpsimd(run_stream("pool"))


W_SPECS = [
    ("w_ada", [D, 9 * D]), ("ffn1_w1", [D, DFF]), ("ffn1_w3", [D, DFF]), ("ffn1_w2", [DFF, D]),
    ("ffn2_w1", [D, DFF]), ("ffn2_w3", [D, DFF]), ("ffn2_w2", [DFF, D]), ("w_in", [D, NIN]),
    ("s5_glu_v", [512, D]), ("s5_glu_g", [512, D]), ("rwkv_proj", [512, D]), ("w_out", [D, D]),
    ("rwkv_w2", [64, 512]), ("rwkv_a2", [64, 512]), ("rwkv_g2", [128, 512]),
    ("g_ffn1", [8, 128]), ("g_mix", [8, 128]), ("g_ffn2", [8, 128]), ("g_final", [8, 128]),
    ("b_ada", [72, 128]), ("s5_d", [4, 128]),
    ("mu_shift", [1, NSH]), ("rwkv_w0", [1, 512]), ("rwkv_a0", [1, 512]), ("rwkv_k_k", [1, 512]),
    ("rwkv_k_a", [1, 512]), ("rwkv_ln_w", [1, 512]), ("rwkv_ln_b", [1, 512]), ("rwkv_r_k", [1, 512]),
    ("s5_lam_re", [32, 64]), ("s5_lam_im", [32, 64]), ("s5_log_dt", [32, 1]),
    ("s5_b_re", [32, 64, 16]), ("s5_b_im", [32, 64, 16]), ("s5_c_re", [32, 16, 64]), ("s5_c_im", [32, 16, 64]),
]
IN_SPECS = [("xP", [2048, D]), ("xS", [16, D]), ("cA", [17, D]), ("s5re0", [16, 2048]), ("s5im0", [16, 2048]),
            ("wkv0", [128, 4096]), ("shift0", [16, NSH])]
OUT_SPECS = [("yP", [2048, D]), ("yS", [16, D]), ("s5reP", [16, 128]), ("s5imP", [16, 128]),
             ("wkvP", [512, 64]), ("shiftP", [1, NSH]), ("s5reS", [16, 2048]), ("s5imS", [16, 2048]),
             ("wkvS", [128, 4096]), ("shiftS", [16, NSH])]


def build_nc(debug=None, upto="all"):
    debug = debug or {}
    nc = bass.Bass("TRN2", target_bir_lowering=False)
    dr = {}
    for n, s in IN_SPECS + W_SPECS:
        dr[n] = nc.dram_tensor(n, s, F32, kind="ExternalInput").ap()
    for n, s in OUT_SPECS:
        dr[n] = nc.dram_tensor(n, s, F32, kind="ExternalOutput").ap()
    for n, s in debug.items():
        dr[n] = nc.dram_tensor(n, s, F32, kind="ExternalOutput").ap()
    scr1 = nc.dram_tensor("scr1", [16, 3072], F32).ap()
    scr2 = nc.dram_tensor("scr2", [128, 64], F32).ap()
    uT_scr = nc.dram_tensor("uT_scr", [128, 8 * 1040], BF16).ap().rearrange("p (k t) -> p k t", k=8)
    cache = {"c_Mw": nc.dram_tensor("c_Mw", [128, 32 * 128], BF16).ap(), "c_Zw": nc.dram_tensor("c_Zw", [128, 2 * 16 * 128], BF16).ap(),
             "c_YwZ": nc.dram_tensor("c_YwZ", [128, 2 * 32 * 128], BF16).ap(), "c_prm": nc.dram_tensor("c_prm", [128, 12 * 16], F32).ap(),
             "c_LP": nc.dram_tensor("c_LP", [128, 10 * 3 * 16], F32).ap(), "c_LI": nc.dram_tensor("c_LI", [128, 2 * 16], F32).ap()}

    st = ExitStack()
    with st:
        uniq = [0]

        def mk_sb(stack):
            def sb(name, shape, dt=F32):
                uniq[0] += 1
                return stack.enter_context(nc.sbuf_tensor("%s_%d" % (name, uniq[0]), shape, dt))
            return sb
        sb = mk_sb(st)
        P = Prog(nc, st)
        ps = [st.enter_context(nc.psum_tensor("ps%d" % i, [128, 512], F32)) for i in range(8)]
        PSK = ["ps%d" % i for i in range(8)]
        psb16 = [p_[:, :].bitcast(BF16) for p_ in ps]

        def dbg_dump(name, ap, key):
            if name in debug:
                P.dma("pool", dr[name], ap, reads=[key])

        onesF = sb("onesF", [128, 128])
        onesB = sb("onesB", [128, 128], BF16)
        identF = sb("identF", [128, 128])
        identB = sb("identB", [128, 128], BF16)
        epsc = sb("epsc", [128, 2])
        P.op("pool", lambda e: e.memset(onesF[:], 1.0), writes=["onesF"])
        P.op("pool", lambda e: e.memset(onesB[:], 1.0), writes=["onesB"])
        P.op("pool", lambda e: e.memset(epsc[:, 0:1], EPS), writes=["epsc"])
        P.op("pool", lambda e: e.memset(epsc[:, 1:2], GN_EPS), writes=["epsc"])
        mhalf = sb("mhalf", [128, 8])
        P.op("pool", lambda e: e.memset(mhalf[:], -0.5), writes=["mhalf"])
        P.op("pool", lambda e: e.affine_select(identF[:], onesF[:], [[-1, 128]], ALU.is_equal, 0.0, base=0, channel_multiplier=1),
             reads=["onesF"], writes=["identF"])
        P.op("pool", lambda e: e.affine_select(identB[:], onesF[:], [[-1, 128]], ALU.is_equal, 0.0, base=0, channel_multiplier=1),
             reads=["onesF"], writes=["identB"])

        NC = 1040
        CP = sb("CP", [128, 108])
        modT = sb("modT", [128, 72, 17])
        GSC = sb("GSC", [128, 3, 8, 17])
        GATE = sb("GATE", [128, 3, 8, 17])
        hT = sb("hT", [128, 8, NC])
        class _NT:
            pass
        NT = _NT()

        def alloc_norm(sbx_):
            NT.sq1 = [sbx_("sq1_%d" % i, [128, 512], BF16) for i in range(2)]
            NT.rstd = sbx_("rstd", [128, 512])
            NT.tmpn = sbx_("tmpn", [128, 512])
        lastrow = sb("lastrow", [1, NSH])
        Hst = sb("Hst", [128, 4, 64])
        Hb = sb("Hb", [128, 4, 64], BF16)
        s5car = sb("s5car", [128, 2, 16])
        P.op("pool", lambda e: e.memset(Hst[:], 0.0), writes=["Hst"])
        P.op("pool", lambda e: e.memset(Hb[:], 0.0), writes=["Hb"])
        P.op("pool", lambda e: e.memset(s5car[:], 0.0), writes=["s5car"])

        def ranges_of(sbk):
            return [(0, 512), (512, 512)] if sbk == 0 else [(0, 512), (512, 512), (1024, 16)]

        def is_samp(c0):
            return c0 == 1024

        def phase0(sb0):
            stg = sb0("stg", [108, 128])
            r0 = 0
            for n, k in (("g_ffn1", 8), ("g_mix", 8), ("g_ffn2", 8), ("g_final", 8), ("b_ada", 72), ("s5_d", 4)):
                P.dma("sp", stg[r0:r0 + k, :], dr[n], writes=["stg"])
                r0 += k
            P.op("pe", lambda e: e.transpose(ps[0][:, 0:108], stg[:, :], identF[0:108, 0:108]), reads=["stg", "identF"], writes=["ps0"])
            P.op("dve", lambda e: e.tensor_copy(CP[:], ps[0][:, 0:108]), reads=["ps0"], writes=["CP"])
            cin = sb0("cin", [17, D])
            csl = sb0("csl", [17, D])
            scT = sb0("scT", [128, 8, 17], BF16)
            P.dma("sp", cin[:], dr["cA"], writes=["cin"])
            P.op("act", lambda e: e.activation(csl[:], cin[:], AF.Silu), reads=["cin"], writes=["csl"])
            for kt in range(8):
                P.op("pe", lambda e, kt=kt: e.transpose(ps[1][:, kt * 17:(kt + 1) * 17], csl[:, kt * 128:(kt + 1) * 128], identF[0:17, 0:17]),
                     reads=["csl", "identF"], writes=["ps1"])
            P.op("dve", lambda e: e.tensor_copy(scT[:].rearrange("p a b -> p (a b)"), ps[1][:, 0:136]), reads=["ps1"], writes=["scT"])
            wada = [sb0("wada%d" % i, [128, 8, 512], BF16) for i in range(2)]
            for ch in range(18):
                wb = wada[ch % 2]
                wk = "wada%d" % (ch % 2)
                P.dma("pool", wb[:], dr["w_ada"][:, ch * 512:(ch + 1) * 512].rearrange("(kt p) n -> p kt n", p=128), writes=[wk])
                pb = ps[2 + (ch % 2)]
                pk = PSK[2 + (ch % 2)]
                for jj in range(4):
                    for kt in range(8):
                        P.op("pe", lambda e, wb=wb, pb=pb, jj=jj, kt=kt: e.matmul(pb[:, jj * 17:(jj + 1) * 17], wb[:, kt, jj * 128:(jj + 1) * 128],
                                                                                  scT[:, kt, :], start=(kt == 0), stop=(kt == 7)),
                             reads=[wk, "scT"], writes=[pk])
                for jj in range(4):
                    j = ch * 4 + jj
                    P.op("act", lambda e, pb=pb, jj=jj, j=j: e.activation(modT[:, j, :], pb[:, jj * 17:(jj + 1) * 17], AF.Identity, bias=CP[:, 32 + j:33 + j], scale=1.0),
                         reads=[pk, "CP"], writes=["modT"])
            for L in range(3):
                for dt in range(8):
                    P.op("dve", lambda e, L=L, dt=dt: e.tensor_scalar(GSC[:, L, dt, :], modT[:, (3 * L + 1) * 8 + dt, :], 1.0, CP[:, L * 8 + dt:L * 8 + dt + 1], ALU.add, ALU.mult),
                         reads=["modT", "CP"], writes=["GSC"])
                gsc = 1.0 if L == 1 else 0.5
                P.op("dve", lambda e, L=L, gsc=gsc: e.tensor_scalar(GATE[:, L, :, :], modT[:, (3 * L + 2) * 8:(3 * L + 3) * 8, :], gsc, None, ALU.mult),
                     reads=["modT"], writes=["GATE"])
            dbg_dump("d_modT", modT[:].rearrange("p a b -> p (a b)"), "modT")

        def sumsq_rstd(c0, n):
            for dt in range(8):
                sq = NT.sq1[dt % 2]
                sk = "sq1_%d" % (dt % 2)
                P.op("act", lambda e, sq=sq, dt=dt: e.activation(sq[:, 0:n], hT[:, dt, c0:c0 + n], AF.Square), reads=["hT"], writes=[sk])
                P.op("pe", lambda e, sq=sq, dt=dt: e.matmul(ps[7][:, 0:n], onesB[:, :], sq[:, 0:n], start=(dt == 0), stop=(dt == 7)),
                     reads=["onesB", sk], writes=["ps7"])
            P.op("act", lambda e: e.activation(NT.rstd[:, 0:n], ps[7][:, 0:n], AF.Sqrt, bias=epsc[:, 0:1], scale=1.0 / D), reads=["ps7", "epsc"], writes=["rstd"])
            P.op("dve", lambda e: e.reciprocal(NT.rstd[:, 0:n], NT.rstd[:, 0:n]), reads=["rstd"], writes=["rstd"])

        def normmod(sbk, L, out_t, out_key0, rngs=None, obase=0, tmp2=None, tmp2_key="tmpn2", krange=False):
            temps = [(NT.tmpn, "tmpn"), ((tmp2, tmp2_key) if tmp2 is not None else (NT.tmpn, "tmpn"))]
            for (c0, n) in (rngs if rngs is not None else ranges_of(sbk)):
                o0 = c0 - obase
                out_key = (out_key0, c0) if krange else out_key0
                sumsq_rstd(c0, n)
                for dt in range(8):
                    tt_, tk_ = temps[dt % 2]
                    if not is_samp(c0):
                        P.op("dve", lambda e, dt=dt, c0=c0, n=n, tt_=tt_: e.scalar_tensor_tensor(tt_[:, 0:n], hT[:, dt, c0:c0 + n], GSC[:, L, dt, 0:1], NT.rstd[:, 0:n], ALU.mult, ALU.mult),
                             reads=["hT", "GSC", "rstd"], writes=[tk_])
                        P.op("act", lambda e, dt=dt, c0=c0, n=n, o0=o0, tt_=tt_: e.activation(out_t[:, dt, o0:o0 + n], tt_[:, 0:n], AF.Identity, bias=modT[:, (3 * L) * 8 + dt, 0:1], scale=1.0),
                             reads=[tk_, "modT"], writes=[out_key])
                    else:
                        P.op("dve", lambda e, dt=dt, c0=c0, n=n: e.tensor_tensor(NT.tmpn[:, 0:n], hT[:, dt, c0:c0 + n], NT.rstd[:, 0:n], ALU.mult), reads=["hT", "rstd"], writes=["tmpn"])
                        P.op("dve", lambda e, dt=dt, n=n: e.tensor_tensor(NT.tmpn[:, 0:n], NT.tmpn[:, 0:n], GSC[:, L, dt, 1:17], ALU.mult), reads=["tmpn", "GSC"], writes=["tmpn"])
                        P.op("dve", lambda e, dt=dt, c0=c0, n=n, o0=o0: e.tensor_tensor(out_t[:, dt, o0:o0 + n], NT.tmpn[:, 0:n], modT[:, (3 * L) * 8 + dt, 1:17], ALU.add),
                             reads=["tmpn", "modT"], writes=[out_key])

        def resid_update(L, dt, c0, n, pb, pk):
            if not is_samp(c0):
                P.op("dve", lambda e: e.scalar_tensor_tensor(hT[:, dt, c0:c0 + n], pb[:, 0:n], GATE[:, L, dt, 0:1], hT[:, dt, c0:c0 + n], ALU.mult, ALU.add),
                     reads=[pk, "GATE", "hT"], writes=["hT"])
            else:
                P.op("dve", lambda e: e.tensor_tensor(NT.tmpn[:, 0:n], pb[:, 0:n], GATE[:, L, dt, 1:17], ALU.mult), reads=[pk, "GATE"], writes=["tmpn"])
                P.op("dve", lambda e: e.tensor_tensor(hT[:, dt, c0:c0 + n], hT[:, dt, c0:c0 + n], NT.tmpn[:, 0:n], ALU.add), reads=["tmpn", "hT"], writes=["hT"])

        def load_x(sbk, xin):
            for ti in range(8 + (1 if sbk == 1 else 0)):
                xb_ = xin[ti % 2]
                xk = "xin%d" % (ti % 2)
                if ti < 8:
                    rows = 128
                    P.dma("sp", xb_[:, :], dr["xP"][sbk * 1024 + ti * 128: sbk * 1024 + (ti + 1) * 128, :], writes=[xk])
                else:
                    rows = 16
                    P.dma("sp", xb_[0:16, :], dr["xS"], writes=[xk])
                for half in range(2):
                    pb = ps[half]
                    for q in range(4):
                        dt = half * 4 + q
                        P.op("pe", lambda e, pb=pb, q=q, dt=dt, xb_=xb_, rows=rows: e.transpose(pb[:, q * 128:q * 128 + rows], xb_[0:rows, dt * 128:(dt + 1) * 128], identF[0:rows, 0:rows]),
                             reads=[xk, "identF"], writes=[PSK[half]])
                    src = pb[:, :].rearrange("p (q t) -> p q t", q=4)[:, :, 0:rows]
                    dst = hT[:, half * 4:(half + 1) * 4, ti * 128:ti * 128 + rows]
                    if half == 0:
                        P.op("dve", lambda e, dst=dst, src=src: e.tensor_copy(dst, src), reads=[PSK[half]], writes=["hT"])
                    else:
                        P.op("act", lambda e, dst=dst, src=src: e.activation(dst, src, AF.Copy), reads=[PSK[half]], writes=["hT"])

        def ffn(sbk, L, w1n, w3n, w2n, sbx):
            nT = sbx("nT", [128, 8, NC], BF16)
            hid = sbx("hid", [128, NJ, NC], BF16)
            wA = [sbx("wA%d" % i, [128, 2, 8, 256], BF16) for i in range(2)]
            wB = [sbx("wB%d" % i, [128, NJ, 128], BF16) for i in range(2)]
            silu_t = [sbx("silu%d" % i, [128, 512]) for i in range(2)]
            tmpn2 = sbx("tmpn2", [128, 512])
            normmod(sbk, L, nT, "nT", tmp2=tmpn2, krange=True)
            rngs = ranges_of(sbk)
            cnt = [0]
            for ch in range(11):
                wb = wA[ch % 2]
                wk = "wA%d" % (ch % 2)
                P.dma("pool", wb[:, 0, :, :], dr[w1n][:, ch * 256:(ch + 1) * 256].rearrange("(kt p) n -> p kt n", p=128), writes=[wk])
                P.dma("pool", wb[:, 1, :, :], dr[w3n][:, ch * 256:(ch + 1) * 256].rearrange("(kt p) n -> p kt n", p=128), writes=[wk])
                for jj in range(2):
                    j = ch * 2 + jj
                    for (c0, n) in rngs:
                        k = cnt[0] % 2
                        cnt[0] += 1
                        p1, p3 = ps[2 * k], ps[2 * k + 1]
                        for kt in range(8):
                            P.op("pe", lambda e, p1=p1, wb=wb, jj=jj, kt=kt, c0=c0, n=n: e.matmul(p1[:, 0:n], wb[:, 0, kt, jj * 128:(jj + 1) * 128], nT[:, kt, c0:c0 + n], start=(kt == 0), stop=(kt == 7)),
                                 reads=[wk, ("nT", c0)], writes=[PSK[2 * k]])
                        for kt in range(8):
                            P.op("pe", lambda e, p3=p3, wb=wb, jj=jj, kt=kt, c0=c0, n=n: e.matmul(p3[:, 0:n], wb[:, 1, kt, jj * 128:(jj + 1) * 128], nT[:, kt, c0:c0 + n], start=(kt == 0), stop=(kt == 7)),
                                 reads=[wk, ("nT", c0)], writes=[PSK[2 * k + 1]])
                        sl = silu_t[k]
                        slk = "silu%d" % k
                        P.op("act", lambda e, sl=sl, p1=p1, n=n: e.activation(sl[:, 0:n], p1[:, 0:n], AF.Silu), reads=[PSK[2 * k]], writes=[slk])
                        P.op("dve", lambda e, sl=sl, p3=p3, j=j, c0=c0, n=n: e.tensor_tensor(hid[:, j, c0:c0 + n], p3[:, 0:n], sl[:, 0:n], ALU.mult),
                             reads=[PSK[2 * k + 1], slk], writes=[("hid", j, c0)])
            for dt in range(8):
                wb = wB[dt % 2]
                wk = "wB%d" % (dt % 2)
                P.dma("pool", wb[:, :, :], dr[w2n][:, dt * 128:(dt + 1) * 128].rearrange("(j p) n -> p j n", p=128), writes=[wk])
                for (c0, n) in rngs:
                    k = 4 + (cnt[0] % 2)
                    cnt[0] += 1
                    pb = ps[k]
                    for j in range(NJ):
                        P.op("pe", lambda e, pb=pb, wb=wb, j=j, c0=c0, n=n: e.matmul(pb[:, 0:n], wb[:, j, :], hid[:, j, c0:c0 + n], start=(j == 0), stop=(j == NJ - 1)),
                             reads=[wk, ("hid", j, c0)], writes=[PSK[k]])
                    resid_update(L, dt, c0, n, pb, PSK[k])

        def final_out(sbk, yout):
            for (c0, n) in ranges_of(sbk):
                sumsq_rstd(c0, n)
                for dt in range(8):
                    P.op("dve", lambda e, dt=dt, c0=c0, n=n: e.scalar_tensor_tensor(hT[:, dt, c0:c0 + n], hT[:, dt, c0:c0 + n], CP[:, 24 + dt:25 + dt], NT.rstd[:, 0:n], ALU.mult, ALU.mult),
                         reads=["hT", "CP", "rstd"], writes=["hT"])
            for ti in range(8 + (1 if sbk == 1 else 0)):
                rows = 128 if ti < 8 else 16
                yo = yout[ti % 2]
                yk = "yout%d" % (ti % 2)
                for half in range(2):
                    pb = ps[half]
                    for q in range(4):
                        dt = half * 4 + q
                        P.op("pe", lambda e, pb=pb, q=q, dt=dt, ti=ti, rows=rows: e.transpose(pb[0:rows, q * 128:(q + 1) * 128], hT[:, dt, ti * 128:ti * 128 + rows], identF[:, :]),
                             reads=["hT", "identF"], writes=[PSK[half]])
                    if half == 0:
                        P.op("dve", lambda e, pb=pb, yo=yo, rows=rows: e.tensor_copy(yo[0:rows, 0:512], pb[0:rows, :]), reads=[PSK[half]], writes=[yk])
                    else:
                        P.op("act", lambda e, pb=pb, yo=yo, rows=rows: e.activation(yo[0:rows, 512:1024], pb[0:rows, :], AF.Copy), reads=[PSK[half]], writes=[yk])
                if ti < 8:
                    P.dma("sp", dr["yP"][sbk * 1024 + ti * 128: sbk * 1024 + (ti + 1) * 128, :], yo[:, :], reads=[yk])
                else:
                    P.dma("sp", dr["yS"], yo[0:16, :], reads=[yk])

        TWO_PI = 6.283185307179586
        import os
        S5STOP = int(os.environ.get('S5STOP', '99'))

        def s5_phase(sbk, sbx, zT):
            rngs = ranges_of(sbk)
            nT = sbx("uT", [128, 8, NC], BF16)
            normmod(sbk, 1, nT, "uT")
            if sbk == 1:
                dbg_dump("d_uT", nT[:].rearrange("p a b -> p (a b)"), "uT")
            s5in = sbx("s5in", [128, 4, NC], BF16)
            wch = sbx("wch", [128, 8, 512], BF16)
            P.dma("pool", wch[:], dr["w_in"][:, 0:512].rearrange("(kt p) n -> p kt n", p=128), writes=["wch"])
            cnt = 0
            for ct in range(4):
                for (c0, n) in rngs:
                    pb = ps[cnt % 2]
                    pk = PSK[cnt % 2]
                    cnt += 1
                    for kt in range(8):
                        P.op("pe", lambda e, pb=pb, ct=ct, kt=kt, c0=c0, n=n: e.matmul(pb[:, 0:n], wch[:, kt, ct * 128:(ct + 1) * 128], nT[:, kt, c0:c0 + n], start=(kt == 0), stop=(kt == 7)),
                             reads=["wch", "uT"], writes=[pk])
                    P.op("act", lambda e, pb=pb, ct=ct, c0=c0, n=n: e.activation(s5in[:, ct, c0:c0 + n], pb[:, 0:n], AF.Copy), reads=[pk], writes=["s5in"])
            if S5STOP <= 1:
                return
            lam = sbx("lam", [128, 3, 16])
            prm = sbx("prm", [128, 12, 16])
            P.dma("sp", lam[:, 0, :], dr["s5_lam_re"].rearrange("(g2 gp) n -> (gp n) g2", gp=2), writes=["lam"], allow_slow_non_contiguous=True)
            P.dma("sp", lam[:, 1, :], dr["s5_lam_im"].rearrange("(g2 gp) n -> (gp n) g2", gp=2), writes=["lam"], allow_slow_non_contiguous=True)
            ldv = dr["s5_log_dt"].rearrange("(g2 gp) o -> gp (g2 o)", gp=2)
            P.dma("sp", lam[0:64, 2, :], ldv[0:1, :].to_broadcast([64, 16]), writes=["lam"], allow_slow_non_contiguous=True)
            P.dma("sp", lam[64:128, 2, :], ldv[1:2, :].to_broadcast([64, 16]), writes=["lam"], allow_slow_non_contiguous=True)
            pr = lambda i: prm[:, i, :]
            def dv(fn, reads=("prm", "lam")):
                P.op("dve", fn, reads=list(reads), writes=["prm"])
            def ac(fn):
                P.op("act", fn, reads=["prm", "lam"], writes=["prm"])
            ac(lambda e: e.activation(pr(0), lam[:, 2, :], AF.Exp))
            dv(lambda e: e.tensor_tensor(pr(1), lam[:, 0, :], pr(0), ALU.mult))
            dv(lambda e: e.tensor_tensor(pr(2), lam[:, 1, :], pr(0), ALU.mult))
            ac(lambda e: e.activation(pr(3), pr(1), AF.Exp))
            def sin_of(dst, shift):
                dv(lambda e: e.tensor_scalar(pr(4), pr(2), shift, 1.0 / TWO_PI, ALU.add, ALU.mult))
                dv(lambda e: e.tensor_scalar(pr(4), pr(4), 12582912.0, 12582912.0, ALU.add, ALU.subtract))
                dv(lambda e: e.scalar_tensor_tensor(pr(4), pr(4), -TWO_PI, pr(2), ALU.mult, ALU.add))
                dv(lambda e: e.tensor_scalar(pr(4), pr(4), shift, 3.1415925, ALU.add, ALU.min))
                dv(lambda e: e.tensor_scalar_max(pr(4), pr(4), -3.1415925))
                ac(lambda e: e.activation(pr(dst), pr(4), AF.Sin))
            sin_of(5, 0.0)
            sin_of(6, 1.5707963267948966)
            if S5STOP <= 2:
                return
            LP = sbx("LP", [128, 10, 3, 16])
            P.op("dve", lambda e: e.tensor_tensor(LP[:, 0, 0, :], pr(3), pr(6), ALU.mult), reads=["prm"], writes=["LP"])
            P.op("dve", lambda e: e.tensor_tensor(LP[:, 0, 1, :], pr(3), pr(5), ALU.mult), reads=["prm"], writes=["LP"])
            for k in range(9):
                P.op("dve", lambda e, k=k: e.tensor_tensor(pr(4), LP[:, k, 0, :], LP[:, k, 0, :], ALU.mult), reads=["LP"], writes=["prm"])
                P.op("dve", lambda e, k=k: e.tensor_tensor(pr(11), LP[:, k, 1, :], LP[:, k, 1, :], ALU.mult), reads=["LP"], writes=["prm"])
                P.op("dve", lambda e, k=k: e.tensor_tensor(LP[:, k + 1, 0, :], pr(4), pr(11), ALU.subtract), reads=["prm"], writes=["LP"])
                P.op("dve", lambda e, k=k: e.scalar_tensor_tensor(LP[:, k + 1, 1, :], LP[:, k, 0, :], 2.0, LP[:, k, 1, :], ALU.mult, ALU.mult), reads=["LP"], writes=["LP"])
            P.op("dve", lambda e: e.tensor_scalar(LP[:, :, 2, :], LP[:, :, 1, :], -1.0, None, ALU.mult), reads=["LP"], writes=["LP"])
            dv(lambda e: e.tensor_tensor(pr(7), lam[:, 0, :], lam[:, 0, :], ALU.mult))
            dv(lambda e: e.tensor_tensor(pr(4), lam[:, 1, :], lam[:, 1, :], ALU.mult))
            dv(lambda e: e.tensor_tensor(pr(7), pr(7), pr(4), ALU.add))
            dv(lambda e: e.reciprocal(pr(7), pr(7)))
            P.op("dve", lambda e: e.tensor_scalar(pr(8), LP[:, 0, 0, :], -1.0, None, ALU.add), reads=["LP"], writes=["prm"])
            dv(lambda e: e.tensor_tensor(pr(9), pr(8), lam[:, 0, :], ALU.mult))
            P.op("dve", lambda e: e.tensor_tensor(pr(4), LP[:, 0, 1, :], lam[:, 1, :], ALU.mult), reads=["LP", "lam"], writes=["prm"])
            dv(lambda e: e.tensor_tensor(pr(9), pr(9), pr(4), ALU.add))
            dv(lambda e: e.tensor_tensor(pr(9), pr(9), pr(7), ALU.mult))
            P.op("dve", lambda e: e.tensor_tensor(pr(10), LP[:, 0, 1, :], lam[:, 0, :], ALU.mult), reads=["LP", "lam"], writes=["prm"])
            dv(lambda e: e.tensor_tensor(pr(4), pr(8), lam[:, 1, :], ALU.mult))
            dv(lambda e: e.tensor_tensor(pr(10), pr(10), pr(4), ALU.subtract))
            dv(lambda e: e.tensor_tensor(pr(10), pr(10), pr(7), ALU.mult))
            if S5STOP <= 3:
                return
            maskcol = sbx("maskcol", [128, 8])
            P.op("pool", lambda e: e.affine_select(maskcol[:], onesF[:, 0:8], [[-16, 8]], ALU.is_ge, 0.0, base=0, channel_multiplier=1), reads=["onesF"], writes=["maskcol"])
            P.op("pool", lambda e: e.affine_select(maskcol[:], maskcol[:], [[16, 8]], ALU.is_ge, 0.0, base=15, channel_multiplier=-1), reads=["maskcol"], writes=["maskcol"])
            bst = sbx("bst", [64, 2, 512])
            P.dma("sp", bst[:, 0, :].rearrange("n (g c) -> n g c", g=32), dr["s5_b_re"].rearrange("g n c -> n g c"), writes=["bst"])
            P.dma("sp", bst[:, 1, :].rearrange("n (g c) -> n g c", g=32), dr["s5_b_im"].rearrange("g n c -> n g c"), writes=["bst"])
            Wb = sbx("Wb", [128, 2, 32, 64], BF16)
            bT = sbx("bT", [128, 64])
            for ri in range(2):
                for ct in range(4):
                    P.op("pe", lambda e, ri=ri, ct=ct: e.transpose(ps[2][:, 0:64], bst[:, ri, ct * 128:(ct + 1) * 128], identF[0:64, 0:64]), reads=["bst", "identF"], writes=["ps2"])
                    P.op("act", lambda e: e.activation(bT[:], ps[2][:, 0:64], AF.Copy), reads=["ps2"], writes=["bT"])
                    for g8 in range(8):
                        g = ct * 8 + g8
                        P.op("dve", lambda e, ri=ri, g=g, g8=g8: e.tensor_scalar(Wb[:, ri, g, :], bT[:], maskcol[:, g8:g8 + 1], None, ALU.mult), reads=["bT", "maskcol"], writes=["Wb"])
            if S5STOP <= 4:
                return
            cst = sbx("cst", [128, 4, 128])
            Cw = sbx("Cw", [128, 2, 32, 128], BF16)
            cTt = sbx("cTt", [128, 128])
            P.op("pool", lambda e: e.memset(Cw[:], 0.0), writes=["Cw"])
            for ri in range(2):
                src = dr["s5_c_re" if ri == 0 else "s5_c_im"].rearrange("(ct g8) c n -> (g8 c) ct n", g8=8)
                P.dma("sp", cst[:, :, 0:64], src, writes=["cst"])
                P.dma("sp", cst[:, :, 64:128], src, writes=["cst"])
                for ct in range(4):
                    P.op("pe", lambda e, ct=ct: e.transpose(ps[3][:, 0:128], cst[:, ct, :], identF[:, :]), reads=["cst", "identF"], writes=["ps3"])
                    P.op("act", lambda e: e.activation(cTt[:], ps[3][:, 0:128], AF.Copy), reads=["ps3"], writes=["cTt"])
                    for g8 in range(8):
                        g = ct * 8 + g8
                        gp, g2 = g % 2, g // 2
                        sgn = 1.0 if ri == 0 else -1.0
                        P.op("dve", lambda e, ri=ri, gp=gp, g2=g2, g8=g8, sgn=sgn: e.tensor_scalar(Cw[gp * 64:(gp + 1) * 64, ri, 2 * g2 + gp, g8 * 16:(g8 + 1) * 16],
                                                                                                   cTt[gp * 64:(gp + 1) * 64, g8 * 16:(g8 + 1) * 16], sgn, None, ALU.mult),
                             reads=["cTt"], writes=["Cw"])
            if S5STOP <= 5:
                return
            TB = 128
            X = [sbx("X%d" % i, [128, 2, 16, TB]) for i in range(2)]
            Xb = sbx("Xb", [128, 2, 16, TB], BF16)
            tmpk = sbx("tmpk", [128, TB])
            yv = sbx("yv", [128, TB])
            gw = sbx("gw", [128, TB])
            gs = sbx("gs", [128, TB])

            def bproj(c0, n, dst):
                for g2 in range(16):
                    for gp in range(2):
                        g = 2 * g2 + gp
                        ct = g // 8
                        for ri in range(2):
                            P.op("pe", lambda e, gp=gp, g=g, ct=ct, ri=ri: e.matmul(ps[ri][gp * 64:(gp + 1) * 64, 0:n], Wb[:, ri, g, :], s5in[:, ct, c0:c0 + n], start=True, stop=True),
                                 reads=["Wb", "s5in"], writes=[PSK[ri]])
                    P.op("dve", lambda e, g2=g2: e.tensor_scalar(tmpk[:, 0:n], ps[1][:, 0:n], prm[:, 10, g2:g2 + 1], None, ALU.mult), reads=["ps1", "prm"], writes=["tmpk"])
                    P.op("dve", lambda e, g2=g2: e.scalar_tensor_tensor(dst[:, 0, g2, 0:n], ps[0][:, 0:n], prm[:, 9, g2:g2 + 1], tmpk[:, 0:n], ALU.mult, ALU.subtract),
                         reads=["ps0", "prm", "tmpk"], writes=["XA"])
                    P.op("dve", lambda e, g2=g2: e.tensor_scalar(tmpk[:, 0:n], ps[0][:, 0:n], prm[:, 10, g2:g2 + 1], None, ALU.mult), reads=["ps0", "prm"], writes=["tmpk"])
                    P.op("dve", lambda e, g2=g2: e.scalar_tensor_tensor(dst[:, 1, g2, 0:n], ps[1][:, 0:n], prm[:, 9, g2:g2 + 1], tmpk[:, 0:n], ALU.mult, ALU.add),
                         reads=["ps1", "prm", "tmpk"], writes=["XA"])

            def cproj(c0, n):
                for ct in range(4):
                    pb = ps[2 + (ct % 2)]
                    pk = PSK[2 + (ct % 2)]
                    first = True
                    for g8 in range(8):
                        g = ct * 8 + g8
                        gp, g2 = g % 2, g // 2
                        for ri in range(2):
                            last = (g8 == 7 and ri == 1)
                            P.op("pe", lambda e, pb=pb, gp=gp, g2=g2, ri=ri, first=first, last=last: e.matmul(pb[:, 0:n], Cw[:, ri, 2 * g2 + gp, :], Xb[:, ri, g2, 0:n], start=first, stop=last),
                                 reads=["Cw", "Xb"], writes=[pk])
                            first = False
                    P.op("dve", lambda e, pb=pb, ct=ct: e.scalar_tensor_tensor(yv[:, 0:n], s5in[:, ct, c0:c0 + n], CP[:, 104 + ct:105 + ct], pb[:, 0:n], ALU.mult, ALU.add),
                         reads=["s5in", "CP", pk], writes=["yv"])
                    P.op("act", lambda e: e.activation(gw[:, 0:n], yv[:, 0:n], AF.Square), reads=["yv"], writes=["gw"])
                    P.op("dve", lambda e: e.tensor_scalar(gw[:, 0:n], gw[:, 0:n], 0.044715, 1.0, ALU.mult, ALU.add), reads=["gw"], writes=["gw"])
                    P.op("dve", lambda e: e.tensor_tensor(gw[:, 0:n], gw[:, 0:n], yv[:, 0:n], ALU.mult), reads=["gw", "yv"], writes=["gw"])
                    P.op("act", lambda e: e.activation(gs[:, 0:n], gw[:, 0:n], AF.Sigmoid, scale=1.5957691216057308), reads=["gw"], writes=["gs"])
                    P.op("dve", lambda e, ct=ct: e.tensor_tensor(zT[:, ct, c0:c0 + n], yv[:, 0:n], gs[:, 0:n], ALU.mult), reads=["yv", "gs"], writes=["zT"])

            if S5STOP <= 6:
                return
            for blk in range(1024 // TB):
                c0 = blk * TB
                A, B_ = X[0], X[1]
                bproj(c0, TB, A)
                if S5STOP <= 7:
                    return
                P.op("dve", lambda e: e.tensor_tensor(prm[:, 4, :], LP[:, 0, 0, :], s5car[:, 0, :], ALU.mult), reads=["LP", "s5car"], writes=["prm"])
                P.op("dve", lambda e: e.tensor_tensor(prm[:, 11, :], LP[:, 0, 1, :], s5car[:, 1, :], ALU.mult), reads=["LP", "s5car"], writes=["prm"])
                P.op("dve", lambda e: e.tensor_tensor(prm[:, 4, :], prm[:, 4, :], prm[:, 11, :], ALU.subtract), reads=["prm"], writes=["prm"])
                P.op("dve", lambda e, A=A: e.tensor_tensor(A[:, 0, :, 0], A[:, 0, :, 0], prm[:, 4, :], ALU.add), reads=["XA", "prm"], writes=["XA"])
                P.op("dve", lambda e: e.tensor_tensor(prm[:, 4, :], LP[:, 0, 0, :], s5car[:, 1, :], ALU.mult), reads=["LP", "s5car"], writes=["prm"])
                P.op("dve", lambda e: e.tensor_tensor(prm[:, 11, :], LP[:, 0, 1, :], s5car[:, 0, :], ALU.mult), reads=["LP", "s5car"], writes=["prm"])
                P.op("dve", lambda e: e.tensor_tensor(prm[:, 4, :], prm[:, 4, :], prm[:, 11, :], ALU.add), reads=["prm"], writes=["prm"])
                P.op("dve", lambda e, A=A: e.tensor_tensor(A[:, 1, :, 0], A[:, 1, :, 0], prm[:, 4, :], ALU.add), reads=["XA", "prm"], writes=["XA"])
                cur_, nxt_ = A, B_
                ck, nk = "XA", "XB"
                k = 0
                while (1 << k) < TB:
                    s = 1 << k
                    P.op("act", lambda e, cur_=cur_, nxt_=nxt_, s=s: e.activation(nxt_[:, :, :, 0:s], cur_[:, :, :, 0:s], AF.Copy), reads=[ck], writes=[nk])
                    for g2 in range(16):
                        lr = LP[:, k, 0, g2:g2 + 1]
                        li = LP[:, k, 1, g2:g2 + 1]
                        nli = LP[:, k, 2, g2:g2 + 1]
                        P.op("dve", lambda e, cur_=cur_, nxt_=nxt_, s=s, g2=g2, lr=lr: e.scalar_tensor_tensor(nxt_[:, 0, g2, s:TB], cur_[:, 0, g2, 0:TB - s], lr, cur_[:, 0, g2, s:TB], ALU.mult, ALU.add),
                             reads=[ck, "LP"], writes=[nk])
                        P.op("dve", lambda e, cur_=cur_, nxt_=nxt_, s=s, g2=g2, nli=nli: e.scalar_tensor_tensor(nxt_[:, 0, g2, s:TB], cur_[:, 1, g2, 0:TB - s], nli, nxt_[:, 0, g2, s:TB], ALU.mult, ALU.add),
                             reads=[ck, nk, "LP"], writes=[nk])
                        P.op("dve", lambda e, cur_=cur_, nxt_=nxt_, s=s, g2=g2, lr=lr: e.scalar_tensor_tensor(nxt_[:, 1, g2, s:TB], cur_[:, 1, g2, 0:TB - s], lr, cur_[:, 1, g2, s:TB], ALU.mult, ALU.add),
                             reads=[ck, "LP"], writes=[nk])
                        P.op("dve", lambda e, cur_=cur_, nxt_=nxt_, s=s, g2=g2, li=li: e.scalar_tensor_tensor(nxt_[:, 1, g2, s:TB], cur_[:, 0, g2, 0:TB - s], li, nxt_[:, 1, g2, s:TB], ALU.mult, ALU.add),
                             reads=[ck, nk, "LP"], writes=[nk])
                    cur_, nxt_ = nxt_, cur_
                    ck, nk = nk, ck
                    k += 1
                if S5STOP <= 8:
                    return
                P.op("dve", lambda e, cur_=cur_: e.tensor_copy(s5car[:, :, :], cur_[:, :, :, TB - 1]), reads=[ck], writes=["s5car"])
                P.op("act", lambda e, cur_=cur_: e.activation(Xb[:], cur_[:], AF.Copy), reads=[ck], writes=["Xb"])
                if S5STOP <= 9:
                    return
                cproj(c0, TB)
            if S5STOP <= 10:
                return
            if sbk == 1:
                for ri, nm in ((0, "s5reP"), (1, "s5imP")):
                    P.op("pe", lambda e, ri=ri: e.transpose(ps[4][0:16, 0:128], s5car[:, ri, :], identF[:, :]), reads=["s5car", "identF"], writes=["ps4"])
                    P.op("act", lambda e: e.activation(tmpk[0:16, 0:128], ps[4][0:16, 0:128], AF.Copy), reads=["ps4"], writes=["tmpk"])
                    P.dma("sp", dr[nm], tmpk[0:16, 0:128], reads=["tmpk"])
                if S5STOP <= 11:
                    return
                h0in = sbx("h0in", [16, 2048])
                h0 = sbx("h0", [128, 2, 16, 16])
                for ri in range(2):
                    P.dma("sp", h0in[:, :], dr["s5re0" if ri == 0 else "s5im0"], writes=["h0in"])
                    for g2 in range(16):
                        P.op("pe", lambda e, ri=ri, g2=g2: e.transpose(ps[5][:, g2 * 16:(g2 + 1) * 16], h0in[:, g2 * 128:(g2 + 1) * 128], identF[0:16, 0:16]), reads=["h0in", "identF"], writes=["ps5"])
                    P.op("act", lambda e, ri=ri: e.activation(h0[:, ri, :, :].rearrange("p a b -> p (a b)"), ps[5][:, 0:256], AF.Copy), reads=["ps5"], writes=["h0"])
                A = X[0]
                bproj(1024, 16, A)
                x1 = sbx("x1", [128, 2, 16, 16])
                t16 = sbx("t16", [128, 16, 16])
                lrb = LP[:, 0, 0, :].unsqueeze(2).to_broadcast([128, 16, 16])
                lib = LP[:, 0, 1, :].unsqueeze(2).to_broadcast([128, 16, 16])
                P.op("dve", lambda e: e.tensor_tensor(x1[:, 0, :, :], h0[:, 0, :, :], lrb, ALU.mult), reads=["h0", "LP"], writes=["x1"])
                P.op("dve", lambda e: e.tensor_tensor(t16[:], h0[:, 1, :, :], lib, ALU.mult), reads=["h0", "LP"], writes=["t16"])
                P.op("dve", lambda e: e.tensor_tensor(x1[:, 0, :, :], x1[:, 0, :, :], t16[:], ALU.subtract), reads=["x1", "t16"], writes=["x1"])
                P.op("dve", lambda e: e.tensor_tensor(x1[:, 0, :, :], x1[:, 0, :, :], A[:, 0, :, 0:16], ALU.add), reads=["x1", "XA"], writes=["x1"])
                P.op("dve", lambda e: e.tensor_tensor(x1[:, 1, :, :], h0[:, 1, :, :], lrb, ALU.mult), reads=["h0", "LP"], writes=["x1"])
                P.op("dve", lambda e: e.tensor_tensor(t16[:], h0[:, 0, :, :], lib, ALU.mult), reads=["h0", "LP"], writes=["t16"])
                P.op("dve", lambda e: e.tensor_tensor(x1[:, 1, :, :], x1[:, 1, :, :], t16[:], ALU.add), reads=["x1", "t16"], writes=["x1"])
                P.op("dve", lambda e: e.tensor_tensor(x1[:, 1, :, :], x1[:, 1, :, :], A[:, 1, :, 0:16], ALU.add), reads=["x1", "XA"], writes=["x1"])
                P.op("act", lambda e: e.activation(Xb[:, :, :, 0:16], x1[:], AF.Copy), reads=["x1"], writes=["Xb"])
                cproj(1024, 16)
                for ri, nm in ((0, "s5reS"), (1, "s5imS")):
                    for half in range(2):
                        for q in range(8):
                            g2 = half * 8 + q
                            P.op("pe", lambda e, ri=ri, g2=g2, q=q, half=half: e.transpose(ps[6 + half][0:16, q * 128:(q + 1) * 128] if q < 4 else ps[6 + half][0:16, q * 128 - 512:(q + 1) * 128 - 512],
                                                                                          x1[:, ri, g2, :], identF[:, :]), reads=["x1", "identF"], writes=[PSK[6 + half]])
                            if q == 3 or q == 7:
                                lo = half * 1024 + (0 if q == 3 else 512)
                                P.op("act", lambda e, ri=ri, half=half, lo=lo: e.activation(h0in[:, lo:lo + 512], ps[6 + half][0:16, :], AF.Copy), reads=[PSK[6 + half]], writes=["h0in"])
                    P.dma("sp", dr[nm], h0in[:, :], reads=["h0in"])

        def s5_setup(sbs):
            lam = sbs("lam", [128, 3, 16])
            prm = sbs("prm", [128, 12, 16])
            LP = sbs("LP", [128, 10, 3, 16])
            LI = sbs("LI", [128, 2, 16])
            Mw = sbs("Mw", [128, 32, 128], BF16)
            Zw = sbs("Zw", [128, 2, 16, 128], BF16)
            YwZ = sbs("YwZ", [128, 2, 32, 128], BF16)
            pr = lambda i: prm[:, i, :]

            def dv(fn, reads=("prm", "lam")):
                P.op("dve", fn, reads=list(reads), writes=["prm"])

            def ac(fn):
                P.op("act", fn, reads=["prm", "lam"], writes=["prm"])
            P.dma("sp", lam[:, 0, :], dr["s5_lam_re"].rearrange("(g2 gp) n -> (gp n) g2", gp=2), writes=["lam"], allow_slow_non_contiguous=True)
            P.dma("sp", lam[:, 1, :], dr["s5_lam_im"].rearrange("(g2 gp) n -> (gp n) g2", gp=2), writes=["lam"], allow_slow_non_contiguous=True)
            ldv = dr["s5_log_dt"].rearrange("(g2 gp) o -> gp (g2 o)", gp=2)
            P.dma("sp", lam[0:64, 2, :], ldv[0:1, :].to_broadcast([64, 16]), writes=["lam"], allow_slow_non_contiguous=True)
            P.dma("sp", lam[64:128, 2, :], ldv[1:2, :].to_broadcast([64, 16]), writes=["lam"], allow_slow_non_contiguous=True)
            ac(lambda e: e.activation(pr(0), lam[:, 2, :], AF.Exp))
            dv(lambda e: e.tensor_tensor(pr(1), lam[:, 0, :], pr(0), ALU.mult))
            dv(lambda e: e.tensor_tensor(pr(2), lam[:, 1, :], pr(0), ALU.mult))
            ac(lambda e: e.activation(pr(3), pr(1), AF.Exp))

            def sin_of(dst, shift):
                dv(lambda e: e.tensor_scalar(pr(4), pr(2), shift, 1.0 / TWO_PI, ALU.add, ALU.mult))
                dv(lambda e: e.tensor_scalar(pr(4), pr(4), 12582912.0, 12582912.0, ALU.add, ALU.subtract))
                dv(lambda e: e.scalar_tensor_tensor(pr(4), pr(4), -TWO_PI, pr(2), ALU.mult, ALU.add))
                dv(lambda e: e.tensor_scalar(pr(4), pr(4), shift, 3.1415925, ALU.add, ALU.min))
                dv(lambda e: e.tensor_scalar_max(pr(4), pr(4), -3.1415925))
                ac(lambda e: e.activation(pr(dst), pr(4), AF.Sin))
            sin_of(5, 0.0)
            sin_of(6, 1.5707963267948966)
            P.op("dve", lambda e: e.tensor_tensor(LP[:, 0, 0, :], pr(3), pr(6), ALU.mult), reads=["prm"], writes=["LP"])
            P.op("dve", lambda e: e.tensor_tensor(LP[:, 0, 1, :], pr(3), pr(5), ALU.mult), reads=["prm"], writes=["LP"])
            for k in range(9):
                P.op("dve", lambda e, k=k: e.tensor_tensor(pr(4), LP[:, k, 0, :], LP[:, k, 0, :], ALU.mult), reads=["LP"], writes=["prm"])
                P.op("dve", lambda e, k=k: e.tensor_tensor(pr(11), LP[:, k, 1, :], LP[:, k, 1, :], ALU.mult), reads=["LP"], writes=["prm"])
                P.op("dve", lambda e, k=k: e.tensor_tensor(LP[:, k + 1, 0, :], pr(4), pr(11), ALU.subtract), reads=["prm"], writes=["LP"])
                P.op("dve", lambda e, k=k: e.scalar_tensor_tensor(LP[:, k + 1, 1, :], LP[:, k, 0, :], 2.0, LP[:, k, 1, :], ALU.mult, ALU.mult), reads=["LP"], writes=["LP"])
            P.op("dve", lambda e: e.tensor_scalar(LP[:, :, 2, :], LP[:, :, 1, :], -1.0, None, ALU.mult), reads=["LP"], writes=["LP"])
            dv(lambda e: e.tensor_tensor(pr(7), lam[:, 0, :], lam[:, 0, :], ALU.mult))
            dv(lambda e: e.tensor_tensor(pr(4), lam[:, 1, :], lam[:, 1, :], ALU.mult))
            dv(lambda e: e.tensor_tensor(pr(7), pr(7), pr(4), ALU.add))
            dv(lambda e: e.reciprocal(pr(7), pr(7)))
            P.op("dve", lambda e: e.tensor_scalar(pr(8), LP[:, 0, 0, :], -1.0, None, ALU.add), reads=["LP"], writes=["prm"])
            dv(lambda e: e.tensor_tensor(pr(9), pr(8), lam[:, 0, :], ALU.mult))
            P.op("dve", lambda e: e.tensor_tensor(pr(4), LP[:, 0, 1, :], lam[:, 1, :], ALU.mult), reads=["LP", "lam"], writes=["prm"])
            dv(lambda e: e.tensor_tensor(pr(9), pr(9), pr(4), ALU.add))
            dv(lambda e: e.tensor_tensor(pr(9), pr(9), pr(7), ALU.mult))
            P.op("dve", lambda e: e.tensor_tensor(pr(10), LP[:, 0, 1, :], lam[:, 0, :], ALU.mult), reads=["LP", "lam"], writes=["prm"])
            dv(lambda e: e.tensor_tensor(pr(4), pr(8), lam[:, 1, :], ALU.mult))
            dv(lambda e: e.tensor_tensor(pr(10), pr(10), pr(4), ALU.subtract))
            dv(lambda e: e.tensor_tensor(pr(10), pr(10), pr(7), ALU.mult))
            P.op("dve", lambda e: e.tensor_tensor(pr(4), pr(3), pr(3), ALU.mult), reads=["prm"], writes=["prm"])
            P.op("dve", lambda e: e.reciprocal(pr(4), pr(4)), reads=["prm"], writes=["prm"])
            P.op("dve", lambda e: e.tensor_tensor(LI[:, 0, :], LP[:, 0, 0, :], pr(4), ALU.mult), reads=["LP", "prm"], writes=["LI"])
            P.op("dve", lambda e: e.tensor_tensor(LI[:, 1, :], LP[:, 0, 2, :], pr(4), ALU.mult), reads=["LP", "prm"], writes=["LI"])
            PW = sbs("PW", [128, 2, 16, 8])
            PN = sbs("PN", [128, 2, 16, 8])
            tq = sbs("tq", [128, 2, 16])
            for (T_, b0r, b0i, key) in ((PW, LP[:, 0, 0, :], LP[:, 0, 1, :], "PW"), (PN, LI[:, 0, :], LI[:, 1, :], "PN")):
                P.op("dve", lambda e, T_=T_, b0r=b0r: e.tensor_copy(T_[:, 0, :, 0], b0r), reads=["LP", "LI"], writes=[key])
                P.op("dve", lambda e, T_=T_, b0i=b0i: e.tensor_copy(T_[:, 1, :, 0], b0i), reads=["LP", "LI"], writes=[key])
                for j in range(7):
                    P.op("dve", lambda e, T_=T_, j=j, b0r=b0r: e.tensor_tensor(tq[:, 0, :], T_[:, 0, :, j], b0r, ALU.mult), reads=[key, "LP", "LI"], writes=["tq"])
                    P.op("dve", lambda e, T_=T_, j=j, b0i=b0i: e.tensor_tensor(tq[:, 1, :], T_[:, 1, :, j], b0i, ALU.mult), reads=[key, "LP", "LI"], writes=["tq"])
                    P.op("dve", lambda e, T_=T_, j=j: e.tensor_tensor(T_[:, 0, :, j + 1], tq[:, 0, :], tq[:, 1, :], ALU.subtract), reads=["tq"], writes=[key])
                    P.op("dve", lambda e, T_=T_, j=j, b0i=b0i: e.tensor_tensor(tq[:, 0, :], T_[:, 0, :, j], b0i, ALU.mult), reads=[key, "LP", "LI"], writes=["tq"])
                    P.op("dve", lambda e, T_=T_, j=j, b0r=b0r: e.tensor_tensor(tq[:, 1, :], T_[:, 1, :, j], b0r, ALU.mult), reads=[key, "LP", "LI"], writes=["tq"])
                    P.op("dve", lambda e, T_=T_, j=j: e.tensor_tensor(T_[:, 1, :, j + 1], tq[:, 0, :], tq[:, 1, :], ALU.add), reads=["tq"], writes=[key])
            bA = sbs("bA", [128, 2, 16, 16])
            CA = sbs("CA", [128, 2, 16, 16])
            for ri, nm in ((0, "s5_b_re"), (1, "s5_b_im")):
                v = dr[nm].rearrange("(g2 gp) n c -> gp n g2 c", gp=2)
                for gp in range(2):
                    P.dma("sp", bA[gp * 64:(gp + 1) * 64, ri, :, :], v[gp], writes=["bA"])
            cst = sbs("cst", [128, 4, 128])
            for ri in range(2):
                src = dr["s5_c_re" if ri == 0 else "s5_c_im"].rearrange("(ct g8) c n -> (g8 c) ct n", g8=8)
                P.dma("sp", cst[:, :, 0:64], src, writes=["cst"])
                P.dma("sp", cst[:, :, 64:128], src, writes=["cst"])
                for ct in range(4):
                    P.op("pe", lambda e, ct=ct: e.transpose(ps[7][:, 0:128], cst[:, ct, :], identF[:, :]), reads=["cst", "identF"], writes=["ps7"])
                    for gp in range(2):
                        srcv = ps[7][gp * 64:(gp + 1) * 64, 0:128].rearrange("p (j q c) -> p j q c", j=4, q=2)[:, :, gp, :]
                        P.op("dve", lambda e, ri=ri, ct=ct, gp=gp, srcv=srcv: e.tensor_copy(CA[gp * 64:(gp + 1) * 64, ri, ct * 4:(ct + 1) * 4, :], srcv), reads=["ps7"], writes=["CA"])
            BB = sbs("BB", [128, 2, 16, 16])
            t16a = sbs("t16a", [128, 16, 16])
            kb = lambda i: prm[:, i, :].unsqueeze(2).to_broadcast([128, 16, 16])
            P.op("dve", lambda e: e.tensor_tensor(BB[:, 0, :, :], bA[:, 0, :, :], kb(9), ALU.mult), reads=["bA", "prm"], writes=["BB"])
            P.op("dve", lambda e: e.tensor_tensor(t16a[:], bA[:, 1, :, :], kb(10), ALU.mult), reads=["bA", "prm"], writes=["t16a"])
            P.op("dve", lambda e: e.tensor_tensor(BB[:, 0, :, :], BB[:, 0, :, :], t16a[:], ALU.subtract), reads=["BB", "t16a"], writes=["BB"])
            P.op("dve", lambda e: e.tensor_tensor(BB[:, 1, :, :], bA[:, 1, :, :], kb(9), ALU.mult), reads=["bA", "prm"], writes=["BB"])
            P.op("dve", lambda e: e.tensor_tensor(t16a[:], bA[:, 0, :, :], kb(10), ALU.mult), reads=["bA", "prm"], writes=["t16a"])
            P.op("dve", lambda e: e.tensor_tensor(BB[:, 1, :, :], BB[:, 1, :, :], t16a[:], ALU.add), reads=["BB", "t16a"], writes=["BB"])
            TA = sbs("TA", [128, 16, 8, 16])
            TBb = sbs("TBb", [128, 16, 8, 16])
            T1 = sbs("T1", [128, 16, 8, 16])
            T2 = sbs("T2", [128, 16, 8, 16])
            Wall = sbs("Wall", [128, 2, 16, 128], BF16)
            Pz = sbs("Pz", [128, 2, 32, 128], BF16)
            SH = [128, 16, 8, 16]

            def outer_cplx(X, Pw, xk, pk_):
                xr = X[:, 0, :, :].unsqueeze(2).to_broadcast(SH)
                xi = X[:, 1, :, :].unsqueeze(2).to_broadcast(SH)
                wr = Pw[:, 0, :, :].unsqueeze(3).to_broadcast(SH)
                wi = Pw[:, 1, :, :].unsqueeze(3).to_broadcast(SH)
                P.op("dve", lambda e: e.tensor_tensor(TA[:], xr, wr, ALU.mult), reads=[xk, pk_], writes=["TA"])
                P.op("dve", lambda e: e.tensor_tensor(T1[:], xi, wi, ALU.mult), reads=[xk, pk_], writes=["T1"])
                P.op("dve", lambda e: e.tensor_tensor(TA[:], TA[:], T1[:], ALU.subtract), reads=["TA", "T1"], writes=["TA"])
                P.op("dve", lambda e: e.tensor_tensor(TBb[:], xr, wi, ALU.mult), reads=[xk, pk_], writes=["TBb"])
                P.op("dve", lambda e: e.tensor_tensor(T2[:], xi, wr, ALU.mult), reads=[xk, pk_], writes=["T2"])
                P.op("dve", lambda e: e.tensor_tensor(TBb[:], TBb[:], T2[:], ALU.add), reads=["TBb", "T2"], writes=["TBb"])

            f3 = lambda t: t[:].rearrange("p g a b -> p g (a b)")
            outer_cplx(CA, PW, "CA", "PW")
            P.op("act", lambda e: e.activation(Wall[:, 0, :, :], f3(TA), AF.Copy), reads=["TA"], writes=["Wall"])
            P.op("act", lambda e: e.activation(Wall[:, 1, :, :], f3(TBb), AF.Copy, scale=-1.0), reads=["TBb"], writes=["Wall"])
            P.op("pool", lambda e: e.memset(YwZ[:], 0.0), writes=["YwZ"])
            P.op("pool", lambda e: e.memset(Pz[:], 0.0), writes=["Pz"])
            for gp in range(2):
                H_ = slice(gp * 64, (gp + 1) * 64)
                for ri in range(2):
                    dst = YwZ[H_, ri, :, :].rearrange("p (g2 q) m -> p g2 q m", q=2)[:, :, gp, :]
                    P.op("dve", lambda e, dst=dst, ri=ri, H_=H_: e.tensor_copy(dst, Wall[H_, ri, :, :]), reads=["Wall"], writes=["YwZ"])
            outer_cplx(BB, PN, "BB", "PN")
            for gp in range(2):
                H_ = slice(gp * 64, (gp + 1) * 64)
                for ri, T_, tk in ((0, TA, "TA"), (1, TBb, "TBb")):
                    dst = Pz[H_, ri, :, :].rearrange("p (g2 q) m -> p g2 q m", q=2)[:, :, gp, :]
                    P.op("act", lambda e, dst=dst, T_=T_, H_=H_: e.activation(dst, T_[H_].rearrange("p g a b -> p g (a b)"), AF.Copy), reads=[tk], writes=["Pz"])
            maskM = sbs("maskM", [128, 128])
            P.op("pool", lambda e: e.affine_select(maskM[:], onesF[:], [[16, 8], [0, 16]], ALU.is_ge, 0.0, base=15, channel_multiplier=-1), reads=["onesF"], writes=["maskM"])
            for g4 in range(8):
                pb = ps[4 + g4 % 2]
                pk = PSK[4 + g4 % 2]
                for q in range(4):
                    g = g4 * 4 + q
                    g2 = g // 2
                    P.op("pe", lambda e, pb=pb, q=q, g=g, g2=g2: e.matmul(pb[:, q * 128:(q + 1) * 128], Pz[:, 0, g, :], Wall[:, 0, g2, :], start=True, stop=False), reads=["Pz", "Wall"], writes=[pk])
                    P.op("pe", lambda e, pb=pb, q=q, g=g, g2=g2: e.matmul(pb[:, q * 128:(q + 1) * 128], Pz[:, 1, g, :], Wall[:, 1, g2, :], start=False, stop=True), reads=["Pz", "Wall"], writes=[pk])
                P.op("dve", lambda e, pb=pb, g4=g4: e.tensor_tensor(Mw[:, g4 * 4:(g4 + 1) * 4, :], pb[:, :].rearrange("p (q m) -> p q m", q=4),
                                                                   maskM[:].unsqueeze(1).to_broadcast([128, 4, 128]), ALU.mult), reads=[pk, "maskM"], writes=["Mw"])
            l8r = LP[:, 3, 0, :].unsqueeze(2).to_broadcast([128, 16, 128])
            l8i = LP[:, 3, 1, :].unsqueeze(2).to_broadcast([128, 16, 128])
            P.op("dve", lambda e: e.tensor_tensor(f3(T1), f3(TA), l8r, ALU.mult), reads=["TA", "LP"], writes=["T1"])
            P.op("dve", lambda e: e.tensor_tensor(f3(T2), f3(TBb), l8i, ALU.mult), reads=["TBb", "LP"], writes=["T2"])
            P.op("dve", lambda e: e.tensor_tensor(f3(T1), f3(T1), f3(T2), ALU.subtract), reads=["T1", "T2"], writes=["T1"])
            P.op("dve", lambda e: e.tensor_tensor(f3(T2), f3(TA), l8i, ALU.mult), reads=["TA", "LP"], writes=["T2"])
            P.op("dve", lambda e: e.tensor_tensor(f3(TA), f3(TBb), l8r, ALU.mult), reads=["TBb", "LP"], writes=["TA"])
            P.op("dve", lambda e: e.tensor_tensor(f3(T2), f3(T2), f3(TA), ALU.add), reads=["T2", "TA"], writes=["T2"])
            for ri, T_, tk in ((0, T1, "T1"), (1, T2, "T2")):
                for g24 in range(4):
                    pb = ps[6 + (g24 % 2)]
                    pk = PSK[6 + (g24 % 2)]
                    for q in range(4):
                        g2 = g24 * 4 + q
                        P.op("pe", lambda e, pb=pb, q=q, g2=g2, T_=T_: e.transpose(pb[:, q * 128:(q + 1) * 128], T_[:, g2, :, :].rearrange("p a b -> p (a b)"), identF[:, :]), reads=[tk, "identF"], writes=[pk])
                    P.op("act", lambda e, pb=pb, ri=ri, g24=g24: e.activation(Zw[:, ri, g24 * 4:(g24 + 1) * 4, :].rearrange("p a b -> p (a b)"), pb[:, :], AF.Copy), reads=[pk], writes=["Zw"])
            for nm_, t_, k_ in (("c_Mw", Mw, "Mw"), ("c_Zw", Zw, "Zw"), ("c_YwZ", YwZ, "YwZ")):
                P.dma("sp", cache[nm_], t_[:].rearrange("p a b -> p (a b)") if len(t_.shape) == 3 else t_[:].rearrange("p a b c -> p (a b c)"), reads=[k_], writes=[nm_])
            P.dma("sp", cache["c_prm"], prm[:].rearrange("p a b -> p (a b)"), reads=["prm"], writes=["c_prm"])
            P.dma("sp", cache["c_LP"], LP[:].rearrange("p a b c -> p (a b c)"), reads=["LP"], writes=["c_LP"])
            P.dma("sp", cache["c_LI"], LI[:].rearrange("p a b -> p (a b)"), reads=["LI"], writes=["c_LI"])

        def s5_phase2(sbk, zT):
            outer = ExitStack()
            with outer:
                sbo = mk_sb(outer)
                lam = sbo("lam", [128, 3, 16])
                prm = sbo("prm", [128, 12, 16])
                LP = sbo("LP", [128, 10, 3, 16])
                LI = sbo("LI", [128, 2, 16])
                Mw = sbo("Mw", [128, 32, 128], BF16)
                Zw = sbo("Zw", [128, 2, 16, 128], BF16)
                YwZ = sbo("YwZ", [128, 2, 32, 128], BF16)
                pr = lambda i: prm[:, i, :]

                def dv(fn, reads=("prm", "lam")):
                    P.op("dve", fn, reads=list(reads), writes=["prm"])

                def ac(fn):
                    P.op("act", fn, reads=["prm", "lam"], writes=["prm"])
                for nm_, t_, k_ in (("c_Mw", Mw, "Mw"), ("c_Zw", Zw, "Zw"), ("c_YwZ", YwZ, "YwZ")):
                    P.dma("sp", t_[:].rearrange("p a b -> p (a b)") if len(t_.shape) == 3 else t_[:].rearrange("p a b c -> p (a b c)"), cache[nm_], writes=[k_])
                P.dma("sp", prm[:].rearrange("p a b -> p (a b)"), cache["c_prm"], writes=["prm"])
                P.dma("sp", LP[:].rearrange("p a b c -> p (a b c)"), cache["c_LP"], writes=["LP"])
                P.dma("sp", LI[:].rearrange("p a b -> p (a b)"), cache["c_LI"], writes=["LI"])
                with ExitStack() as ssc:
                    sbt = mk_sb(ssc)
                    alloc_norm(sbt)
                    uTs = sbt("uTs", [128, 8, 8, 128], BF16)
                    uS = sbt("uS", [128, 8, 16], BF16)
                    s5in = sbt("s5in", [128, 4, NC], BF16)
                    wch = sbt("wch", [128, 8, 512], BF16)
                    TMU = sbt("TMU", [128, 8192], BF16)
                    TM = TMU[:, 0:4096].rearrange("p (g s c) -> p g s c", g=32, s=8)
                    U = TMU[:, 4096:8192].rearrange("p (g c) -> p g c", g=32)
                    TMY = uTs[:].rearrange("p a b c -> p (a b c)").bitcast(F32).rearrange("p (t c) -> p t c", t=8)
                    XA = sbt("XA", [128, 2, 16, 128])
                    XB = sbt("XB", [128, 2, 16, 128])
                    Hp = wch[:].rearrange("p a b -> p (a b)").rearrange("p (r g c) -> p r g c", r=2, g=16)
                    Ysb = [sbt("Ysb%d" % i, [128, 512]) for i in range(2)]
                    yv = sbt("yv0", [128, 512])
                    gw = sbt("gw", [128, 512])
                    gs = sbt("gs", [128, 512])
                    yvB = [yv, sbt("yv2", [128, 512])]
                    gwB = [gw, sbt("gw2", [128, 512])]
                    gsB = [gs, sbt("gs2", [128, 512])]

                    ust = TMU[:, 0:4096].rearrange("p (k t) -> p k t", k=8)
                    tmp2v = TMU[:, 4096:5120].bitcast(F32)
                    for (c0, n) in ranges_of(sbk):
                        normmod(sbk, 1, ust, "TM", rngs=[(c0, n)], obase=c0, tmp2=tmp2v, tmp2_key="U")
                        P.dma("sp", uT_scr[:, :, c0:c0 + n], ust[:, :, 0:n], reads=["TM"], writes=["uT_scr"])
                        if is_samp(c0):
                            P.op("pool", lambda e: e.tensor_copy(uS[:, :, :], ust[:, :, 0:16]), reads=["TM"], writes=["uTs"])
                        else:
                            dstv = uTs[:, :, :, c0 // 8:(c0 + n) // 8].rearrange("p k s c -> p k c s")
                            srcv = ust[:, :, 0:n].rearrange("p k (c s) -> p k c s", s=8)
                            P.op("dve", lambda e, dstv=dstv, srcv=srcv: e.tensor_copy(dstv[:, 0:4], srcv[:, 0:4]), reads=["TM"], writes=["uTs"])
                            P.op("act", lambda e, dstv=dstv, srcv=srcv: e.activation(dstv[:, 4:8], srcv[:, 4:8], AF.Copy), reads=["TM"], writes=["uTs"])
                    P.dma("pool", wch[:], dr["w_in"][:, 0:512].rearrange("(kt p) n -> p kt n", p=128), writes=["wch"])
                    rngs = ranges_of(sbk)
                    for sg in range(8):
                        pb = ps[2 + (sg % 2)]
                        pk = PSK[2 + (sg % 2)]
                        for kt in range(8):
                            P.op("pe", lambda e, pb=pb, sg=sg, kt=kt: e.matmul(pb[:, :], uTs[:, kt, sg, :], wch[:, kt, :], start=(kt == 0), stop=(kt == 7)), reads=["uTs", "wch"], writes=[pk])
                        if sg % 2 == 0:
                            P.op("act", lambda e, pb=pb, sg=sg: e.activation(TM[:, :, sg, :], pb[:, :].rearrange("p (g c) -> p g c", g=32), AF.Copy), reads=[pk], writes=["TM"])
                        else:
                            P.op("dve", lambda e, pb=pb, sg=sg: e.tensor_copy(TM[:, :, sg, :], pb[:, :].rearrange("p (g c) -> p g c", g=32)), reads=[pk], writes=["TM"])
                    for g4 in range(8):
                        pbk = 4 + (g4 % 2)
                        for q in range(4):
                            g = g4 * 4 + q
                            P.op("pe", lambda e, pbk=pbk, q=q, g=g: e.transpose(psb16[pbk][:, q * 128:(q + 1) * 128], TM[:, g, :, :].rearrange("p a b -> p (a b)"), identB[:, :]), reads=["TM", "identB"], writes=[PSK[pbk]])
                        if g4 % 2 == 0:
                            P.op("act", lambda e, pbk=pbk, g4=g4: e.activation(U[:, g4 * 4:(g4 + 1) * 4, :].rearrange("p a b -> p (a b)"), psb16[pbk][:, 0:512], AF.Copy), reads=[PSK[pbk]], writes=["U"])
                        else:
                            P.op("dve", lambda e, pbk=pbk, g4=g4: e.tensor_copy(U[:, g4 * 4:(g4 + 1) * 4, :].rearrange("p a b -> p (a b)"), psb16[pbk][:, 0:512]), reads=[PSK[pbk]], writes=["U"])
                    for ri in range(2):
                        for g24 in range(4):
                            pb = ps[6 + (g24 % 2)]
                            pk = PSK[6 + (g24 % 2)]
                            for q in range(4):
                                g2 = g24 * 4 + q
                                for gp in range(2):
                                    g = 2 * g2 + gp
                                    P.op("pe", lambda e, pb=pb, q=q, g2=g2, gp=gp, g=g, ri=ri: e.matmul(pb[gp * 64:(gp + 1) * 64, q * 128:(q + 1) * 128], Zw[:, ri, g2, gp * 64:(gp + 1) * 64], U[:, g, :], start=True, stop=True),
                                         reads=["Zw", "U"], writes=[pk])
                            P.op("dve", lambda e, pb=pb, ri=ri, g24=g24: e.tensor_copy(XA[:, ri, g24 * 4:(g24 + 1) * 4, :].rearrange("p a b -> p (a b)"), pb[:, :]), reads=[pk], writes=["XA"])
                    cnt = 0
                    for ct in range(4):
                        for sq in range(2):
                            pb = ps[cnt % 2]
                            pk = PSK[cnt % 2]
                            cnt += 1
                            for q in range(4):
                                sg = sq * 4 + q
                                for kt in range(8):
                                    P.op("pe", lambda e, pb=pb, ct=ct, kt=kt, q=q, sg=sg: e.matmul(pb[:, q * 128:(q + 1) * 128], wch[:, kt, ct * 128:(ct + 1) * 128], uTs[:, kt, sg, :], start=(kt == 0), stop=(kt == 7)),
                                         reads=["wch", "uTs"], writes=[pk])
                            dstv = s5in[:, ct, 0:1024].rearrange("p (c s) -> p s c", s=8)[:, sq * 4:(sq + 1) * 4, :]
                            P.op("act", lambda e, pb=pb, dstv=dstv: e.activation(dstv, pb[:, :].rearrange("p (s c) -> p s c", s=4), AF.Copy), reads=[pk], writes=["s5in"])
                        if sbk == 1:
                            pb = ps[cnt % 2]
                            pk = PSK[cnt % 2]
                            cnt += 1
                            for kt in range(8):
                                P.op("pe", lambda e, pb=pb, ct=ct, kt=kt: e.matmul(pb[:, 0:16], wch[:, kt, ct * 128:(ct + 1) * 128], uS[:, kt, :], start=(kt == 0), stop=(kt == 7)), reads=["wch", "uTs"], writes=[pk])
                            P.op("act", lambda e, pb=pb, ct=ct: e.activation(s5in[:, ct, 1024:1040], pb[:, 0:16], AF.Copy), reads=[pk], writes=["s5in"])
                    L0 = 3
                    P.op("dve", lambda e: e.tensor_tensor(prm[:, 4, :], LP[:, L0, 0, :], s5car[:, 0, :], ALU.mult), reads=["LP", "s5car"], writes=["prm"])
                    P.op("dve", lambda e: e.tensor_tensor(prm[:, 11, :], LP[:, L0, 1, :], s5car[:, 1, :], ALU.mult), reads=["LP", "s5car"], writes=["prm"])
                    P.op("dve", lambda e: e.tensor_tensor(prm[:, 4, :], prm[:, 4, :], prm[:, 11, :], ALU.subtract), reads=["prm"], writes=["prm"])
                    P.op("dve", lambda e: e.tensor_tensor(XA[:, 0, :, 0], XA[:, 0, :, 0], prm[:, 4, :], ALU.add), reads=["XA", "prm"], writes=["XA"])
                    P.op("dve", lambda e: e.tensor_tensor(prm[:, 4, :], LP[:, L0, 0, :], s5car[:, 1, :], ALU.mult), reads=["LP", "s5car"], writes=["prm"])
                    P.op("dve", lambda e: e.tensor_tensor(prm[:, 11, :], LP[:, L0, 1, :], s5car[:, 0, :], ALU.mult), reads=["LP", "s5car"], writes=["prm"])
                    P.op("dve", lambda e: e.tensor_tensor(prm[:, 4, :], prm[:, 4, :], prm[:, 11, :], ALU.add), reads=["prm"], writes=["prm"])
                    P.op("dve", lambda e: e.tensor_tensor(XA[:, 1, :, 0], XA[:, 1, :, 0], prm[:, 4, :], ALU.add), reads=["XA", "prm"], writes=["XA"])
                    cur_, nxt_ = XA, XB
                    ck_, nk = "XA", "XB"
                    TB = 128
                    k = 0
                    P.op("pool", lambda e: e.engine_nop() if False else e.memset(gs[:, 0:1], 0.0), reads=["XA"], writes=[("XA", g2) for g2 in range(16)] + ["gs"])
                    while (1 << k) < TB:
                        s_ = 1 << k
                        P.op("act", lambda e, cur_=cur_, nxt_=nxt_, s_=s_: e.activation(nxt_[:, :, :, 0:s_], cur_[:, :, :, 0:s_], AF.Copy),
                             reads=[(ck_, g2) for g2 in range(16)], writes=[(nk, g2) for g2 in range(16)])
                        for opi in range(3):
                            for g2 in range(16):
                                lr = LP[:, L0 + k, 0, g2:g2 + 1]
                                li = LP[:, L0 + k, 1, g2:g2 + 1]
                                nli = LP[:, L0 + k, 2, g2:g2 + 1]
                                if opi == 0:
                                    P.op("dve", lambda e, cur_=cur_, nxt_=nxt_, s_=s_, g2=g2, lr=lr: e.scalar_tensor_tensor(nxt_[:, :, g2, s_:TB], cur_[:, :, g2, 0:TB - s_], lr, cur_[:, :, g2, s_:TB], ALU.mult, ALU.add),
                                         reads=[(ck_, g2), "LP"], writes=[(nk, g2, 0), (nk, g2, 1)])
                                elif opi == 1:
                                    P.op("dve", lambda e, cur_=cur_, nxt_=nxt_, s_=s_, g2=g2, nli=nli: e.scalar_tensor_tensor(nxt_[:, 0, g2, s_:TB], cur_[:, 1, g2, 0:TB - s_], nli, nxt_[:, 0, g2, s_:TB], ALU.mult, ALU.add), reads=[(ck_, g2), (nk, g2, 0), "LP"], writes=[(nk, g2, 0)])
                                else:
                                    P.op("dve", lambda e, cur_=cur_, nxt_=nxt_, s_=s_, g2=g2, li=li: e.scalar_tensor_tensor(nxt_[:, 1, g2, s_:TB], cur_[:, 0, g2, 0:TB - s_], li, nxt_[:, 1, g2, s_:TB], ALU.mult, ALU.add), reads=[(ck_, g2), (nk, g2, 1), "LP"], writes=[(nk, g2, 1)])
                        P.op("dve", lambda e: e.memset(gs[:, 0:1], 0.0), reads=[(nk, g2, r) for g2 in range(16) for r in range(2)] + [(ck_, g2) for g2 in range(16)],
                             writes=[(nk, g2) for g2 in range(16)] + [(ck_, g2, r) for g2 in range(16) for r in range(2)] + ["gs"])
                        cur_, nxt_ = nxt_, cur_
                        ck_, nk = nk, ck_
                        k += 1
                    P.op("dve", lambda e: e.memset(gs[:, 0:1], 0.0), reads=[(ck_, g2) for g2 in range(16)], writes=[ck_, "gs"])
                    P.op("act", lambda e: e.activation(Hp[:, :, :, 0], s5car[:, :, :], AF.Copy), reads=["s5car", "wch", "s5in", "TM"], writes=["wch"])
                    P.op("act", lambda e, cur_=cur_: e.activation(Hp[:, :, :, 1:128], cur_[:, :, :, 0:127], AF.Copy), reads=[ck_, "s5in", "TM"], writes=["wch"])
                    P.op("dve", lambda e, cur_=cur_: e.tensor_copy(s5car[:, :, :], cur_[:, :, :, 127]), reads=[ck_, "wch"], writes=["s5car"])
                    for g4 in range(8):
                        pb = ps[g4 % 2]
                        pk = PSK[g4 % 2]
                        for q in range(4):
                            g = g4 * 4 + q
                            g2 = g // 2
                            O = pb[:, q * 128:(q + 1) * 128]
                            P.op("pe", lambda e, O=O, g=g: e.matmul(O, Mw[:, g, :], U[:, g, :], start=True, stop=False), reads=["Mw", "U"], writes=[pk])
                            P.op("pe", lambda e, O=O, g=g, g2=g2: e.matmul(O, YwZ[:, 0, g, :], Hp[:, 0, g2, :], start=False, stop=False), reads=["YwZ", "wch"], writes=[pk])
                            P.op("pe", lambda e, O=O, g=g, g2=g2: e.matmul(O, YwZ[:, 1, g, :], Hp[:, 1, g2, :], start=False, stop=True), reads=["YwZ", "wch"], writes=[pk])
                        ysb = Ysb[g4 % 2]
                        yk = "Ysb%d" % (g4 % 2)
                        P.op("act", lambda e, pb=pb, ysb=ysb: e.activation(ysb[:, :], pb[:, :], AF.Copy), reads=[pk], writes=[yk])
                        pb2 = ps[2 + (g4 % 2)]
                        pk2 = PSK[2 + (g4 % 2)]
                        for q in range(4):
                            P.op("pe", lambda e, pb2=pb2, q=q, ysb=ysb: e.transpose(pb2[:, q * 128:(q + 1) * 128], ysb[:, q * 128:(q + 1) * 128], identF[:, :]), reads=[yk, "identF"], writes=[pk2])
                        dst = TMY[:, :, g4 * 64:(g4 + 1) * 64].rearrange("p t (q c) -> p q t c", q=4)
                        src = pb2[:, :].rearrange("p (q t c) -> p q t c", q=4, t=8)
                        P.op("dve", lambda e, dst=dst, src=src: e.tensor_copy(dst, src), reads=[pk2], writes=["uTs"])
                    for ct in range(4):
                        for tq_ in range(2):
                            pb = ps[4 + (tq_ % 2)]
                            pk = PSK[4 + (tq_ % 2)]
                            for q in range(4):
                                t_ = tq_ * 4 + q
                                P.op("pe", lambda e, pb=pb, q=q, t_=t_, ct=ct: e.transpose(pb[:, q * 128:(q + 1) * 128], TMY[:, t_, ct * 128:(ct + 1) * 128], identF[:, :]), reads=["uTs", "identF"], writes=[pk])
                            sv = s5in[:, ct, 0:1024].rearrange("p (c s) -> p s c", s=8)[:, tq_ * 4:(tq_ + 1) * 4, :]
                            zv = zT[:, ct, 0:1024].rearrange("p (c s) -> p s c", s=8)[:, tq_ * 4:(tq_ + 1) * 4, :]
                            v3 = lambda t: t[:, :].rearrange("p (s c) -> p s c", s=4)
                            yv_, gw_, gs_ = yvB[tq_], gwB[tq_], gsB[tq_]
                            yk_, wk_, sk_ = "yv%d" % tq_, "gw%d" % tq_, "gs%d" % tq_
                            P.op("dve", lambda e, pb=pb, ct=ct, sv=sv, yv_=yv_: e.scalar_tensor_tensor(v3(yv_), sv, CP[:, 104 + ct:105 + ct], v3(pb), ALU.mult, ALU.add), reads=["s5in", "CP", pk], writes=[yk_])
                            P.op("act", lambda e, yv_=yv_, gw_=gw_: e.activation(gw_[:, :], yv_[:, :], AF.Square), reads=[yk_], writes=[wk_])
                            P.op("dve", lambda e, gw_=gw_: e.tensor_scalar(gw_[:, :], gw_[:, :], 0.044715, 1.0, ALU.mult, ALU.add), reads=[wk_], writes=[wk_])
                            P.op("pool", lambda e, gw_=gw_, yv_=yv_: e.tensor_tensor(gw_[:, :], gw_[:, :], yv_[:, :], ALU.mult), reads=[wk_, yk_], writes=[wk_])
                            P.op("act", lambda e, gw_=gw_, gs_=gs_: e.activation(gs_[:, :], gw_[:, :], AF.Sigmoid, scale=1.5957691216057308), reads=[wk_], writes=[sk_])
                            P.op("dve", lambda e, zv=zv, yv_=yv_, gs_=gs_: e.tensor_tensor(zv, v3(yv_), v3(gs_), ALU.mult), reads=[yk_, sk_], writes=["zT"])
                    if sbk == 1:
                        for ri, nm in ((0, "s5reP"), (1, "s5imP")):
                            P.op("pe", lambda e, ri=ri: e.transpose(ps[4][0:16, 0:128], s5car[:, ri, :], identF[:, :]), reads=["s5car", "identF"], writes=["ps4"])
                            P.op("act", lambda e: e.activation(yv[0:16, 0:128], ps[4][0:16, 0:128], AF.Copy), reads=["ps4"], writes=["yv0"])
                            P.dma("sp", dr[nm], yv[0:16, 0:128], reads=["yv0"])
                        TMf = TMU[:, :].bitcast(F32)
                        TMs = TMU[0:16, 0:4096].rearrange("p (g s c) -> p g s c", g=32, s=8)
                        Us = sbt("Us", [128, 32, 16], BF16)
                        h0 = sbt("h0", [128, 2, 16, 16])
                        x1 = sbt("x1", [128, 2, 16, 16])
                        t16 = sbt("t16", [128, 16, 16])
                        Hps = sbt("Hps", [128, 2, 16, 16], BF16)
                        P.op("pool", lambda e: e.memset(TMU[0:16, 0:4096], 0.0), reads=["TM", "U"], writes=["TM"])
                        for ct in range(4):
                            P.op("pe", lambda e, ct=ct: e.transpose(psb16[0][0:16, ct * 128:(ct + 1) * 128], s5in[:, ct, 1024:1040], identB[:, :]), reads=["s5in", "identB"], writes=["ps0"])
                        P.op("act", lambda e: e.activation(TMs[:, :, 7, :], psb16[0][0:16, 0:512].rearrange("p (g c) -> p g c", g=32), AF.Copy), reads=["ps0"], writes=["TM"])
                        for g4 in range(8):
                            for q in range(4):
                                g = g4 * 4 + q
                                P.op("pe", lambda e, g=g: e.transpose(psb16[1][:, g * 16:(g + 1) * 16], TMs[:, g, :, :].rearrange("p a b -> p (a b)"), identB[0:16, 0:16]), reads=["TM", "identB"], writes=["ps1"])
                        P.op("act", lambda e: e.activation(Us[:].rearrange("p a b -> p (a b)"), psb16[1][:, 0:512], AF.Copy), reads=["ps1"], writes=["Us"])
                        for ri in range(2):
                            for g2 in range(16):
                                for gp in range(2):
                                    g = 2 * g2 + gp
                                    c_ = (ri * 16 + g2) * 16
                                    P.op("pe", lambda e, ri=ri, g2=g2, gp=gp, g=g, c_=c_: e.matmul(ps[2][gp * 64:(gp + 1) * 64, c_:c_ + 16], Zw[:, ri, g2, gp * 64:(gp + 1) * 64], Us[:, g, :], start=True, stop=True),
                                         reads=["Zw", "Us"], writes=["ps2"])
                        Zs = ps[2][:, :].rearrange("p (r g b) -> p r g b", r=2, g=16)
                        h0in = TMf[0:16, 2048:4096]
                        for ri in range(2):
                            P.dma("sp", h0in, dr["s5re0" if ri == 0 else "s5im0"], reads=["U", "TM", "Us"], writes=["U"])
                            for g2 in range(16):
                                P.op("pe", lambda e, ri=ri, g2=g2: e.transpose(ps[5][:, g2 * 16:(g2 + 1) * 16], h0in[:, g2 * 128:(g2 + 1) * 128], identF[0:16, 0:16]), reads=["U", "identF"], writes=["ps5"])
                            P.op("act", lambda e, ri=ri: e.activation(h0[:, ri, :, :].rearrange("p a b -> p (a b)"), ps[5][:, 0:256], AF.Copy), reads=["ps5"], writes=["h0"])
                        lrb = LP[:, 0, 0, :].unsqueeze(2).to_broadcast([128, 16, 16])
                        lib = LP[:, 0, 1, :].unsqueeze(2).to_broadcast([128, 16, 16])
                        P.op("dve", lambda e: e.tensor_tensor(x1[:, 0, :, :], h0[:, 0, :, :], lrb, ALU.mult), reads=["h0", "LP"], writes=["x1"])
                        P.op("dve", lambda e: e.tensor_tensor(t16[:], h0[:, 1, :, :], lib, ALU.mult), reads=["h0", "LP"], writes=["t16"])
                        P.op("dve", lambda e: e.tensor_tensor(x1[:, 0, :, :], x1[:, 0, :, :], t16[:], ALU.subtract), reads=["x1", "t16"], writes=["x1"])
                        P.op("dve", lambda e: e.tensor_tensor(x1[:, 0, :, :], x1[:, 0, :, :], Zs[:, 0, :, :], ALU.add), reads=["x1", "ps2"], writes=["x1"])
                        P.op("dve", lambda e: e.tensor_tensor(x1[:, 1, :, :], h0[:, 1, :, :], lrb, ALU.mult), reads=["h0", "LP"], writes=["x1"])
                        P.op("dve", lambda e: e.tensor_tensor(t16[:], h0[:, 0, :, :], lib, ALU.mult), reads=["h0", "LP"], writes=["t16"])
                        P.op("dve", lambda e: e.tensor_tensor(x1[:, 1, :, :], x1[:, 1, :, :], t16[:], ALU.add), reads=["x1", "t16"], writes=["x1"])
                        P.op("dve", lambda e: e.tensor_tensor(x1[:, 1, :, :], x1[:, 1, :, :], Zs[:, 1, :, :], ALU.add), reads=["x1", "ps2"], writes=["x1"])
                        for ri, nm in ((0, "s5reS"), (1, "s5imS")):
                            for half in range(2):
                                for q in range(8):
                                    g2 = half * 8 + q
                                    P.op("pe", lambda e, ri=ri, g2=g2, q=q, half=half: e.transpose(ps[6 + half][0:16, (q % 4) * 128:(q % 4 + 1) * 128], x1[:, ri, g2, :], identF[:, :]), reads=["x1", "identF"], writes=[PSK[6 + half]])
                                    if q == 3 or q == 7:
                                        lo = half * 1024 + (0 if q == 3 else 512)
                                        P.op("act", lambda e, half=half, lo=lo: e.activation(h0in[:, lo:lo + 512], ps[6 + half][0:16, :], AF.Copy), reads=[PSK[6 + half]], writes=["U"])
                            P.dma("sp", dr[nm], h0in, reads=["U"])
                        ir = LI[:, 0, :].unsqueeze(2).to_broadcast([128, 16, 16])
                        ii = LI[:, 1, :].unsqueeze(2).to_broadcast([128, 16, 16])
                        hx = h0
                        P.op("dve", lambda e: e.tensor_tensor(hx[:, 0, :, :], x1[:, 0, :, :], ir, ALU.mult), reads=["x1", "LI"], writes=["h0"])
                        P.op("dve", lambda e: e.tensor_tensor(t16[:], x1[:, 1, :, :], ii, ALU.mult), reads=["x1", "LI"], writes=["t16"])
                        P.op("dve", lambda e: e.tensor_tensor(hx[:, 0, :, :], hx[:, 0, :, :], t16[:], ALU.subtract), reads=["h0", "t16"], writes=["h0"])
                        P.op("dve", lambda e: e.tensor_tensor(hx[:, 1, :, :], x1[:, 1, :, :], ir, ALU.mult), reads=["x1", "LI"], writes=["h0"])
                        P.op("dve", lambda e: e.tensor_tensor(t16[:], x1[:, 0, :, :], ii, ALU.mult), reads=["x1", "LI"], writes=["t16"])
                        P.op("dve", lambda e: e.tensor_tensor(hx[:, 1, :, :], hx[:, 1, :, :], t16[:], ALU.add), reads=["h0", "t16"], writes=["h0"])
                        P.op("act", lambda e: e.activation(Hps[:], hx[:], AF.Copy), reads=["h0"], writes=["Hps"])
                        for g in range(32):
                            g2 = g // 2
                            P.op("pe", lambda e, g=g, g2=g2: e.matmul(ps[3][0:16, g * 16:(g + 1) * 16], Hps[:, 0, g2, :], YwZ[:, 0, g, 0:16], start=True, stop=False), reads=["Hps", "YwZ"], writes=["ps3"])
                            P.op("pe", lambda e, g=g, g2=g2: e.matmul(ps[3][0:16, g * 16:(g + 1) * 16], Hps[:, 1, g2, :], YwZ[:, 1, g, 0:16], start=False, stop=True), reads=["Hps", "YwZ"], writes=["ps3"])
                        P.op("act", lambda e: e.activation(yv[0:16, :], ps[3][0:16, :], AF.Copy), reads=["ps3"], writes=["yv0"])
                        for ct in range(4):
                            P.op("pe", lambda e, ct=ct: e.transpose(ps[4][:, ct * 16:(ct + 1) * 16], yv[0:16, ct * 128:(ct + 1) * 128], identF[0:16, 0:16]), reads=["yv0", "identF"], writes=["ps4"])
                        ysv = sbt("ysv", [128, 4, 16])
                        gws = sbt("gws", [128, 4, 16])
                        sv = s5in[:, :, 1024:1040]
                        for ct in range(4):
                            P.op("dve", lambda e, ct=ct: e.scalar_tensor_tensor(ysv[:, ct, :], s5in[:, ct, 1024:1040], CP[:, 104 + ct:105 + ct], ps[4][:, ct * 16:(ct + 1) * 16], ALU.mult, ALU.add), reads=["s5in", "CP", "ps4"], writes=["ysv"])
                        P.op("act", lambda e: e.activation(gws[:], ysv[:], AF.Square), reads=["ysv"], writes=["gws"])
                        P.op("dve", lambda e: e.tensor_scalar(gws[:], gws[:], 0.044715, 1.0, ALU.mult, ALU.add), reads=["gws"], writes=["gws"])
                        P.op("dve", lambda e: e.tensor_tensor(gws[:], gws[:], ysv[:], ALU.mult), reads=["gws", "ysv"], writes=["gws"])
                        P.op("act", lambda e: e.activation(gws[:], gws[:], AF.Sigmoid, scale=1.5957691216057308), reads=["gws"], writes=["gws"])
                        P.op("dve", lambda e: e.tensor_tensor(zT[:, :, 1024:1040], ysv[:], gws[:], ALU.mult), reads=["ysv", "gws"], writes=["zT"])
                    if sbk == 1:
                        dbg_dump("d_zT", zT[:].rearrange("p a b -> p (a b)"), "zT")
                    P.flush()

        C0 = -0.6065306597126334
        RWSTOP = int(os.environ.get('RWSTOP', '99'))

        class _Stop(Exception):
            pass

        def ck(l):
            if RWSTOP <= l:
                raise _Stop()

        def rwkv_phase(sbk, sbx, oT):
            nT2 = [sbx("uTt%d" % i, [128, 8, 128], BF16) for i in range(2)]
            tix = [0]
            wcur = sbx("wcur", [128, 8, NSH], BF16)
            for c4 in range(4):
                lo = 512 + c4 * 512
                w = 512 if c4 < 3 else 256
                P.dma("pool", wcur[:, :, c4 * 512:c4 * 512 + w], dr["w_in"][:, lo:lo + w].rearrange("(kt p) n -> p kt n", p=128), writes=["wcur"])
            rowp = [sbx("rowp%d" % i, [1, 512]) for i in range(2)]
            bc = sbx("bc", [128, NSH + 7 * 512])
            pieces = [("mu_shift", q * 512, 512 if q < 3 else 256, q * 512) for q in range(4)]
            for i, nm in enumerate(("rwkv_w0", "rwkv_a0", "rwkv_k_k", "rwkv_k_a", "rwkv_ln_w", "rwkv_ln_b", "rwkv_r_k")):
                pieces.append((nm, 0, 512, NSH + i * 512))
            for pc, (nm, so, w, lo) in enumerate(pieces):
                rp = rowp[pc % 2]
                rk = "rowp%d" % (pc % 2)
                pb = ps[pc % 2]
                P.dma("sp", rp[0:1, 0:w], dr[nm][:, so:so + w], writes=[rk])
                P.op("pe", lambda e, pb=pb, rp=rp, w=w: e.matmul(pb[:, 0:w], onesF[0:1, :], rp[0:1, 0:w], start=True, stop=True), reads=["onesF", rk], writes=[PSK[pc % 2]])
                P.op("act", lambda e, pb=pb, lo=lo, w=w: e.activation(bc[:, lo:lo + w], pb[:, 0:w], AF.Copy), reads=[PSK[pc % 2]], writes=["bc"])
            mu_bc = bc[:, 0:NSH]
            def bcs(i):
                return bc[:, NSH + i * 512: NSH + (i + 1) * 512]
            w0_bc, a0_bc, kk_bc, ka_bc, lnw_bc, lnb_bc, rk_bc = [bcs(i) for i in range(7)]
            LW = sbx("LW", [128, 512], BF16)
            G2 = sbx("G2w", [128, 512], BF16)
            P.dma("pool", LW[0:64, :], dr["rwkv_w2"], writes=["LW"])
            P.dma("pool", LW[64:128, :], dr["rwkv_a2"], writes=["LW"])
            P.dma("pool", G2[:, :], dr["rwkv_g2"], writes=["G2w"])
            mSU = sbx("mSU", [128, 128])
            mIU = sbx("mIU", [128, 128])
            mSL = sbx("mSL", [128, 128])
            ShM = sbx("ShM", [128, 128])
            CM = sbx("CM", [128, 512])
            P.op("pool", lambda e: e.affine_select(mSU[:], onesF[:], [[1, 128]], ALU.is_gt, 0.0, base=0, channel_multiplier=-1), reads=["onesF"], writes=["mSU"])
            P.op("pool", lambda e: e.affine_select(mIU[:], onesF[:], [[1, 128]], ALU.is_ge, 0.0, base=0, channel_multiplier=-1), reads=["onesF"], writes=["mIU"])
            P.op("pool", lambda e: e.affine_select(mSL[:], onesF[:], [[-1, 128]], ALU.is_gt, 0.0, base=0, channel_multiplier=1), reads=["onesF"], writes=["mSL"])
            P.op("pool", lambda e: e.affine_select(ShM[:], onesF[:], [[1, 128]], ALU.is_equal, 0.0, base=-1, channel_multiplier=-1), reads=["onesF"], writes=["ShM"])
            P.op("pool", lambda e: e.tensor_tensor(ShM[:], ShM[:], identF[:], ALU.subtract), reads=["ShM", "identF"], writes=["ShM"])
            for q in range(4):
                src = mSU if q % 2 == 0 else mIU
                P.op("pool", lambda e, q=q, src=src: e.tensor_copy(CM[:, q * 128:(q + 1) * 128], src[:]), reads=["mSU", "mIU"], writes=["CM"])
            ck(1)
            cur = sbx("cur", [128, NSH])
            mixed = sbx("mixed", [128, NSH])
            Lb = sbx("Lb", [128, 256], BF16)
            LT = sbx("LT", [128, 2, 128], BF16)
            F = {n: sbx("f_" + n, [128, 512]) for n in ("logw", "a", "g", "kk", "k2", "kka", "t1", "t2", "y")}
            st8 = sbx("st8", [128, 6, 8])
            TB16 = {n: sbx("b_" + n, [128, 512], BF16) for n in ("At", "Bt", "Kt", "Rt", "Bg", "Kg", "Vb", "ob")}
            XT = {n: sbx("xt_" + n, [128, 4, 128], BF16) for n in ("At", "Bt", "Kt", "Rt")}
            gT = sbx("gT", [128, 4])
            NHG = 8
            AMall = sbx("AMall", [128, NHG, 512], BF16)
            AM = [AMall[:, i, :] for i in range(NHG)]
            ptmp = AMall[:, 0:2, :].rearrange("p a b -> p (a b)").bitcast(F32)
            Fg = [F["g"], sbx("f_g2", [128, 512])]
            bon = [sbx("bon%d" % i, [128, 512]) for i in range(2)]
            Ak = [[sbx("Ak%d_%d" % (i, j), [128, 384], BF16) for j in range(2)] for i in range(NHG)]
            W1 = [sbx("W1_%d" % i, [128, 64], BF16) for i in range(NHG)]
            AU = [sbx("AU%d" % i, [128, 128], BF16) for i in range(8)]
            RhT = sbx("RhT", [128, 8, 128], BF16)
            GTp = sbx("GTp", [128, 8, 64], BF16)
            P.op("pool", lambda e: e.memset(RhT[:], 0.0), writes=["RhT"])
            P.op("pool", lambda e: e.memset(GTp[:], 0.0), writes=["GTp"])
            r_ = mixed[:, 0:512]
            k_ = mixed[:, 512:1024]
            v_ = mixed[:, 1024:1536]

            def prelim(rows, tcols, first_tile, samp, par):
                R = slice(0, rows)
                gk = "f_g%d" % par
                nT = nT2[tix[0] % 2]
                nTk = "uTt%d" % (tix[0] % 2)
                tix[0] += 1
                P.dma("sp", nT[:, :, 0:rows], uT_scr[:, :, tcols], writes=[nTk])
                for pc in range(4):
                    lo = pc * 512
                    w = 512 if pc < 3 else 256
                    pb = ps[pc]
                    for kt in range(8):
                        P.op("pe", lambda e, pb=pb, kt=kt, lo=lo, w=w, nT=nT: e.matmul(pb[R, 0:w], nT[:, kt, 0:rows], wcur[:, kt, lo:lo + w], start=(kt == 0), stop=(kt == 7)),
                             reads=[nTk, "wcur"], writes=[PSK[pc]])
                    if pc % 2 == 0:
                        P.op("act", lambda e, pb=pb, lo=lo, w=w: e.activation(cur[R, lo:lo + w], pb[R, 0:w], AF.Copy), reads=[PSK[pc]], writes=["cur"])
                    else:
                        P.op("dve", lambda e, pb=pb, lo=lo, w=w: e.tensor_copy(cur[R, lo:lo + w], pb[R, 0:w]), reads=[PSK[pc]], writes=["cur"])
                ck(2)
                if not samp:
                    for pc in range(4):
                        lo = pc * 512
                        w = 512 if pc < 3 else 256
                        bsh = 4 + (pc % 3)
                        pb = ps[bsh]
                        P.op("pe", lambda e, pb=pb, lo=lo, w=w: e.matmul(pb[R, 0:w], ShM[R, 0:rows], cur[R, lo:lo + w], start=True, stop=first_tile), reads=["ShM", "cur"], writes=[PSK[bsh]])
                        if not first_tile:
                            P.op("pe", lambda e, pb=pb, lo=lo, w=w: e.matmul(pb[R, 0:w], identF[0:1, 0:rows], lastrow[0:1, lo:lo + w], start=False, stop=True), reads=["identF", "lastrow"], writes=[PSK[bsh]])
                        P.op("dve", lambda e, pb=pb, lo=lo, w=w: e.tensor_tensor(mixed[R, lo:lo + w], pb[R, 0:w], mu_bc[R, lo:lo + w], ALU.mult), reads=[PSK[bsh], "bc"], writes=["mixed"])
                        P.op("dve", lambda e, lo=lo, w=w: e.tensor_tensor(mixed[R, lo:lo + w], mixed[R, lo:lo + w], cur[R, lo:lo + w], ALU.add), reads=["mixed", "cur"], writes=["mixed"])
                    P.dma("sp", lastrow[0:1, :], cur[127:128, :], reads=["cur"], writes=["lastrow"])
                else:
                    P.dma("sp", mixed[R, :], dr["shift0"], writes=["mixed"])
                    P.op("dve", lambda e: e.tensor_tensor(mixed[R, :], mixed[R, :], cur[R, :], ALU.subtract), reads=["mixed", "cur"], writes=["mixed"])
                    P.op("dve", lambda e: e.tensor_tensor(mixed[R, :], mixed[R, :], mu_bc[R, :], ALU.mult), reads=["mixed", "bc"], writes=["mixed"])
                    P.op("dve", lambda e: e.tensor_tensor(mixed[R, :], mixed[R, :], cur[R, :], ALU.add), reads=["mixed", "cur"], writes=["mixed"])
                ck(3)
                m_l = P.mark()
                P.op("act", lambda e: e.activation(Lb[R, 0:64], mixed[R, 1536:1600], AF.Tanh), reads=["mixed"], writes=["Lb"])
                P.op("act", lambda e: e.activation(Lb[R, 64:128], mixed[R, 1600:1664], AF.Copy), reads=["mixed"], writes=["Lb"])
                P.op("act", lambda e: e.activation(Lb[R, 128:256], mixed[R, 1664:1792], AF.Sigmoid), reads=["mixed"], writes=["Lb"])
                for q in range(2):
                    P.op("pe", lambda e, q=q: e.transpose(psb16[0][:, q * 128:q * 128 + rows], Lb[R, q * 128:(q + 1) * 128], identB[0:rows, 0:rows]), reads=["Lb", "identB"], writes=["ps0"])
                P.op("dve", lambda e: e.tensor_copy(LT[:, :, 0:rows], psb16[0][:, 0:256].rearrange("p (q t) -> p q t", q=2)[:, :, 0:rows]), reads=["ps0"], writes=["LT"])
                P.op("pe", lambda e: e.matmul(ps[1][R, :], LT[0:64, 0, 0:rows], LW[0:64, :], start=True, stop=True), reads=["LT", "LW"], writes=["ps1"])
                P.op("pe", lambda e: e.matmul(ps[2][R, :], LT[64:128, 0, 0:rows], LW[64:128, :], start=True, stop=True), reads=["LT", "LW"], writes=["ps2"])
                P.op("pe", lambda e: e.matmul(ps[3][R, :], LT[:, 1, 0:rows], G2[:, :], start=True, stop=True), reads=["LT", "G2w"], writes=["ps3"])
                P.op("dve", lambda e: e.tensor_tensor(F["t1"][R, :], ps[1][R, :], w0_bc[R, :], ALU.add), reads=["ps1", "bc"], writes=["f_t1"])
                P.op("act", lambda e: e.activation(F["t1"][R, :], F["t1"][R, :], AF.Sigmoid), reads=["f_t1"], writes=["f_t1"])
                P.op("dve", lambda e: e.tensor_scalar(F["logw"][R, :], F["t1"][R, :], C0, None, ALU.mult), reads=["f_t1"], writes=["f_logw"])
                P.op("dve", lambda e: e.tensor_tensor(F["t2"][R, :], ps[2][R, :], a0_bc[R, :], ALU.add), reads=["ps2", "bc"], writes=["f_t2"])
                P.op("act", lambda e: e.activation(F["a"][R, :], F["t2"][R, :], AF.Sigmoid), reads=["f_t2"], writes=["f_a"])
                P.op("act", lambda e: e.activation(Fg[par][R, :], ps[3][R, :], AF.Copy), reads=["ps3"], writes=[gk])
                l_lora = P.take(m_l)
                P.op("dve", lambda e: e.tensor_tensor(F["kk"][R, :], k_[R, :], kk_bc[R, :], ALU.mult), reads=["mixed", "bc"], writes=["f_kk"])
                P.op("dve", lambda e: e.tensor_tensor(F["kka"][R, :], F["kk"][R, :], F["kk"][R, :], ALU.mult), reads=["f_kk"], writes=["f_kka"])
                P.op("dve", lambda e: e.tensor_reduce(st8[R, 0, :], F["kka"][R, :].rearrange("p (h j) -> p h j", h=8), AX.X, ALU.add), reads=["f_kka"], writes=["st8a"])
                l_kk = P.take(m_l)
                P.put_interleaved(l_lora, l_kk)
                P.op("act", lambda e: e.activation(st8[R, 0, :], st8[R, 0, :], AF.Sqrt), reads=["st8a"], writes=["st8a"])
                P.op("dve", lambda e: e.tensor_scalar_max(st8[R, 0, :], st8[R, 0, :], 1e-12), reads=["st8a"], writes=["st8a"])
                P.op("dve", lambda e: e.reciprocal(st8[R, 0, :], st8[R, 0, :]), reads=["st8a"], writes=["st8a"])
                P.op("dve", lambda e: e.tensor_tensor(F["kk"][R, :].rearrange("p (h j) -> p h j", h=8), F["kk"][R, :].rearrange("p (h j) -> p h j", h=8),
                                                      st8[R, 0, :].unsqueeze(2).to_broadcast([rows, 8, 64]), ALU.mult), reads=["f_kk", "st8a"], writes=["f_kk"])
                P.op("dve", lambda e: e.scalar_tensor_tensor(F["t2"][R, :], F["a"][R, :], -1.0, ka_bc[R, :], ALU.add, ALU.mult), reads=["f_a", "bc"], writes=["f_t2"])
                P.op("dve", lambda e: e.scalar_tensor_tensor(F["k2"][R, :], F["t2"][R, :], 1.0, k_[R, :], ALU.add, ALU.mult), reads=["f_t2", "mixed"], writes=["f_k2"])
                P.op("pool", lambda e: e.tensor_tensor(F["kka"][R, :], F["kk"][R, :], F["a"][R, :], ALU.mult), reads=["f_kk", "f_a"], writes=["f_kka"])

            def bonus(rows, par):
                R = slice(0, rows)
                v3p = lambda ap: ap.rearrange("p (h j) -> p h j", h=8)
                bk = "bon%d" % par
                P.op("dve", lambda e: e.tensor_tensor(F["t1"][R, :], r_[R, :], F["k2"][R, :], ALU.mult), reads=["mixed", "f_k2"], writes=["f_t1"])
                P.op("dve", lambda e: e.tensor_tensor(F["t1"][R, :], F["t1"][R, :], rk_bc[R, :], ALU.mult), reads=["f_t1", "bc"], writes=["f_t1"])
                P.op("dve", lambda e: e.tensor_reduce(st8[R, 3, :], v3p(F["t1"][R, :]), AX.X, ALU.add), reads=["f_t1"], writes=["st8a"])
                P.op("dve", lambda e: e.tensor_tensor(v3p(bon[par][R, :]), v3p(v_[R, :]), st8[R, 3, :].unsqueeze(2).to_broadcast([rows, 8, 64]), ALU.mult), reads=["mixed", "st8a"], writes=[bk])

            def post(rows, tcols, par):
                R = slice(0, rows)
                v3 = lambda ap: ap.rearrange("p (h j) -> p h j", h=8)
                bc8 = lambda i: st8[R, i, :].unsqueeze(2).to_broadcast([rows, 8, 64])
                y = F["y"]
                P.op("dve", lambda e: e.tensor_reduce(st8[R, 1, :], v3(y[R, :]), AX.X, ALU.add), reads=["f_y"], writes=["st8"])
                P.op("dve", lambda e: e.tensor_scalar(st8[R, 1, :], st8[R, 1, :], 1.0 / 64, None, ALU.mult), reads=["st8"], writes=["st8"])
                P.op("dve", lambda e: e.tensor_tensor(v3(y[R, :]), v3(y[R, :]), bc8(1), ALU.subtract), reads=["f_y", "st8"], writes=["f_y"])
                P.op("dve", lambda e: e.tensor_tensor(ptmp[R, :], y[R, :], y[R, :], ALU.mult), reads=["f_y", "AM0", "AM1"], writes=["AM0", "AM1"])
                P.op("dve", lambda e: e.tensor_reduce(st8[R, 2, :], v3(ptmp[R, :]), AX.X, ALU.add), reads=["AM0", "AM1"], writes=["st8"])
                P.op("act", lambda e: e.activation(st8[R, 2, :], st8[R, 2, :], AF.Sqrt, bias=epsc[R, 1:2], scale=1.0 / 64), reads=["st8", "epsc"], writes=["st8"])
                P.op("dve", lambda e: e.reciprocal(st8[R, 2, :], st8[R, 2, :]), reads=["st8"], writes=["st8"])
                P.op("dve", lambda e: e.tensor_tensor(v3(y[R, :]), v3(y[R, :]), bc8(2), ALU.mult), reads=["f_y", "st8"], writes=["f_y"])
                P.op("dve", lambda e: e.tensor_tensor(y[R, :], y[R, :], lnw_bc[R, :], ALU.mult), reads=["f_y", "bc"], writes=["f_y"])
                P.op("dve", lambda e: e.tensor_tensor(y[R, :], y[R, :], lnb_bc[R, :], ALU.add), reads=["f_y", "bc"], writes=["f_y"])
                P.op("dve", lambda e: e.tensor_tensor(y[R, :], y[R, :], bon[par][R, :], ALU.add), reads=["f_y", "bon%d" % par], writes=["f_y"])
                P.op("dve", lambda e: e.tensor_tensor(TB16["ob"][R, :], y[R, :], Fg[par][R, :], ALU.mult), reads=["f_y", "f_g%d" % par], writes=["b_ob"])
                for ct in range(4):
                    P.op("pe", lambda e, ct=ct: e.transpose(psb16[7][:, ct * 128:ct * 128 + rows], TB16["ob"][R, ct * 128:(ct + 1) * 128], identB[0:rows, 0:rows]), reads=["b_ob", "identB"], writes=["ps7"])
                P.op("act", lambda e: e.activation(oT[:, :, tcols], psb16[7][:, 0:512].rearrange("p (c t) -> p c t", c=4)[:, :, 0:rows], AF.Copy), reads=["ps7"], writes=["oT"])

            def chunk_tile(ti):
                rows = 128
                bonus(128, ti % 2)
                ck(5)
                P.op("pe", lambda e: e.matmul(ps[4][:, :], mIU[:, :], F["logw"][:, :], start=True, stop=True), reads=["mIU", "f_logw"], writes=["ps4"])
                P.op("pe", lambda e: e.matmul(ps[5][:, :], mSL[:, :], F["logw"][:, :], start=True, stop=True), reads=["mSL", "f_logw"], writes=["ps5"])
                for ct in range(4):
                    P.op("pe", lambda e, ct=ct: e.matmul(ps[6][:, ct:ct + 1], F["logw"][:, ct * 128:(ct + 1) * 128], onesF[:, 0:1], start=True, stop=True), reads=["f_logw", "onesF"], writes=["ps6"])
                P.op("act", lambda e: e.activation(gT[:, :], ps[6][:, 0:4], AF.Exp), reads=["ps6"], writes=["gT"])
                ex = F["y"]
                P.op("act", lambda e: e.activation(ex[:, :], ps[4][:, :], AF.Exp), reads=["ps4"], writes=["f_y"])
                P.op("dve", lambda e: e.tensor_tensor(TB16["Rt"][:, :], r_, ex[:, :], ALU.mult), reads=["mixed", "f_y"], writes=["b_Rt"])
                P.op("act", lambda e: e.activation(F["t2"][:, :], ps[4][:, :], AF.Exp, scale=-1.0), reads=["ps4"], writes=["f_t2"])
                P.op("dve", lambda e: e.tensor_tensor(TB16["Bt"][:, :], F["kka"][:, :], F["t2"][:, :], ALU.mult), reads=["f_kka", "f_t2"], writes=["b_Bt"])
                P.op("dve", lambda e: e.tensor_tensor(TB16["Kt"][:, :], F["k2"][:, :], F["t2"][:, :], ALU.mult), reads=["f_k2", "f_t2"], writes=["b_Kt"])
                P.op("dve", lambda e: e.tensor_tensor(F["t1"][:, :], ps[4][:, :], F["logw"][:, :], ALU.subtract), reads=["ps4", "f_logw"], writes=["f_t1"])
                P.op("act", lambda e: e.activation(F["t1"][:, :], F["t1"][:, :], AF.Exp), reads=["f_t1"], writes=["f_t1"])
                P.op("dve", lambda e: e.scalar_tensor_tensor(TB16["At"][:, :], F["kk"][:, :], -1.0, F["t1"][:, :], ALU.mult, ALU.mult), reads=["f_kk", "f_t1"], writes=["b_At"])
                P.op("act", lambda e: e.activation(F["t2"][:, :], ps[5][:, :], AF.Exp), reads=["ps5"], writes=["f_t2"])
                P.op("dve", lambda e: e.tensor_tensor(TB16["Bg"][:, :], F["kka"][:, :], F["t2"][:, :], ALU.mult), reads=["f_kka", "f_t2"], writes=["b_Bg"])
                P.op("dve", lambda e: e.tensor_tensor(TB16["Kg"][:, :], F["k2"][:, :], F["t2"][:, :], ALU.mult), reads=["f_k2", "f_t2"], writes=["b_Kg"])
                P.op("act", lambda e: e.activation(TB16["Vb"][:, :], v_, AF.Copy), reads=["mixed"], writes=["b_Vb"])
                ck(6)
                for qi, n in enumerate(("At", "Bt", "Kt", "Rt")):
                    pbk = qi % 4
                    for ct in range(4):
                        P.op("pe", lambda e, n=n, ct=ct, pbk=pbk: e.transpose(psb16[pbk][:, ct * 128:(ct + 1) * 128], TB16[n][:, ct * 128:(ct + 1) * 128], identB[:, :]), reads=["b_" + n, "identB"], writes=[PSK[pbk]])
                    if qi % 2 == 0:
                        P.op("act", lambda e, n=n, pbk=pbk: e.activation(XT[n][:].rearrange("p c t -> p (c t)"), psb16[pbk][:, 0:512], AF.Copy), reads=[PSK[pbk]], writes=["xt_" + n])
                    else:
                        P.op("dve", lambda e, n=n, pbk=pbk: e.tensor_copy(XT[n][:].rearrange("p c t -> p (c t)"), psb16[pbk][:, 0:512]), reads=[PSK[pbk]], writes=["xt_" + n])
                ck(7)
                for hg in range(8 // NHG):
                    heads = [hg * NHG + i for i in range(NHG)]
                    def hs(h):
                        return h // 2, 64 * (h % 2), h - hg * NHG
                    bank = lambda i, par: i
                    for h in heads:
                        ct, pb0, i = hs(h)
                        S = slice(pb0, pb0 + 64)
                        b = bank(i, 0)
                        P.op("pe", lambda e, ct=ct, S=S, b=b: e.matmul(ps[b][:, 0:128], XT["Bt"][S, ct, :], XT["At"][S, ct, :], start=True, stop=True), reads=["xt_Bt", "xt_At"], writes=[PSK[b]])
                        P.op("pe", lambda e, ct=ct, S=S, b=b: e.matmul(ps[b][:, 128:256], XT["Bt"][S, ct, :], XT["Rt"][S, ct, :], start=True, stop=True), reads=["xt_Bt", "xt_Rt"], writes=[PSK[b]])
                        P.op("pe", lambda e, ct=ct, S=S, b=b: e.matmul(ps[b][:, 256:384], XT["Kt"][S, ct, :], XT["At"][S, ct, :], start=True, stop=True), reads=["xt_Kt", "xt_At"], writes=[PSK[b]])
                        P.op("pe", lambda e, ct=ct, S=S, b=b: e.matmul(ps[b][:, 384:512], XT["Kt"][S, ct, :], XT["Rt"][S, ct, :], start=True, stop=True), reads=["xt_Kt", "xt_Rt"], writes=[PSK[b]])
                        P.op("dve", lambda e, i=i, b=b: e.tensor_tensor(AM[i][:, :], ps[b][:, :], CM[:, :], ALU.mult), reads=[PSK[b], "CM"], writes=["AM%d" % i])
                    ck(8)
                    for h in heads:
                        ct, pb0, i = hs(h)
                        S = slice(pb0, pb0 + 64)
                        b = bank(i, 1)
                        P.op("pe", lambda e, ct=ct, S=S, b=b: e.matmul(ps[b][:, 0:128], XT["At"][S, ct, :], XT["Bt"][S, ct, :], start=True, stop=True), reads=["xt_At", "xt_Bt"], writes=[PSK[b]])
                        P.op("dve", lambda e, i=i, b=b: e.tensor_tensor(Ak[i][0][:, 0:128], ps[b][:, 0:128], mSL[:, :], ALU.mult), reads=[PSK[b], "mSL"], writes=["Ak%d_0" % i])
                        P.op("act", lambda e, i=i: e.activation(Ak[i][0][:, 128:256], AM[i][:, 0:128], AF.Copy), reads=["AM%d" % i], writes=["Ak%d_0" % i])
                        P.op("dve", lambda e, i=i: e.tensor_tensor(Ak[i][0][:, 256:384], AM[i][:, 0:128], identB[:, :], ALU.add), reads=["AM%d" % i, "identB"], writes=["Ak%d_0" % i])
                    ck(9)
                    for lv in range(1, 8):
                        src, dst = (lv - 1) % 2, lv % 2
                        for h in heads:
                            ct, pb0, i = hs(h)
                            b = bank(i, 0)
                            rk = ["Ak%d_%d" % (i, src)]
                            if lv <= 6:
                                P.op("pe", lambda e, i=i, b=b, src=src: e.matmul(ps[b][:, 0:128], Ak[i][src][:, 128:256], Ak[i][src][:, 0:128], start=True, stop=True), reads=rk, writes=[PSK[b]])
                            if lv < 6:
                                P.op("pe", lambda e, i=i, b=b, src=src: e.matmul(ps[b][:, 128:256], Ak[i][src][:, 0:128], Ak[i][src][:, 128:256], start=True, stop=True), reads=rk, writes=[PSK[b]])
                            if lv >= 2:
                                P.op("pe", lambda e, i=i, b=b, src=src: e.matmul(ps[b][:, 256:384], Ak[i][src][:, 0:128], Ak[i][src][:, 256:384], start=True, stop=False), reads=rk, writes=[PSK[b]])
                                P.op("pe", lambda e, i=i, b=b, src=src: e.matmul(ps[b][:, 256:384], identB[:, :], Ak[i][src][:, 256:384], start=False, stop=True), reads=rk + ["identB"], writes=[PSK[b]])
                            else:
                                P.op("pe", lambda e, i=i, b=b, src=src: e.matmul(ps[b][:, 256:384], identB[:, :], Ak[i][src][:, 256:384], start=True, stop=True), reads=rk + ["identB"], writes=[PSK[b]])
                        for h in heads:
                            ct, pb0, i = hs(h)
                            b = bank(i, 0)
                            c_lo = 0 if lv <= 6 else 256
                            if i % 2 == 0:
                                P.op("act", lambda e, i=i, b=b, dst=dst, c_lo=c_lo: e.activation(Ak[i][dst][:, c_lo:384], ps[b][:, c_lo:384], AF.Copy), reads=[PSK[b]], writes=["Ak%d_%d" % (i, dst)])
                            else:
                                P.op("dve", lambda e, i=i, b=b, dst=dst, c_lo=c_lo: e.tensor_copy(Ak[i][dst][:, c_lo:384], ps[b][:, c_lo:384]), reads=[PSK[b]], writes=["Ak%d_%d" % (i, dst)])
                    pfin = 7 % 2
                    ck(10)
                    for h in heads:
                        ct, pb0, i = hs(h)
                        b = bank(i, 0)
                        P.op("pe", lambda e, i=i, b=b, h=h: e.matmul(ps[b][:, 0:64], AM[i][:, 256:384], TB16["Vb"][:, h * 64:(h + 1) * 64], start=True, stop=True), reads=["AM%d" % i, "b_Vb"], writes=[PSK[b]])
                        P.op("act", lambda e, i=i, b=b: e.activation(W1[i][:, :], ps[b][:, 0:64], AF.Copy), reads=[PSK[b]], writes=["W1_%d" % i])
                    for h in heads:
                        ct, pb0, i = hs(h)
                        b = bank(i, 1)
                        P.op("pe", lambda e, i=i, b=b, h=h: e.matmul(ps[b][:, 0:64], Ak[i][pfin][:, 256:384], TB16["At"][:, h * 64:(h + 1) * 64], start=True, stop=True), reads=["Ak%d_%d" % (i, pfin), "b_At"], writes=[PSK[b]])
                        P.op("pe", lambda e, i=i, b=b: e.matmul(ps[b][:, 64:128], Ak[i][pfin][:, 256:384], W1[i][:, :], start=True, stop=True), reads=["Ak%d_%d" % (i, pfin), "W1_%d" % i], writes=[PSK[b]])
                        P.op("dve", lambda e, h=h, b=b: e.tensor_copy(AU[h][:, :], ps[b][:, 0:128]), reads=[PSK[b]], writes=["AU%d" % h])
                    ck(11)
                    for h in heads:
                        ct, pb0, i = hs(h)
                        S = slice(pb0, pb0 + 64)
                        b = bank(i, 0)
                        RWVAR = os.environ.get("RWVAR", "abcd")
                        if "a" in RWVAR:
                            P.op("pe", lambda e, i=i, b=b, h=h, S=S: e.matmul(ps[b][S, 0:128], AU[h][:, 0:64], AM[i][:, 128:256], start=True, stop=True), reads=["AU%d" % h, "AM%d" % i], writes=[PSK[b]])
                        if "b" in RWVAR:
                            P.op("pe", lambda e, b=b, h=h, S=S: e.matmul(ps[b][S, 128:192], AU[h][:, 0:64], TB16["Bg"][:, h * 64:(h + 1) * 64], start=True, stop=True), reads=["AU%d" % h, "b_Bg"], writes=[PSK[b]])
                        if "c" in RWVAR:
                            P.op("dve", lambda e, b=b, S=S, ct=ct, h=h: e.tensor_tensor(RhT[S, h, :], ps[b][S, 0:128], XT["Rt"][S, ct, :], ALU.add), reads=[PSK[b], "xt_Rt"], writes=["RhT"])
                        if "d" in RWVAR:
                            P.op("dve", lambda e, b=b, S=S, h=h: e.tensor_copy(GTp[S, h, :], ps[b][S, 128:192]), reads=[PSK[b]], writes=["GTp"])
                    ck(12)
                    for h in heads:
                        ct, pb0, i = hs(h)
                        S = slice(pb0, pb0 + 64)
                        yb = 7
                        P.op("pe", lambda e, i=i, h=h: e.matmul(ps[7][:, h * 64:(h + 1) * 64], AM[i][:, 128:256], AU[h][:, 64:128], start=True, stop=False), reads=["AM%d" % i, "AU%d" % h], writes=["ps7"])
                        P.op("pe", lambda e, i=i, h=h: e.matmul(ps[7][:, h * 64:(h + 1) * 64], AM[i][:, 384:512], TB16["Vb"][:, h * 64:(h + 1) * 64], start=False, stop=False), reads=["AM%d" % i, "b_Vb"], writes=["ps7"])
                        P.op("pe", lambda e, h=h, S=S, ct=ct: e.matmul(ps[7][:, h * 64:(h + 1) * 64], RhT[:, h, :], Hb[:, ct, :], start=False, stop=True), reads=["RhT", "Hb"], writes=["ps7"])
                    GC = slice(hg * NHG * 64, (hg + 1) * NHG * 64)
                    P.op("act", lambda e, GC=GC: e.activation(F["y"][:, GC], ps[7][:, GC], AF.Copy), reads=["ps7"], writes=["f_y"])
                ck(13)
                for h in range(8):
                    ct, pb0 = h // 2, 64 * (h % 2)
                    S = slice(pb0, pb0 + 64)
                    P.op("pe", lambda e, h=h, S=S, ct=ct: e.matmul(ps[6][S, ct * 64:(ct + 1) * 64], GTp[:, h, :], Hb[:, ct, :], start=True, stop=False), reads=["GTp", "Hb"], writes=["ps6"])
                    P.op("pe", lambda e, h=h, S=S, ct=ct: e.matmul(ps[6][S, ct * 64:(ct + 1) * 64], TB16["Bg"][:, h * 64:(h + 1) * 64], AU[h][:, 64:128], start=False, stop=False), reads=["b_Bg", "AU%d" % h], writes=["ps6"])
                    P.op("pe", lambda e, h=h, S=S, ct=ct: e.matmul(ps[6][S, ct * 64:(ct + 1) * 64], TB16["Kg"][:, h * 64:(h + 1) * 64], TB16["Vb"][:, h * 64:(h + 1) * 64], start=False, stop=True), reads=["b_Kg", "b_Vb"], writes=["ps6"])
                for ct in range(4):
                    P.op("dve", lambda e, ct=ct: e.scalar_tensor_tensor(Hst[:, ct, :], Hst[:, ct, :], gT[:, ct:ct + 1], ps[6][:, ct * 64:(ct + 1) * 64], ALU.mult, ALU.add), reads=["Hst", "gT", "ps6"], writes=["Hst"])
                P.op("act", lambda e: e.activation(Hb[:], Hst[:], AF.Copy), reads=["Hst"], writes=["Hb"])

            prelim(128, slice(0, 128), first_tile=(sbk == 0), samp=False, par=0)
            for ti in range(8):
                tcols = slice(ti * 128, (ti + 1) * 128)
                if sbk == 1 and ti == 7:
                    P.dma("sp", dr["shiftP"], cur[127:128, :], reads=["cur"])
                chunk_tile(ti)
                m_p = P.mark()
                post(128, tcols, ti % 2)
                l_post = P.take(m_p)
                if ti < 7:
                    prelim(128, slice((ti + 1) * 128, (ti + 2) * 128), first_tile=False, samp=False, par=(ti + 1) % 2)
                elif sbk == 1:
                    prelim(16, slice(1024, 1040), first_tile=False, samp=True, par=0)
                l_pre = P.take(m_p)
                P.put_interleaved(l_post, l_pre)
            if sbk == 1:
                bonus(16, 0)
            if sbk == 1:
                ck(16)
                for ct in range(4):
                    P.op("pe", lambda e, ct=ct: e.transpose(ps[0][0:64, ct * 128:(ct + 1) * 128], Hst[:, ct, :], identF[:, :]), reads=["Hst", "identF"], writes=["ps0"])
                P.op("act", lambda e: e.activation(F["t1"][0:64, :], ps[0][0:64, :], AF.Copy), reads=["ps0"], writes=["f_t1"])
                P.dma("sp", dr["wkvP"].rearrange("(h i) j -> i h j", h=8), F["t1"][0:64, :].rearrange("i (h j) -> i h j", h=8), reads=["f_t1"])
                ck(17)
                tcols = slice(1024, 1040)
                P.dma("sp", dr["shiftS"], cur[0:16, :], reads=["cur"])
                ck(18)
                R = slice(0, 16)
                wf = wcur[:].rearrange("p a b -> p (a b)").bitcast(F32)
                pack = wf[0:16, 0:3072].rearrange("b (h q j) -> b h q j", h=8, q=6)
                v4 = lambda ap: ap.rearrange("p (h j) -> p h j", h=8)
                P.op("dve", lambda e: e.tensor_scalar(pack[:, :, 0, :], v4(F["kk"][R, :]), -1.0, None, ALU.mult), reads=["f_kk", "cur"], writes=["wcur"])
                P.op("act", lambda e: e.activation(pack[:, :, 1, :], v4(F["logw"][R, :]), AF.Exp), reads=["f_logw"], writes=["wcur"])
                P.op("dve", lambda e: e.tensor_copy(pack[:, :, 2, :], v4(F["kka"][R, :])), reads=["f_kka"], writes=["wcur"])
                P.op("dve", lambda e: e.tensor_copy(pack[:, :, 3, :], v4(F["k2"][R, :])), reads=["f_k2"], writes=["wcur"])
                P.op("dve", lambda e: e.tensor_copy(pack[:, :, 4, :], v4(r_[R, :])), reads=["mixed"], writes=["wcur"])
                P.op("dve", lambda e: e.tensor_copy(pack[:, :, 5, :], v4(v_[R, :])), reads=["mixed"], writes=["wcur"])
                P.dma("sp", scr1, wf[0:16, 0:3072], reads=["wcur"], writes=["scr1"])
                vec = wf[:, 5120:5504].rearrange("p (q j) -> p q j", q=6)
                P.dma("sp", wf[:, 5120:5504], scr1.rearrange("b (h x) -> (b h) x", h=8), reads=["scr1"], writes=["vec"])
                ysm = wf[:, 5504:5568]
                Sc = wf[:, 3072:4096].rearrange("p (i j) -> p i j", i=16)
                Tc = wf[:, 4096:5120].rearrange("p (i j) -> p i j", i=16)
                sa = wf[:, 5568:5584]
                bj = lambda q: vec[:, q, :].unsqueeze(1).to_broadcast([128, 16, 64])
                for ic in range(4):
                    isl = slice(ic * 16, (ic + 1) * 16)
                    P.dma("sp", wf[:, 3072:4096], dr["wkv0"][:, ic * 1024:(ic + 1) * 1024], reads=["scr1"], writes=["Sc"])
                    P.op("dve", lambda e: e.tensor_tensor(Tc, Sc, bj(0), ALU.mult), reads=["Sc", "vec"], writes=["Tc"])
                    P.op("dve", lambda e: e.tensor_reduce(sa, Tc, AX.X, ALU.add), reads=["Tc"], writes=["sa"])
                    P.op("dve", lambda e: e.tensor_tensor(Sc, Sc, bj(1), ALU.mult), reads=["Sc", "vec"], writes=["Sc"])
                    P.op("dve", lambda e: e.tensor_tensor(Tc, sa.unsqueeze(2).to_broadcast([128, 16, 64]), bj(2), ALU.mult), reads=["sa", "vec"], writes=["Tc"])
                    P.op("dve", lambda e: e.tensor_tensor(Sc, Sc, Tc, ALU.add), reads=["Sc", "Tc"], writes=["Sc"])
                    P.op("dve", lambda e, isl=isl: e.tensor_tensor(Tc, vec[:, 5, isl].unsqueeze(2).to_broadcast([128, 16, 64]), bj(3), ALU.mult), reads=["vec"], writes=["Tc"])
                    P.op("dve", lambda e: e.tensor_tensor(Sc, Sc, Tc, ALU.add), reads=["Sc", "Tc"], writes=["Sc"])
                    P.dma("sp", dr["wkvS"][:, ic * 1024:(ic + 1) * 1024], wf[:, 3072:4096], reads=["Sc"])
                    P.op("dve", lambda e: e.tensor_tensor(Tc, Sc, bj(4), ALU.mult), reads=["Sc", "vec"], writes=["Tc"])
                    P.op("dve", lambda e, isl=isl: e.tensor_reduce(ysm[:, isl], Tc, AX.X, ALU.add), reads=["Tc"], writes=["ysm"])
                P.dma("sp", scr2, ysm, reads=["ysm"], writes=["scr2"])
                P.dma("sp", F["y"][0:16, :], scr2.rearrange("(b h) i -> b (h i)", h=8), reads=["scr2"], writes=["f_y"])
                post(16, tcols, 0)

        def merge_phase(sbk, sbx, zT, oT):
            rngs = ranges_of(sbk)
            alloc_norm(sbx)
            nT = sbx("uT", [128, 8, NC], BF16)
            ncol_ = 1024 if sbk == 0 else NC
            P.dma("sp", nT[:, :, 0:ncol_], uT_scr[:, :, 0:ncol_], writes=["uT"])
            gv = sbx("gv", [128, 4, D], BF16)
            gg = sbx("gg", [128, 4, D], BF16)
            pj = sbx("pj", [128, 4, D], BF16)
            wo = sbx("wo", [128, 8, D], BF16)
            def ld_half(t_, nm, key, hh):
                P.dma("pool", t_[:, :, hh * 512:(hh + 1) * 512], dr[nm][:, hh * 512:(hh + 1) * 512].rearrange("(kt p) n -> p kt n", p=128), writes=[(key, hh)])
            ld_half(gv, "s5_glu_v", "gv", 0)
            ld_half(gg, "s5_glu_g", "gg", 0)
            wg = [sbx("wg%d" % i, [128, 8, 512], BF16) for i in range(2)]
            mT = sbx("mT", [128, 8, NC], BF16)
            ta = sbx("ta", [128, 512])
            tb = sbx("tb", [128, 512])
            tcg = sbx("tcg", [128, 512])
            for dg in range(2):
                if dg == 1:
                    ld_half(gv, "s5_glu_v", "gv", 1)
                    ld_half(gg, "s5_glu_g", "gg", 1)
                P.dma("pool", wg[0][:], dr["w_in"][:, 2304 + dg * 512: 2304 + (dg + 1) * 512].rearrange("(kt p) n -> p kt n", p=128), writes=["wg0"])
                ld_half(pj, "rwkv_proj", "pj", dg)
                P.dma("pool", wg[1][:], dr["w_in"][:, 3328 + dg * 512: 3328 + (dg + 1) * 512].rearrange("(kt p) n -> p kt n", p=128), writes=["wg1"])
                if dg == 1:
                    P.dma("pool", wo[:], dr["w_out"].rearrange("(kt p) n -> p kt n", p=128), writes=["wo"])
                for dd in range(4):
                    dt = dg * 4 + dd
                    DS = slice(dt * 128, (dt + 1) * 128)
                    for (c0, n) in rngs:
                        CS = slice(c0, c0 + n)
                        for kt in range(4):
                            P.op("pe", lambda e, kt=kt, DS=DS, CS=CS, n=n: e.matmul(ps[0][:, 0:n], gv[:, kt, DS], zT[:, kt, CS], start=(kt == 0), stop=(kt == 3)), reads=[("gv", dg), "zT"], writes=["ps0"])
                        for kt in range(4):
                            P.op("pe", lambda e, kt=kt, DS=DS, CS=CS, n=n: e.matmul(ps[1][:, 0:n], gg[:, kt, DS], zT[:, kt, CS], start=(kt == 0), stop=(kt == 3)), reads=[("gg", dg), "zT"], writes=["ps1"])
                        for kt in range(8):
                            P.op("pe", lambda e, kt=kt, dd=dd, CS=CS, n=n: e.matmul(ps[2][:, 0:n], wg[0][:, kt, dd * 128:(dd + 1) * 128], nT[:, kt, CS], start=(kt == 0), stop=(kt == 7)), reads=["wg0", "uT"], writes=["ps2"])
                        for kt in range(4):
                            P.op("pe", lambda e, kt=kt, DS=DS, CS=CS, n=n: e.matmul(ps[3][:, 0:n], pj[:, kt, DS], oT[:, kt, CS], start=(kt == 0), stop=(kt == 3)), reads=[("pj", dg), "oT"], writes=["ps3"])
                        for kt in range(8):
                            P.op("pe", lambda e, kt=kt, dd=dd, CS=CS, n=n: e.matmul(ps[4][:, 0:n], wg[1][:, kt, dd * 128:(dd + 1) * 128], nT[:, kt, CS], start=(kt == 0), stop=(kt == 7)), reads=["wg1", "uT"], writes=["ps4"])
                        P.op("act", lambda e, n=n: e.activation(ta[:, 0:n], ps[1][:, 0:n], AF.Sigmoid), reads=["ps1"], writes=["ta"])
                        P.op("dve", lambda e, n=n: e.tensor_tensor(ta[:, 0:n], ps[0][:, 0:n], ta[:, 0:n], ALU.mult), reads=["ps0", "ta"], writes=["ta"])
                        P.op("act", lambda e, n=n: e.activation(tb[:, 0:n], ps[2][:, 0:n], AF.Sigmoid), reads=["ps2"], writes=["tb"])
                        P.op("pool", lambda e, n=n: e.tensor_tensor(ta[:, 0:n], ta[:, 0:n], tb[:, 0:n], ALU.mult), reads=["ta", "tb"], writes=["ta"])
                        P.op("act", lambda e, n=n: e.activation(tcg[:, 0:n], ps[4][:, 0:n], AF.Sigmoid), reads=["ps4"], writes=["tcg"])
                        P.op("dve", lambda e, n=n: e.tensor_tensor(tcg[:, 0:n], ps[3][:, 0:n], tcg[:, 0:n], ALU.mult), reads=["ps3", "tcg"], writes=["tcg"])
                        P.op("dve", lambda e, n=n, dt=dt, CS=CS: e.tensor_tensor(mT[:, dt, CS], ta[:, 0:n], tcg[:, 0:n], ALU.add), reads=["ta", "tcg"], writes=["mT"])
            cnt = 0
            for dt in range(8):
                for (c0, n) in rngs:
                    k = 5 + (cnt % 2)
                    cnt += 1
                    for kt in range(8):
                        P.op("pe", lambda e, k=k, kt=kt, dt=dt, c0=c0, n=n: e.matmul(ps[k][:, 0:n], wo[:, kt, dt * 128:(dt + 1) * 128], mT[:, kt, c0:c0 + n], start=(kt == 0), stop=(kt == 7)), reads=["wo", "mT"], writes=[PSK[k]])
                    resid_update(1, dt, c0, n, ps[k], PSK[k])

        with ExitStack() as sp0:
            sbx = mk_sb(sp0)
            xin = [sbx("xin%d" % i, [128, D]) for i in range(2)]
            load_x(0, xin)
            m0 = P.mark()
            phase0(sbx)
            la = P.take(m0)
            s5_setup(sbx)
            lb = P.take(m0)
            P.put_interleaved(la, lb)
            P.flush()
        for sbk in range(2):
            with ExitStack() as s1:
                sbx = mk_sb(s1)
                alloc_norm(sbx)
                if sbk == 1:
                    xin = [sbx("xin%d" % i, [128, D]) for i in range(2)]
                    load_x(sbk, xin)
                ffn(sbk, 0, "ffn1_w1", "ffn1_w3", "ffn1_w2", sbx)
                if sbk == 1:
                    dbg_dump("d_h1", hT[:].rearrange("p a b -> p (a b)"), "hT")
                P.flush()
            if upto != "ffn1":
                with ExitStack() as sm:
                    sbm = mk_sb(sm)
                    zT = sbm("zT", [128, 4, NC], BF16)
                    if _os.environ.get("S5OLD", "0") == "1":
                        with ExitStack() as s2:
                            s5_phase(sbk, mk_sb(s2), zT)
                            if sbk == 1:
                                dbg_dump("d_zT", zT[:].rearrange("p a b -> p (a b)"), "zT")
                            P.flush()
                    else:
                        s5_phase2(sbk, zT)
                    oT = sbm("oT", [128, 4, NC], BF16)
                    if upto != "s5":
                        with ExitStack() as s3:
                            try:
                                rwkv_phase(sbk, mk_sb(s3), oT)
                            except _Stop:
                                pass
                            if sbk == 1:
                                dbg_dump("d_oT", oT[:].rearrange("p a b -> p (a b)"), "oT")
                            P.flush()
                        with ExitStack() as s4:
                            merge_phase(sbk, mk_sb(s4), zT, oT)
                            if sbk == 1:
                                dbg_dump("d_h2", hT[:].rearrange("p a b -> p (a b)"), "hT")
                            P.flush()
            with ExitStack() as s5:
                sbx = mk_sb(s5)
                alloc_norm(sbx)
                if upto == "all":
                    ffn(sbk, 2, "ffn2_w1", "ffn2_w3", "ffn2_w2", sbx)
                yout = [sbx("yout%d" % i, [128, D]) for i in range(2)]
                final_out(sbk, yout)
                P.flush(final=(sbk == 1))
        build_nc.n_instr = dict(P.n_instr)
    return nc


def _prep_core_inputs(inputs, b):
    f = lambda a: np.ascontiguousarray(np.asarray(a, dtype=np.float32))
    s = slice(16 * b, 16 * b + 16)
    m = {
        "xP": f(inputs["x_prompt"][b]),
        "xS": f(inputs["x_sample"][s, 0, :]),
        "cA": f(np.concatenate([inputs["c_prompt"][b:b + 1], inputs["c_sample"][s]], axis=0)),
        "s5re0": f(inputs["state_s5_re"][s].reshape(16, 2048)),
        "s5im0": f(inputs["state_s5_im"][s].reshape(16, 2048)),
        "wkv0": f(inputs["state_wkv"][s].reshape(128, 4096)),
        "shift0": f(inputs["state_shift"][s]),
    }
    for n, shp in W_SPECS:
        m[n] = f(inputs[n]).reshape(shp)
    return m


def kernel(**inputs):
    nc = build_nc()
    in_maps = [_prep_core_inputs(inputs, b) for b in range(8)]
    res = run_bass_kernel_spmd(nc, in_maps, core_ids=list(range(8)))
    R = res.results
    cat = lambda k: np.concatenate([np.asarray(r[k]) for r in R], axis=0)
    y_prompt = np.stack([np.asarray(r["yP"]) for r in R], axis=0)
    y_sample = cat("yS").reshape(128, 1, D)
    s5_re_p = np.stack([np.asarray(r["s5reP"]).reshape(32, 64) for r in R], axis=0)
    s5_im_p = np.stack([np.asarray(r["s5imP"]).reshape(32, 64) for r in R], axis=0)
    wkv_p = np.stack([np.asarray(r["wkvP"]).reshape(8, 64, 64) for r in R], axis=0)
    shift_p = cat("shiftP")
    s5_re_s = cat("s5reS").reshape(128, 32, 64)
    s5_im_s = cat("s5imS").reshape(128, 32, 64)
    wkv_s = cat("wkvS").reshape(128, 8, 64, 64)
    shift_s = cat("shiftS")
    return (y_prompt, y_sample, s5_re_p, s5_im_p, wkv_p, shift_p, s5_re_s, s5_im_s, wkv_s, shift_s)
```
